# Optimizing a Trainium2 kernel written in Bass

```python
import jax, jax.numpy as jnp
from jax import lax
import numpy as np

D_MODEL = 1024
BATCH = 8
SEQ = 2048
DEPTH = 1

HGRN_HEADS = 8
HGRN_EXPAND = 128
HGRN_FWD = HGRN_HEADS * HGRN_EXPAND
HGRN_IN = D_MODEL
HGRN_VDIM = HGRN_IN // HGRN_HEADS
HGRN_SCALE = HGRN_EXPAND ** -0.5
CHUNK = 32
RWKV_HEAD = 64
RWKV_DIM = D_MODEL
RWKV_HEADS = RWKV_DIM // RWKV_HEAD
W_LORA = 64
A_LORA = 64
G_LORA = 128
GN_EPS = 1e-5 * RWKV_HEAD
D_FF = 2816
CONV_W = 3
EPS = 1e-6

HGRN_COLS = 2 * HGRN_FWD + 2 * HGRN_IN
RWKV_COLS = 3 * RWKV_DIM + W_LORA + A_LORA + G_LORA
GATE_COLS = 2 * D_MODEL
IN_COLS = HGRN_COLS + RWKV_COLS + GATE_COLS

kernel_name = "hgrn2_rwkv7_gated_hybrid_convffn"

F32 = jnp.float32


def _rmsnorm(x, g):
    xf = x.astype(F32)
    y = xf * lax.rsqrt(jnp.mean(xf * xf, axis=-1, keepdims=True) + EPS)
    return (y * g.astype(F32)).astype(x.dtype)


def _split(z, sizes):
    outs, off = [], 0
    for s in sizes:
        outs.append(z[..., off:off + s])
        off += s
    return outs


def _shift1(z):
    return jnp.pad(z[:, :-1], ((0, 0), (1, 0), (0, 0)))


def _causal_dwconv(h, w, b):
    S = h.shape[1]
    hp = jnp.pad(h, ((0, 0), (CONV_W - 1, 0), (0, 0)))
    out = b
    for j in range(CONV_W):
        out = out + w[j] * hp[:, j:j + S]
    return out


def _hgrn2_chunkwise(q, f_log, k, v):
    B, S, H, K = q.shape
    V = v.shape[-1]
    N = S // CHUNK

    def chunk(t):
        return t.reshape(B, N, CHUNK, H, t.shape[-1]).transpose(0, 3, 1, 2, 4)

    qc, gc, kc, vc = chunk(q), chunk(f_log), chunk(k), chunk(v)
    b = jnp.cumsum(gc, axis=3)
    b_ref = b[:, :, :, CHUNK // 2 - 1:CHUNK // 2, :]
    q_in = qc * jnp.exp(b - b_ref)
    k_in = kc * jnp.exp(b_ref - b)
    scores = jnp.einsum('bhnck,bhndk->bhncd', q_in, k_in)
    mask = jnp.tril(jnp.ones((CHUNK, CHUNK), dtype=bool))
    scores = jnp.where(mask, scores, 0.0)
    o_intra = jnp.einsum('bhncd,bhndv->bhncv', scores, vc)

    b_last = b[:, :, :, -1, :]
    u = jnp.einsum('bhnck,bhncv->bhnkv', kc * jnp.exp(b_last[:, :, :, None, :] - b), vc)
    decay = jnp.exp(b_last)

    def step(state, inp):
        d, u_n = inp
        return d[..., None] * state + u_n, state

    s0 = jnp.zeros((B, H, K, V), F32)
    _, s_prev = lax.scan(step, s0, (decay.transpose(2, 0, 1, 3), u.transpose(2, 0, 1, 3, 4)))
    s_prev = s_prev.transpose(1, 2, 0, 3, 4)
    o_inter = jnp.einsum('bhnck,bhnkv->bhncv', qc * jnp.exp(b), s_prev)
    o = o_intra + o_inter
    return o.transpose(0, 2, 3, 1, 4).reshape(B, S, H, V)


def _rwkv7_scan(r, w, k, v, a_vec, b_vec):
    B, S, H, N = r.shape

    def step(state, inp):
        r_t, w_t, k_t, v_t, a_t, b_t = inp
        sa = jnp.einsum('bhvk,bhk->bhv', state, a_t)
        state = (state * w_t[:, :, None, :] + sa[..., None] * b_t[:, :, None, :]
                 + v_t[..., None] * k_t[:, :, None, :])
        y = jnp.einsum('bhvk,bhk->bhv', state, r_t)
        return state, y

    xs = tuple(t.astype(F32).transpose(1, 0, 2, 3) for t in (r, w, k, v, a_vec, b_vec))
    _, y = lax.scan(step, jnp.zeros((B, H, N, N), F32), xs)
    return y.transpose(1, 0, 2, 3)


def _token_mixer(xn, lb, w_in, hgrn_gnorm, w_branch_a, rwkv_mu, rwkv_w0, rwkv_w2, rwkv_a0,
                 rwkv_a2, rwkv_g2, rwkv_k_k, rwkv_k_a, rwkv_r_k, rwkv_ln_w, rwkv_ln_b,
                 w_branch_b, w_out):
    B, S, _ = xn.shape
    z = xn @ w_in
    z_h, z_r, z_g = _split(z, [HGRN_COLS, RWKV_COLS, GATE_COLS])

    hq, hf, hi, hg = _split(z_h, [HGRN_FWD, HGRN_FWD, HGRN_IN, HGRN_IN])
    q = jax.nn.silu(hq.astype(F32)).reshape(B, S, HGRN_HEADS, HGRN_EXPAND) * HGRN_SCALE
    f = lb + (1.0 - lb) * jax.nn.sigmoid(hf.astype(F32))
    f = f.reshape(B, S, HGRN_HEADS, HGRN_EXPAND)
    k_h = 1.0 - f
    vi = hi.astype(F32).reshape(B, S, HGRN_HEADS, HGRN_VDIM)
    o_a = _hgrn2_chunkwise(q, jnp.log(f), k_h, vi)
    o_a = o_a * lax.rsqrt(jnp.mean(o_a * o_a, axis=-1, keepdims=True) + EPS)
    o_a = o_a * hgrn_gnorm.astype(F32).reshape(HGRN_HEADS, HGRN_VDIM)
    o_a = o_a.reshape(B, S, HGRN_IN) * jax.nn.silu(hg.astype(F32))
    y_a = o_a.astype(xn.dtype) @ w_branch_a

    z_r = z_r + rwkv_mu * (_shift1(z_r) - z_r)
    rr, kr, vr, wz, az, gz = _split(z_r, [RWKV_DIM, RWKV_DIM, RWKV_DIM, W_LORA, A_LORA, G_LORA])
    w_log = -jax.nn.softplus(-(rwkv_w0 + jnp.tanh(wz) @ rwkv_w2).astype(F32)) - 0.5
    decay = jnp.exp(-jnp.exp(w_log))
    a = jax.nn.sigmoid((rwkv_a0 + az @ rwkv_a2).astype(F32))
    g = jax.nn.sigmoid(gz) @ rwkv_g2
    kr = kr.astype(F32)
    kk = (kr * rwkv_k_k).reshape(B, S, RWKV_HEADS, RWKV_HEAD)
    kk = kk / jnp.maximum(jnp.linalg.norm(kk, axis=-1, keepdims=True), 1e-12)
    kr = kr * (1.0 + (a - 1.0) * rwkv_k_a)

    def heads(t):
        return t.astype(F32).reshape(B, S, RWKV_HEADS, RWKV_HEAD)

    r_h, k_r, v_r, w_h, a_h = heads(rr), heads(kr), heads(vr), heads(decay), heads(a)
    y = _rwkv7_scan(r_h, w_h, k_r, v_r, -kk, kk * a_h)
    mu_y = jnp.mean(y, axis=-1, keepdims=True)
    var_y = jnp.mean(jnp.square(y - mu_y), axis=-1, keepdims=True)
    y = ((y - mu_y) * lax.rsqrt(var_y + GN_EPS)).reshape(B, S, RWKV_DIM)
    y = y * rwkv_ln_w.astype(F32) + rwkv_ln_b.astype(F32)
    bonus = jnp.sum(r_h * k_r * rwkv_r_k.astype(F32), axis=-1, keepdims=True) * v_r
    o_b = (y + bonus.reshape(B, S, RWKV_DIM)) * g.astype(F32)
    y_b = o_b.astype(xn.dtype) @ w_branch_b

    ga, gb = _split(z_g, [D_MODEL, D_MODEL])
    merged = jax.nn.sigmoid(ga) * y_a + jax.nn.sigmoid(gb) * y_b
    return merged @ w_out


def _conv_ffn(xn, w_up, conv_w, conv_b, w_down):
    hu = xn @ w_up
    hc = _causal_dwconv(hu, conv_w, conv_b)
    gate, val = _split(hc, [D_FF, D_FF])
    return (jax.nn.silu(gate) * val) @ w_down


def setup_inputs(seed: int = 0) -> dict:
    key = jax.random.key(seed)
    ks = jax.random.split(key, 32)
    L = DEPTH

    def nrm(k, shape, scale):
        return jax.random.normal(k, shape, F32) * scale

    def gain(k, shape):
        return 1.0 + 0.02 * jax.random.normal(k, shape, F32)

    return {
        "x": jax.random.normal(ks[0], (BATCH, SEQ, D_MODEL), F32),
        "attn_pre_norm": gain(ks[1], (L, D_MODEL)),
        "w_in": nrm(ks[2], (L, D_MODEL, IN_COLS), D_MODEL ** -0.5),
        "hgrn_lb": nrm(ks[3], (DEPTH + 1, HGRN_FWD), 0.1),
        "hgrn_gnorm": gain(ks[4], (L, HGRN_IN)),
        "w_branch_a": nrm(ks[5], (L, HGRN_IN, D_MODEL), HGRN_IN ** -0.5),
        "rwkv_mu": jax.random.uniform(ks[6], (L, RWKV_COLS), F32),
        "rwkv_w0": jax.random.uniform(ks[7], (L, RWKV_DIM), F32, minval=-6.0, maxval=0.0),
        "rwkv_w2": nrm(ks[8], (L, W_LORA, RWKV_DIM), 0.1 * W_LORA ** -0.5),
        "rwkv_a0": nrm(ks[9], (L, RWKV_DIM), 0.1),
        "rwkv_a2": nrm(ks[10], (L, A_LORA, RWKV_DIM), 0.5 * A_LORA ** -0.5),
        "rwkv_g2": nrm(ks[11], (L, G_LORA, RWKV_DIM), G_LORA ** -0.5),
        "rwkv_k_k": 0.85 + 0.02 * jax.random.normal(ks[12], (L, RWKV_DIM), F32),
        "rwkv_k_a": gain(ks[13], (L, RWKV_DIM)),
        "rwkv_r_k": nrm(ks[14], (L, RWKV_HEADS, RWKV_HEAD), 0.1),
        "rwkv_ln_w": gain(ks[15], (L, RWKV_DIM)),
        "rwkv_ln_b": nrm(ks[16], (L, RWKV_DIM), 0.01),
        "w_branch_b": nrm(ks[17], (L, RWKV_DIM, D_MODEL), RWKV_DIM ** -0.5),
        "w_out": nrm(ks[18], (L, D_MODEL, D_MODEL), D_MODEL ** -0.5),
        "attn_post_norm": gain(ks[19], (L, D_MODEL)),
        "ffn_pre_norm": gain(ks[20], (L, D_MODEL)),
        "w_up": nrm(ks[21], (L, D_MODEL, 2 * D_FF), D_MODEL ** -0.5),
        "conv_w": nrm(ks[22], (L, CONV_W, 2 * D_FF), CONV_W ** -0.5),
        "conv_b": nrm(ks[23], (L, 2 * D_FF), 0.01),
        "w_down": nrm(ks[24], (L, D_FF, D_MODEL), D_FF ** -0.5),
        "ffn_post_norm": gain(ks[25], (L, D_MODEL)),
    }


def reference(x, attn_pre_norm, w_in, hgrn_lb, hgrn_gnorm, w_branch_a, rwkv_mu, rwkv_w0,
              rwkv_w2, rwkv_a0, rwkv_a2, rwkv_g2, rwkv_k_k, rwkv_k_a, rwkv_r_k, rwkv_ln_w,
              rwkv_ln_b, w_branch_b, w_out, attn_post_norm, ffn_pre_norm, w_up, conv_w,
              conv_b, w_down, ffn_post_norm):
    lb_table = jnp.cumsum(jax.nn.softmax(hgrn_lb.astype(F32), axis=0), axis=0)
    h = x
    for l in range(DEPTH):
        xn = _rmsnorm(h, attn_pre_norm[l])
        mix = _token_mixer(xn, lb_table[l], w_in[l], hgrn_gnorm[l], w_branch_a[l], rwkv_mu[l],
                           rwkv_w0[l], rwkv_w2[l], rwkv_a0[l], rwkv_a2[l], rwkv_g2[l],
                           rwkv_k_k[l], rwkv_k_a[l], rwkv_r_k[l], rwkv_ln_w[l], rwkv_ln_b[l],
                           w_branch_b[l], w_out[l])
        h = h + _rmsnorm(mix, attn_post_norm[l])
        xn = _rmsnorm(h, ffn_pre_norm[l])
        ff = _conv_ffn(xn, w_up[l], conv_w[l], conv_b[l], w_down[l])
        h = h + _rmsnorm(ff, ffn_post_norm[l])
    return h
```

```python
import contextlib
import sys
import math
import numpy as np
import concourse.bass as bass
import concourse.mybir as mybir
from concourse.bass_utils import run_bass_kernel_spmd

F32 = mybir.dt.float32
BF16 = mybir.dt.bfloat16
AF = mybir.ActivationFunctionType
ALU = mybir.AluOpType

T = 2048
D = 1024
TBS = 512
NB = T // TBS
CH = 128
NCC = TBS // CH
DFF = 2816
NJ = DFF // 128
INC = 9472
EPS = 1e-6
GN_EPS = 1e-5 * 64
C0 = math.exp(-0.5)
HSCALE = 128 ** -0.5

CP = {}
_o = 0
for _n, _w in [("g1", 8), ("lb0", 8), ("lb1", 8), ("gn", 8), ("mu", 26), ("w0", 8), ("a0", 8),
               ("kk", 8), ("ka", 8), ("rk", 8), ("lnw", 8), ("lnb", 8), ("g3", 8),
               ("cw", 132), ("cb", 44)]:
    CP[_n] = _o
    _o += _w
NCOL = _o

DEBUG = {}
MAXOPS = None
PSUM_PREFIXES = ("pj", "ptr", "psc", "pst", "phg", "pmi")


class Op:
    __slots__ = ("eng", "fn", "deps", "sig", "seq", "dma", "dsem", "dval", "idx", "tag", "ph")


class Prog:
    NDS = 24

    def __init__(self):
        self.ops = []
        self.lastw = {}
        self.readers = {}
        self.dma_cnt = {}
        self.dma_last = {}
        self.phase = ''

    def add(self, eng, fn, r=(), w=(), dma=False):
        if MAXOPS is not None and len(self.ops) >= MAXOPS:
            return None
        op = Op()
        op.eng, op.fn, op.dma, op.sig, op.seq = eng, fn, dma, False, 0
        op.idx = len(self.ops)
        op.ph = self.phase
        fr = sys._getframe(1)
        tg = []
        while fr is not None and len(tg) < 3:
            tg.append(str(fr.f_lineno))
            fr = fr.f_back
        op.tag = "/".join(tg)
        deps = {}
        for k in r:
            d = self.lastw.get(k)
            if d is not None:
                deps[d.idx] = (d, True)
            if k.startswith(PSUM_PREFIXES):
                for d in self.readers.get(k, ()):
                    if d.eng != eng and d.idx not in deps:
                        deps[d.idx] = (d, False)
        for k in w:
            d = self.lastw.get(k)
            if d is not None and d.idx not in deps:
                deps[d.idx] = (d, False)
            for d in self.readers.get(k, ()):
                if d.idx not in deps:
                    deps[d.idx] = (d, False)
        keep = []
        for d, raw in deps.values():
            if d is op:
                continue
            if d.eng == eng and not d.dma and not dma:
                if eng == "pe":
                    continue
            keep.append(d)
            d.sig = True
        op.deps = keep
        for k in r:
            self.readers.setdefault(k, []).append(op)
        for k in w:
            self.lastw[k] = op
            self.readers[k] = []
        if dma:
            i = self.dma_cnt.get(eng, 0)
            self.dma_cnt[eng] = i + 1
            op.dsem = (eng, i % self.NDS)
            op.dval = 16 * (i // self.NDS + 1)
            prev = self.dma_last.get(op.dsem)
            if prev is not None and prev not in op.deps:
                op.deps.append(prev)
            self.dma_last[op.dsem] = op
        self.ops.append(op)
        return op

    def emit(self, nc, block, sems, dsems):
        cnt = {}
        for op in self.ops:
            if op.dma:
                continue
            if op.sig:
                cnt[op.eng] = cnt.get(op.eng, 0) + 1
                op.seq = cnt[op.eng]
        byeng = {}
        for op in self.ops:
            byeng.setdefault(op.eng, []).append(op)

        def run(e, ename):
            waited = {}
            for op in byeng.get(ename, []):
                need = {}
                for d in op.deps:
                    if d.dma:
                        key, val = ("d",) + d.dsem, d.dval
                    else:
                        key, val = ("e", d.eng), d.seq
                    if val > need.get(key, 0):
                        need[key] = val
                for key, val in need.items():
                    if waited.get(key, 0) >= val:
                        continue
                    waited[key] = val
                    s = dsems[key[1:]] if key[0] == "d" else sems[key[1]]
                    e.wait_ge(s, val)
                inst = op.fn(e)
                if op.dma:
                    inst.then_inc(dsems[op.dsem], 16)
                elif op.sig:
                    inst.then_inc(sems[ename], 1)
            n = self.dma_cnt.get(ename, 0)
            for i in range(min(n, self.NDS)):
                tot = (n - i + self.NDS - 1) // self.NDS
                e.wait_ge(dsems[(ename, i)], 16 * tot)

        @block.sync
        def _(e):
            run(e, "sp")

        @block.tensor
        def _(e):
            run(e, "pe")

        @block.scalar
        def _(e):
            run(e, "act")

        @block.vector
        def _(e):
            run(e, "dve")

        @block.gpsimd
        def _(e):
            run(e, "pool")


def build():
    nc = bass.Bass("TRN2", target_bir_lowering=False)
    global _P
    P = Prog()
    _P = P
    es = contextlib.ExitStack()

    def dram(name, shape, kind="ExternalInput"):
        return nc.dram_tensor(name, shape, F32, kind=kind).ap()

    x_d = dram("x", [T, D])
    cols_d = dram("cols", [128, NCOL])
    win_d = dram("w_in", [D, INC])
    wba_d = dram("w_branch_a", [D, D])
    wbb_d = dram("w_branch_b", [D, D])
    wout_d = dram("w_out", [D, D])
    w2a2_d = dram("w2a2", [128, D])
    g2_d = dram("g2", [128, D])
    gpa_d = dram("gpost_a", [1, D])
    gpf_d = dram("gpost_f", [1, D])
    wup_d = dram("w_up", [D, 2 * DFF])
    wdn_d = dram("w_down", [DFF, D])
    out_d = dram("out", [T, D], kind="ExternalOutput")
    dbg_d = {}
    for k, shp in DEBUG.items():
        dbg_d[k] = dram("dbg_" + k, shp, kind="ExternalOutput")

    def sb(name, shape, dt=F32):
        return es.enter_context(nc.sbuf_tensor("sb_" + name, shape, dt))

    def ps(name, shape, dt=F32):
        return es.enter_context(nc.psum_tensor("ps_" + name, shape, dt))

    def mm(out, lhsT, rhs, r, w, start=True, stop=True):
        P.add("pe", lambda e: e.matmul(out, lhsT, rhs, start=start, stop=stop), r, w)

    def mmg(out, pairs, r, w):
        n = len(pairs)
        for i, (l, rr) in enumerate(pairs):
            mm(out, l, rr, r, w, start=(i == 0), stop=(i == n - 1))

    def tr(out, in_, ident_ap, r, w):
        P.add("pe", lambda e: e.transpose(out, in_, ident_ap), r, w)

    def act(out, in_, func, r, w, bias=None, scale=None, accum=None, eng="act"):
        kw = {}
        if bias is not None:
            kw["bias"] = bias
        if scale is not None:
            kw["scale"] = scale
        if accum is not None:
            kw["accum_out"] = accum
        P.add("act", lambda e: e.activation(out=out, in_=in_, func=func, **kw), r, w)

    def tt(eng, out, a, b, op, r, w):
        P.add(eng, lambda e: e.tensor_tensor(out=out, in0=a, in1=b, op=op), r, w)

    def ts(eng, out, a, s1, op0, r, w, s2=None, op1=None):
        if op1 is None:
            P.add(eng, lambda e: e.tensor_scalar(out=out, in0=a, scalar1=s1, scalar2=None, op0=op0), r, w)
        else:
            P.add(eng, lambda e: e.tensor_scalar(out=out, in0=a, scalar1=s1, scalar2=s2, op0=op0, op1=op1), r, w)

    def stt(out, a, s, b, op0, op1, r, w):
        P.add("dve", lambda e: e.scalar_tensor_tensor(out=out, in0=a, scalar=s, in1=b, op0=op0, op1=op1), r, w)

    def cp(eng, out, in_, r, w):
        if eng == "act":
            P.add("act", lambda e: e.activation(out=out, in_=in_, func=AF.Copy), r, w)
        else:
            P.add(eng, lambda e: e.tensor_copy(out=out, in_=in_), r, w)

    def recip(out, in_, r, w):
        P.add("dve", lambda e: e.reciprocal(out=out, in_=in_), r, w)

    def scan(out, d0, d1, r, w):
        P.add("dve", lambda e: e.tensor_tensor_scan(out=out, data0=d0, data1=d1, initial=0.0,
                                                    op0=ALU.mult, op1=ALU.add), r, w)

    def memset(eng, ap, val, w):
        P.add(eng, lambda e: e.memset(ap, val), (), w)

    def dma(eng, out, in_, r, w):
        P.add(eng, lambda e: e.dma_start(out=out, in_=in_), r, w, dma=True)

    def dbg(name, ap, r):
        if name in dbg_d:
            dma("sp", dbg_d[name], ap, r, ["dbg_" + name])

    ident = sb("ident", [128, 128], BF16)
    identf = sb("identf", [128, 128], F32)
    mask2 = sb("mask2", [128, 256], F32)
    strictT = sb("strictT", [128, 128], F32)
    bones = sb("bones", [128, 128], F32)
    onesf = sb("onesf", [128, 128], F32)
    rmask = sb("rmask", [128, TBS], F32)
    cols = sb("cols", [128, NCOL], F32)
    lbc = sb("lbc", [128, 8], F32)
    omlc = sb("omlc", [128, 8], F32)
    omu = sb("omu", [128, 26], F32)
    gp = sb("gp", [128, D], F32)
    w2a2 = sb("w2a2", [128, D], BF16)
    g2 = sb("g2sb", [128, D], BF16)
    lnsc = sb("lnsc", [128, 1], F32)

    memset("pool", identf[:], 0.0, ["identf"])
    P.add("pool", lambda e: e.affine_select(out=identf[:], in_=identf[:], pattern=[[-1, 128]],
                                            compare_op=ALU.not_equal, fill=1.0, base=0,
                                            channel_multiplier=1), ["identf"], ["identf"])
    cp("pool", ident[:], identf[:], ["identf"], ["ident"])
    memset("pool", mask2[:], 1.0, ["mask2"])
    P.add("pool", lambda e: e.affine_select(out=mask2[:, 0:128], in_=mask2[:, 0:128], pattern=[[1, 128]],
                                            compare_op=ALU.is_gt, fill=0.0, base=0,
                                            channel_multiplier=-1), ["mask2"], ["mask2"])
    P.add("pool", lambda e: e.affine_select(out=mask2[:, 128:256], in_=mask2[:, 128:256], pattern=[[1, 128]],
                                            compare_op=ALU.is_ge, fill=0.0, base=0,
                                            channel_multiplier=-1), ["mask2"], ["mask2"])
    memset("pool", strictT[:], 1.0, ["strictT"])
    P.add("pool", lambda e: e.affine_select(out=strictT[:], in_=strictT[:], pattern=[[-1, 128]],
                                            compare_op=ALU.is_gt, fill=0.0, base=0,
                                            channel_multiplier=1), ["strictT"], ["strictT"])
    memset("pool", bones[:], 0.0, ["bones"])
    memset("pool", bones[0:64, 0:64], 1.0, ["bones"])
    memset("pool", bones[64:128, 64:128], 1.0, ["bones"])
    memset("pool", onesf[:], 1.0, ["onesf"])
    memset("pool", rmask[:], 1.0, ["rmask"])
    memset("pool", rmask[:].rearrange("p (c t) -> p c t", t=CH)[:, :, 0:1], 0.0, ["rmask"])
    memset("pool", lnsc[:], math.log(HSCALE), ["lnsc"])

    dma("sp", cols[:], cols_d[:, :], [], ["cols"])
    dma("pool", w2a2[:], w2a2_d[:, :], [], ["w2a2"])
    dma("pool", g2[:], g2_d[:, :], [], ["g2"])

    def col(name, i, n=1):
        o = CP[name] + i
        return cols[:, o:o + n]

    tt("dve", lbc[:], col("lb0", 0, 8), col("lb1", 0, 8), ALU.subtract, ["cols"], ["lbc"])
    act(lbc[:], lbc[:], AF.Sigmoid, ["lbc"], ["lbc"])
    ts("dve", omlc[:], lbc[:], -1.0, ALU.mult, ["lbc"], ["omlc"], s2=1.0, op1=ALU.add)
    ts("dve", omu[:], cols[:, CP["mu"]:CP["mu"] + 26], -1.0, ALU.mult, ["cols"], ["omu"], s2=1.0, op1=ALU.add)

    Sh = sb("Sh", [128, 8, 128], F32)
    Hr = sb("Hr", [128, 8, 128], F32)
    memset("pool", Sh[:], 0.0, ["Sh%d" % g for g in range(8)])
    memset("pool", Hr[:], 0.0, ["Hr%d" % g for g in range(8)])
    carry = sb("carry", [128, 26], F32)
    memset("pool", carry[:], 0.0, ["carry%d" % i for i in range(26)])
    halo = sb("halo", [128, 44, 2], F32)
    memset("pool", halo[:], 0.0, ["halo%d" % i for i in range(44)])

    NPAD = 1
    Vpad = [[sb("Vpad%d_%d" % (s, h), [128, NCC, 128], BF16) for h in range(2)] for s in range(NPAD)]
    Bpad = [[sb("Bpad%d_%d" % (s, h), [128, NCC, 128], BF16) for h in range(2)] for s in range(NPAD)]
    Kpad = [[sb("Kpad%d_%d" % (s, h), [128, NCC, 128], BF16) for h in range(2)] for s in range(NPAD)]
    Upad = [[sb("Upad%d_%d" % (s, h), [128, 128], BF16) for h in range(2)] for s in range(2)]
    for s in range(NPAD):
        for h in range(2):
            for nm, bufs in (("Vpad", Vpad), ("Bpad", Bpad), ("Kpad", Kpad)):
                memset("pool", bufs[s][h][:], 0.0, ["%s%d_%d_%d" % (nm, s, h, c) for c in range(NCC)])
    for s in range(2):
        for h in range(2):
            memset("pool", Upad[s][h][:], 0.0, ["Upad%d_%d" % (s, h)])

    xt = [sb("xt%d" % i, [128, D], F32) for i in range(2)]
    junk = sb("junk", [128, D], BF16)
    xnb = [sb("xnb%d" % i, [128, D], BF16) for i in range(2)]
    st1 = [sb("st1_%d" % i, [128, 4], F32) for i in range(4)]
    xnT = sb("xnT", [128, 8, TBS], BF16)
    xn2T = xnT
    AAR = sb("AAR", [128, 24, TBS], BF16)
    OAT = AAR[:, 0:8, :]
    OBT = AAR[:, 8:16, :]
    MT = AAR[:, 16:24, :]
    ACTT = AAR[:, 0:NJ, :]
    hblk = sb("hblk", [128, NCC, D], F32)
    LT = sb("LT", [128, TBS], BF16)
    LG = sb("LG", [128, TBS], BF16)
    Wl = sb("Wl", [128, 8, 256], BF16)
    WAR = sb("WAR", [128, 16 * 1024], BF16)

    def arena(slot0, nslots, pattern, **kw):
        ap = WAR[:, slot0 * 1024:(slot0 + nslots) * 1024].rearrange(pattern, **kw)
        return ap, ["ar%d" % i for i in range(slot0, slot0 + nslots)]
    Wg = [arena(7 * i, 7, "p (j c n) -> p j c n", c=8, j=7) for i in range(2)]
    Wm = [arena(4 * i, 4, "p (j c n) -> p j c n", c=8, j=4) for i in range(2)]
    WO, WOK = arena(8, 8, "p (c n) -> p c n", c=8)
    WU = [arena(2 * i, 2, "p (j c n) -> p j c n", c=8, j=2) for i in range(5)]
    WDR = [arena(10 + i, 1, "p (j n) -> p j n", j=1) for i in range(4)]
    NF = 16
    ftmp = [sb("ft%d" % i, [128, TBS], F32) for i in range(NF)]
    Vh2 = [sb("Vh%d" % i, [128, NCC, 128], BF16) for i in range(2)]
    QI2 = [sb("QI%d" % i, [128, TBS], BF16) for i in range(2)]
    KI2 = [sb("KI%d" % i, [128, TBS], BF16) for i in range(2)]
    KItm = [sb("KItm%d" % i, [128, 128], BF16) for i in range(2)]
    SCT = [sb("SCT%d" % i, [128, 128], BF16) for i in range(2)]
    Sg = [sb("Sg%d" % i, [128, 128], BF16) for i in range(2)]
    OT = sb("OT", [128, TBS], F32)
    SHG2 = [sb("SHG%d" % i, [128, TBS], BF16) for i in range(2)]
    sc4 = [sb("sc4_%d" % i, [128, 4], F32) for i in range(16)]
    ART = sb("ART", [128, NCC, 2, 128], BF16)
    BT = sb("BT", [128, TBS], BF16)
    KT = sb("KT", [128, TBS], BF16)
    RTlo = sb("RTlo", [128, TBS], BF16)
    KTlo = sb("KTlo", [128, TBS], BF16)
    VB = sb("VB", [128, TBS], BF16)
    GG = sb("GG", [128, TBS], F32)
    BON = sb("BON", [128, TBS], F32)
    YT = sb("YT", [128, TBS], F32)
    NTARB = [[sb("NTARB%d_%d" % (c, h), [128, 256], BF16) for h in range(2)] for c in range(NCC)]
    AKRK = [[sb("AKRK%d_%d" % (c, h), [128, 256], BF16) for h in range(2)] for c in range(NCC)]
    XT = [[sb("XT%d_%d" % (c, h), [128, 128], BF16) for h in range(2)] for c in range(NCC)]
    PX = [sb("PX%d" % i, [128, 256], BF16) for i in range(8)]
    Pb = [sb("Pb%d" % i, [128, 128], BF16) for i in range(8)]
    G0 = [sb("G0_%d" % i, [128, 128], BF16) for i in range(2)]
    RHSb = [sb("RHSb%d" % i, [128, 128], BF16) for i in range(2)]
    tmp128 = [sb("tmp128_%d" % i, [128, 128], F32) for i in range(2)]

    pj = [ps("pj%d" % i, [128, 512]) for i in range(2)]
    ptr = ps("ptr", [128, 8, 128], BF16)
    psc = [ps("psc%d" % i, [128, 512]) for i in range(2)]
    pst = ps("pst", [128, 4, 128])
    phg = ps("phg", [128, 4, 128])
    pmi = ps("pmi", [128, 512])

    cnt = {"pj": 0, "raw": 0, "ft": 0}

    def nxt(name, n):
        i = cnt.get(name, 0)
        cnt[name] = i + 1
        return i % n

    for tb in range(NB):
        t0 = tb * TBS

        def norm_T(src_tile_fn, dstT, gname, pre):
            for i in range(NCC):
                xb, xk = src_tile_fn(i)
                s = nxt("st1", 4)
                st = st1[s]
                stk = "st1_%d" % s
                act(junk[:], xb, AF.Square, [xk], ["junk", stk], accum=st[:, 0:1])
                ts("dve", st[:, 1:2], st[:, 0:1], 1.0 / D, ALU.mult, [stk], [stk], s2=EPS, op1=ALU.add)
                act(st[:, 1:2], st[:, 1:2], AF.Sqrt, [stk], [stk])
                b = nxt("xnb", 2)
                recip(st[:, 2:3], st[:, 1:2], [stk], [stk])
                ts("dve", xnb[b][:], xb, st[:, 2:3], ALU.mult, [xk, stk], ["xnb%d" % b])
                for c in range(8):
                    tr(ptr[:, c, :], xnb[b][:, c * 128:(c + 1) * 128], ident[:], ["xnb%d" % b, "ident"], ["ptr"])
                tt("dve", dstT[:, :, i * 128:(i + 1) * 128], ptr[:, :, :],
                   cols[:, CP[gname]:CP[gname] + 8].unsqueeze(2).to_broadcast([128, 8, 128]),
                   ALU.mult, ["ptr", "cols"], ["%s_%d" % (pre, i)])

        def xsrc(i):
            b = nxt("xt", 2)
            dma("sp", xt[b][:], x_d[t0 + i * 128:t0 + (i + 1) * 128, :], [], ["xt%d" % b])
            return xt[b][:], "xt%d" % b

        P.phase = "A"
        norm_T(xsrc, xnT, "g1", "xnT")
        XNT_KEYS = ["xnT_%d" % i for i in range(NCC)]

        def shift_lerp(psrc, pkey, muidx, dst, dkey, eng2="dve"):
            ck = "carry%d" % muidx
            muc = col("mu", muidx)
            act(dst, psrc, AF.Identity, [pkey, "omu"], [dkey], scale=omu[:, muidx:muidx + 1])
            stt(dst[:, 1:TBS], psrc[:, 0:TBS - 1], muc, dst[:, 1:TBS], ALU.mult, ALU.add, [pkey, "cols", dkey], [dkey])
            stt(dst[:, 0:1], carry[:, muidx:muidx + 1], muc, dst[:, 0:1], ALU.mult, ALU.add, [ck, "cols", dkey], [dkey])
            cp("act", carry[:, muidx:muidx + 1], psrc[:, TBS - 1:TBS], [pkey], [ck])

        B4 = [(pj[0][:], "pj0"), (pj[1][:], "pj1"), (psc[0][:], "psc0"), (psc[1][:], "psc1")]
        B7 = B4 + [(pst[:].rearrange("p a b -> p (a b)"), "pst"), (phg[:].rearrange("p a b -> p (a b)"), "phg"),
                   (pmi[:], "pmi")]

        def proj_fm(wtile_fn, wkeys, rhsT, rkeys, banks=B4):
            b = nxt("pjb", len(banks))
            bap, bkey = banks[b]
            mmg(bap, [(wtile_fn(c), rhsT[:, c, :]) for c in range(8)], wkeys + rkeys, [bkey])
            return bap, bkey

        P.phase = "L"
        if tb == 0:
            dma("pool", Wl[:], win_d[:, 7168:7424].rearrange("(c p) n -> p c n", p=128), [], ["Wl"])
        pa, pk = proj_fm(lambda c: Wl[:, c, 0:128], ["Wl"], xnT, XNT_KEYS)
        f = nxt("ft", NF)
        shift_lerp(pa, pk, 24, ftmp[f][:], "ft%d" % f)
        act(LT[0:64, :], ftmp[f][0:64, :], AF.Tanh, ["ft%d" % f], ["LT"])
        cp("act", LT[64:128, :], ftmp[f][64:128, :], ["ft%d" % f], ["LT"])
        pa, pk = proj_fm(lambda c: Wl[:, c, 128:256], ["Wl"], xnT, XNT_KEYS)
        f = nxt("ft", NF)
        shift_lerp(pa, pk, 25, ftmp[f][:], "ft%d" % f)
        act(LG[:], ftmp[f][:], AF.Sigmoid, ["ft%d" % f], ["LG"])

        segs = [0, 1024, 2048, 3072, 4096, 5120, 6144]

        def load_Wg(g):
            wb = g % 2
            W, WK = Wg[wb]
            for j, so in enumerate(segs):
                dma("pool", W[:, j, :, :],
                    win_d[:, so + g * 128: so + (g + 1) * 128].rearrange("(c p) n -> p c n", p=128),
                    [], [WK[j]])

        def F():
            i = nxt("ft", NF)
            return ftmp[i][:], "ft%d" % i

        ctx = {}

        def stageA(g):
            c = ctx.setdefault(g, {})
            W, WK = Wg[g % 2]
            if g + 1 < 8:
                load_Wg(g + 1)
            cnt["ft"] = 0
            pp = g % 2
            QI, KI, Vh, SHG = QI2[pp], KI2[pp], Vh2[pp], SHG2[pp]
            QIk, KIk, SHGk = "QI%d" % pp, "KI%d" % pp, "SHG%d" % pp

            def wfn(j):
                return (lambda cc_: W[:, j, cc_, :]), [WK[j]]
            fn_, wkk = wfn(0)
            pa, pk = proj_fm(fn_, wkk, xnT, XNT_KEYS)
            qs, qsk = F()
            act(qs, pa, AF.Silu, [pk], [qsk])
            yield
            fn_, wkk = wfn(1)
            pa, pk = proj_fm(fn_, wkk, xnT, XNT_KEYS)
            fg, fgk = F()
            act(fg, pa, AF.Sigmoid, [pk], [fgk])
            yield
            ts("dve", fg, fg, omlc[:, g:g + 1], ALU.mult, [fgk, "omlc", "lbc"], [fgk], s2=lbc[:, g:g + 1], op1=ALU.add)
            lnf, lnfk = F()
            act(lnf, fg, AF.Ln, [fgk], [lnfk])
            kkh, kkhk = F()
            ts("pool", kkh, fg, -1.0, ALU.mult, [fgk], [kkhk], s2=1.0, op1=ALU.add)
            yield
            bb, bbk = F()
            scan(bb, rmask[:], lnf, ["rmask", lnfk], [bbk])
            b3 = bb.rearrange("p (c t) -> p c t", t=CH)
            dd, ddk = F()
            tt("dve", dd.rearrange("p (c t) -> p c t", t=CH), b3, b3[:, :, 63:64].to_broadcast([128, NCC, CH]),
               ALU.subtract, [bbk], [ddk])
            yield
            e1, e1k = F()
            act(e1, dd, AF.Exp, [ddk, "lnsc"], [e1k], bias=lnsc[:, 0:1])
            tt("dve", QI[:], qs, e1, ALU.mult, [qsk, e1k], [QIk])
            yield
            e2, e2k = F()
            act(e2, dd, AF.Exp, [ddk], [e2k], scale=-1.0)
            tt("pool", KI[:], kkh, e2, ALU.mult, [kkhk, e2k], [KIk])
            yield
            si = nxt("sc4", 16)
            eref, erefk = sc4[si], "sc4_%d" % si
            act(eref[:].unsqueeze(2), b3[:, :, 63:64], AF.Exp, [bbk], [erefk])
            si = nxt("sc4", 16)
            elast, elastk = sc4[si], "sc4_%d" % si
            act(elast[:].unsqueeze(2), b3[:, :, 127:128], AF.Exp, [bbk], [elastk])
            si = nxt("sc4", 16)
            elr, elrk = sc4[si], "sc4_%d" % si
            tt("dve", elr[:].unsqueeze(2), b3[:, :, 127:128], b3[:, :, 63:64], ALU.subtract, [bbk], [elrk])
            act(elr[:], elr[:], AF.Exp, [elrk], [elrk])
            c["h"] = (eref, erefk, elast, elastk, elr, elrk)
            yield
            fn_, wkk = wfn(3)
            pa, pk = proj_fm(fn_, wkk, xnT, XNT_KEYS)
            act(SHG[:], pa, AF.Silu, [pk], [SHGk])
            yield
            for i in range(NCC):
                b = nxt("pj", 2)
                mmg(pj[b][:, 0:128], [(xnT[:, c_, i * 128:(i + 1) * 128], W[:, 2, c_, :]) for c_ in range(8)],
                    [WK[2], "xnT_%d" % i], ["pj%d" % b])
                cp("act", Vh[:, i, :], pj[b][:, 0:128], ["pj%d" % b], ["Vh%d_%d" % (pp, i)])
                yield

        def chainH(g):
            eref, erefk, elast, elastk, elr, elrk = ctx[g]["h"]
            SK = "Sh%d" % g
            pp = g % 2
            QI, KI, Vh, SHG = QI2[pp], KI2[pp], Vh2[pp], SHG2[pp]
            QIk, KIk, SHGk = "QI%d" % pp, "KI%d" % pp, "SHG%d" % pp
            for cc in range(NCC):
                csl = slice(cc * CH, (cc + 1) * CH)
                kb = nxt("KItm", 2)
                tr(ptr[:, 0, :], KI[:, csl], ident[:], [KIk, "ident"], ["ptr"])
                cp("act", KItm[kb][:], ptr[:, 0, :], ["ptr"], ["KItm%d" % kb])
                yield
                mm(phg[:, 0, :], KI[:, csl], QI[:, csl], [KIk, QIk], ["phg"])
                mm(phg[:, 2, :], KItm[kb][:], Vh[:, cc, :], ["KItm%d" % kb, "Vh%d_%d" % (pp, cc)], ["phg"])
                sb_ = nxt("SCT", 2)
                tt("dve", SCT[sb_][:], phg[:, 0, :], mask2[:, 128:256], ALU.mult, ["phg", "mask2"], ["SCT%d" % sb_])
                tb_ = nxt("tmp128", 2)
                ts("dve", tmp128[tb_][:], phg[:, 2, :], elr[:, cc:cc + 1], ALU.mult, ["phg", elrk], ["tmp128_%d" % tb_])
                gb = nxt("Sg", 2)
                ts("dve", Sg[gb][:], Sh[:, g, :], eref[:, cc:cc + 1], ALU.mult, [SK, erefk], ["Sg%d" % gb])
                yield
                mmg(phg[:, 1, :], [(Vh[:, cc, :], SCT[sb_][:]), (Sg[gb][:], QI[:, csl])],
                    ["Vh%d_%d" % (pp, cc), "SCT%d" % sb_, "Sg%d" % gb, QIk], ["phg"])
                stt(Sh[:, g, :], Sh[:, g, :], elast[:, cc:cc + 1], tmp128[tb_][:], ALU.mult, ALU.add,
                    [SK, elastk, "tmp128_%d" % tb_], [SK])
                cp("act", OT[:, csl], phg[:, 1, :], ["phg"], ["OT%d" % cc])
                yield
            OTK = ["OT%d" % c_ for c_ in range(NCC)]
            osq, osqk = ftmp[13][:], "ft13"
            act(osq, OT[:], AF.Square, OTK, [osqk])
            mm(pmi[:], onesf[:], osq, ["onesf", osqk], ["pmi"])
            yield
            sd, sdk = osq, osqk
            ts("dve", sd, pmi[:], 1.0 / 128, ALU.mult, ["pmi"], [sdk], s2=EPS, op1=ALU.add)
            act(sd, sd, AF.Sqrt, [sdk], [sdk])
            recip(sd, sd, [sdk], [sdk])
            yield
            tt("dve", sd, OT[:], sd, ALU.mult, OTK + [sdk], [sdk])
            stt(OAT[:, g, :], sd, col("gn", g), SHG[:], ALU.mult, ALU.mult, [sdk, "cols", SHGk], ["A%d" % g])
            yield

        def stageB(g):
            c = ctx.setdefault(g, {})
            W, WK = Wg[g % 2]
            cnt["ft"] = 0

            def wfn(j):
                return (lambda cc_: W[:, j, cc_, :]), [WK[j]]
            fn_, wkk = wfn(4)
            pa, pk = proj_fm(fn_, wkk, xnT, XNT_KEYS)
            RP, RPk = F()
            shift_lerp(pa, pk, g, RP, RPk)
            yield
            fn_, wkk = wfn(5)
            pa, pk = proj_fm(fn_, wkk, xnT, XNT_KEYS)
            KP, KPk = F()
            shift_lerp(pa, pk, 8 + g, KP, KPk)
            yield
            fn_, wkk = wfn(6)
            pa, pk = proj_fm(fn_, wkk, xnT, XNT_KEYS)
            VP, VPk = F()
            shift_lerp(pa, pk, 16 + g, VP, VPk)
            cp("pool", VB[:], VP, [VPk], ["VB"])
            yield
            gs = slice(g * 128, (g + 1) * 128)
            b = nxt("pj", 2)
            mm(pj[b][:], w2a2[0:64, gs], LT[0:64, :], ["w2a2", "LT"], ["pj%d" % b])
            lwp, lwpk = F()
            act(lwp, pj[b][:], AF.Sigmoid, ["pj%d" % b, "cols"], [lwpk], bias=col("w0", g))
            b = nxt("pj", 2)
            mm(pj[b][:], w2a2[64:128, gs], LT[64:128, :], ["w2a2", "LT"], ["pj%d" % b])
            asg, asgk = F()
            act(asg, pj[b][:], AF.Sigmoid, ["pj%d" % b, "cols"], [asgk], bias=col("a0", g))
            yield
            cwp, cwpk = F()
            scan(cwp, rmask[:], lwp, ["rmask", lwpk], [cwpk])
            c3 = cwp.rearrange("p (c t) -> p c t", t=CH)
            dd, ddk = F()
            tt("dve", dd.rearrange("p (c t) -> p c t", t=CH), c3, c3[:, :, 63:64].to_broadcast([128, NCC, CH]),
               ALU.subtract, [cwpk], [ddk])
            da, dak = F()
            tt("pool", da, dd, lwp, ALU.subtract, [ddk, lwpk], [dak])
            yield
            epos, eposk = F()
            act(epos, dd, AF.Exp, [ddk], [eposk], scale=-C0)
            eneg, enegk = F()
            act(eneg, dd, AF.Exp, [ddk], [enegk], scale=C0)
            act(da, da, AF.Exp, [dak], [dak], scale=-C0)
            yield
            si = nxt("sc4", 16)
            reref, rerefk = sc4[si], "sc4_%d" % si
            act(reref[:].unsqueeze(2), c3[:, :, 63:64], AF.Exp, [cwpk], [rerefk], scale=-C0)
            si = nxt("sc4", 16)
            relast, relastk = sc4[si], "sc4_%d" % si
            act(relast[:].unsqueeze(2), c3[:, :, 127:128], AF.Exp, [cwpk], [relastk], scale=-C0)
            si = nxt("sc4", 16)
            relr, relrk = sc4[si], "sc4_%d" % si
            tt("dve", relr[:].unsqueeze(2), c3[:, :, 127:128], c3[:, :, 63:64], ALU.subtract, [cwpk], [relrk])
            act(relr[:], relr[:], AF.Exp, [relrk], [relrk], scale=-C0)
            c["r"] = (reref, rerefk, relast, relastk, relr, relrk)
            yield
            kkr, kkrk = F()
            ts("dve", kkr, KP, col("kk", g), ALU.mult, [KPk, "cols"], [kkrk])
            sq, sqk = F()
            act(sq, kkr, AF.Square, [kkrk], [sqk])
            mm(pmi[:], bones[:], sq, ["bones", sqk], ["pmi"])
            act(sq, pmi[:], AF.Sqrt, ["pmi"], [sqk])
            yield
            ts("dve", sq, sq, 1e-12, ALU.max, [sqk], [sqk])
            recip(sq, sq, [sqk], [sqk])
            tt("dve", kkr, kkr, sq, ALU.mult, [kkrk, sqk], [kkrk])
            k2, k2k = F()
            ts("pool", k2, asg, -1.0, ALU.add, [asgk, "cols"], [k2k], s2=col("ka", g), op1=ALU.mult)
            stt(k2, k2, 1.0, KP, ALU.add, ALU.mult, [k2k, KPk], [k2k])
            yield
            c["b1"] = dict(RP=RP, RPk=RPk, VP=VP, VPk=VPk, asg=asg, asgk=asgk, da=da, dak=dak, epos=epos,
                           eposk=eposk, eneg=eneg, enegk=enegk, kkr=kkr, kkrk=kkrk, sq=sq, sqk=sqk, k2=k2, k2k=k2k)

        def stageB2(g):
            c = ctx[g]
            d_ = c["b1"]
            RP, RPk, VP, VPk, asg, asgk = d_["RP"], d_["RPk"], d_["VP"], d_["VPk"], d_["asg"], d_["asgk"]
            da, dak, epos, eposk, eneg, enegk = d_["da"], d_["dak"], d_["epos"], d_["eposk"], d_["eneg"], d_["enegk"]
            kkr, kkrk, sq, sqk, k2, k2k = d_["kkr"], d_["kkrk"], d_["sq"], d_["sqk"], d_["k2"], d_["k2k"]
            gs = slice(g * 128, (g + 1) * 128)
            b = nxt("pj", 2)
            mm(pj[b][:], g2[:, gs], LG[:], ["g2", "LG"], ["pj%d" % b])
            cp("act", GG[:], pj[b][:], ["pj%d" % b], ["GG"])
            yield
            stt(sq, RP, col("rk", g), k2, ALU.mult, ALU.mult, [RPk, "cols", k2k], [sqk])
            mm(pmi[:], bones[:], sq, ["bones", sqk], ["pmi"])
            tt("dve", BON[:], pmi[:], VP, ALU.mult, ["pmi", VPk], ["BON"])
            yield
            tt("dve", epos, RP, epos, ALU.mult, [RPk, eposk], [eposk])
            cp("pool", ART[:, :, 1, :], epos.rearrange("p (c t) -> p c t", t=CH), [eposk], ["ART_R"])
            tt("pool", RTlo[:].rearrange("p (c t) -> p c t", t=CH), epos.rearrange("p (c t) -> p c t", t=CH),
               ART[:, :, 1, :], ALU.subtract, [eposk, "ART_R"], ["RTlo"])
            stt(ART[:, :, 0, :], kkr.rearrange("p (c t) -> p c t", t=CH), -1.0,
                da.rearrange("p (c t) -> p c t", t=CH), ALU.mult, ALU.mult, [kkrk, dak], ["ART_A"])
            tt("pool", sq, kkr, asg, ALU.mult, [kkrk, asgk], [sqk])
            tt("dve", BT[:], sq, eneg, ALU.mult, [sqk, enegk], ["BT"])
            tt("dve", k2, k2, eneg, ALU.mult, [k2k, enegk], [k2k])
            cp("pool", KT[:], k2, [k2k], ["KT"])
            tt("pool", KTlo[:], k2, KT[:], ALU.subtract, [k2k, "KT"], ["KTlo"])
            yield
            pset = 0
            trio = ((VB, "VB", Vpad, "Vpad"), (BT, "BT", Bpad, "Bpad"), (KT, "KT", Kpad, "Kpad"))
            for cc in range(NCC):
                csl = slice(cc * CH, (cc + 1) * CH)
                for q, (src, skey, bufs, nm) in enumerate(trio):
                    tr(ptr[:, 1 + q, :], src[:, csl], ident[:], [skey, "ident"], ["ptr"])
                for q, (src, skey, bufs, nm) in enumerate(trio):
                    for h in range(2):
                        hs = slice(64 * h, 64 * h + 64)
                        cp("act" if q != 1 else "dve", bufs[pset][h][:, cc, hs], ptr[:, 1 + q, hs], ["ptr"],
                           ["%s%d_%d_%d" % (nm, pset, h, cc)])
                yield

        def preR(g):
            chains = [(cc, h) for cc in range(NCC) for h in range(2)]
            IB = [(psc[0], "psc0"), (psc[1], "psc1"), (pst[:].rearrange("p a b -> p (a b)"), "pst")]
            for ci, (cc, h) in enumerate(chains):
                csl = slice(cc * CH, (cc + 1) * CH)
                ph = slice(64 * h, 64 * h + 64)
                pbank, pk2 = IB[nxt("ib", 3)]
                art2 = ART[ph, cc, :, :].rearrange("p a t -> p (a t)")
                mm(pbank[:, 0:256], BT[ph, csl], art2, ["BT", "ART_A", "ART_R"], [pk2])
                P.add("pe", lambda e, o=pbank[:, 256:512], l=KT[ph, csl], rr=art2:
                      e.matmul(o, l, rr, start=True, stop=False, skip_group_check=True),
                      ["KT", "ART_A", "ART_R"], [pk2])
                P.add("pe", lambda e, o=pbank[:, 384:512], l=KT[ph, csl], rr=RTlo[ph, csl]:
                      e.matmul(o, l, rr, start=False, stop=False, skip_group_check=True), ["KT", "RTlo"], [pk2])
                P.add("pe", lambda e, o=pbank[:, 384:512], l=KTlo[ph, csl], rr=ART[ph, cc, 1, :]:
                      e.matmul(o, l, rr, start=False, stop=True, skip_group_check=True), ["KTlo", "ART_R"], [pk2])
                tt("dve", NTARB[cc][h][:], pbank[:, 0:256], mask2[:], ALU.mult, [pk2, "mask2"],
                   ["NTARB%d_%d" % (cc, h)])
                tt("dve", AKRK[cc][h][:], pbank[:, 256:512], mask2[:], ALU.mult, [pk2, "mask2"],
                   ["AKRK%d_%d" % (cc, h)])
                yield
                pbank, pk2 = IB[nxt("ib", 3)]
                mm(pbank[:, 0:128], ART[ph, cc, 0, :], BT[ph, csl], ["BT", "ART_A"], [pk2])
                tt("dve", Pb[ci][:], pbank[:, 0:128], strictT[:], ALU.mult, [pk2, "strictT"], ["Pb%d" % ci])
                cp("pool", PX[ci][:, 0:128], NTARB[cc][h][:, 0:128], ["NTARB%d_%d" % (cc, h)], ["PX%d" % ci])
                tt("pool", PX[ci][:, 128:256], NTARB[cc][h][:, 0:128], ident[:], ALU.add,
                   ["NTARB%d_%d" % (cc, h), "ident"], ["PX%d" % ci])
                yield
            for j in range(0, 7):
                for ci, (cc, h) in enumerate(chains):
                    pxk, pbk = "PX%d" % ci, "Pb%d" % ci
                    ev = "act" if ci % 2 == 0 else "dve"
                    pbank, pk2 = IB[nxt("ib", 3)]
                    if j == 0:
                        mm(pbank[:, 0:128], Pb[ci][:], PX[ci][:, 0:128], [pbk, pxk], [pk2])
                        mm(pbank[:, 256:384], PX[ci][:, 0:128], Pb[ci][:], [pbk, pxk], [pk2])
                        cp(ev, PX[ci][:, 0:128], pbank[:, 0:128], [pk2], [pxk])
                        cp(ev, Pb[ci][:], pbank[:, 256:384], [pk2], [pbk])
                    elif j < 6:
                        P.add("pe", lambda e, o=pbank[:, 0:256], l=Pb[ci][:], rr=PX[ci][:]:
                              e.matmul(o, l, rr, start=True, stop=False, skip_group_check=True), [pbk, pxk], [pk2])
                        P.add("pe", lambda e, o=pbank[:, 128:256], l=ident[:], rr=PX[ci][:, 128:256]:
                              e.matmul(o, l, rr, start=False, stop=True, skip_group_check=True), [pxk, "ident"], [pk2])
                        P.add("pe", lambda e, o=pbank[:, 256:384], l=PX[ci][:, 0:128], rr=Pb[ci][:]:
                              e.matmul(o, l, rr, start=True, stop=True, skip_group_check=True), [pbk, pxk], [pk2])
                        cp(ev, PX[ci][:], pbank[:, 0:256], [pk2], [pxk])
                        cp(ev, Pb[ci][:], pbank[:, 256:384], [pk2], [pbk])
                    else:
                        mmg(pbank[:, 0:128], [(Pb[ci][:], PX[ci][:, 128:256]),
                                                 (ident[:], PX[ci][:, 128:256])], [pbk, pxk, "ident"], [pk2])
                        cp(ev, XT[cc][h][:], pbank[:, 0:128], [pk2], ["XT%d_%d" % (cc, h)])
                    yield

        def chainR(g):
            reref, rerefk, relast, relastk, relr, relrk = ctx[g]["r"]
            pset = 0
            HK = "Hr%d" % g
            for cc in range(NCC):
                csl = slice(cc * CH, (cc + 1) * CH)
                g0 = nxt("G0", 2)
                ts("dve", G0[g0][:], Hr[:, g, :], reref[:, cc:cc + 1], ALU.mult, [HK, rerefk], ["G0_%d" % g0])
                vk = ["Vpad%d_%d_%d" % (pset, h, cc) for h in range(2)]
                bk = ["Bpad%d_%d_%d" % (pset, h, cc) for h in range(2)]
                kk_ = ["Kpad%d_%d_%d" % (pset, h, cc) for h in range(2)]
                ak = ["AKRK%d_%d" % (cc, h) for h in range(2)]
                nk = ["NTARB%d_%d" % (cc, h) for h in range(2)]
                mmg(pst[:, 0, :], [(AKRK[cc][0][:, 0:128], Vpad[pset][0][:, cc, :]),
                                   (AKRK[cc][1][:, 0:128], Vpad[pset][1][:, cc, :]),
                                   (ART[:, cc, 0, :], G0[g0][:])],
                    ak + vk + ["ART_A", "G0_%d" % g0], ["pst"])
                rb = nxt("RHSb", 2)
                cp("act", RHSb[rb][:], pst[:, 0, :], ["pst"], ["RHSb%d" % rb])
                yield
                us = nxt("Upad", 2)
                for h in range(2):
                    hs = slice(64 * h, 64 * h + 64)
                    mm(pst[:, 1, hs], XT[cc][h][:], RHSb[rb][:, hs], ["XT%d_%d" % (cc, h), "RHSb%d" % rb], ["pst"])
                for h in range(2):
                    hs = slice(64 * h, 64 * h + 64)
                    cp("act", Upad[us][h][:, hs], pst[:, 1, hs], ["pst"], ["Upad%d_%d" % (us, h)])
                yield
                uk = ["Upad%d_%d" % (us, h) for h in range(2)]
                mmg(pst[:, 2, :], [(G0[g0][:], ART[:, cc, 1, :]),
                                   (Upad[us][0][:], NTARB[cc][0][:, 128:256]),
                                   (Upad[us][1][:], NTARB[cc][1][:, 128:256]),
                                   (Vpad[pset][0][:, cc, :], AKRK[cc][0][:, 128:256]),
                                   (Vpad[pset][1][:, cc, :], AKRK[cc][1][:, 128:256])],
                    ["G0_%d" % g0, "ART_R"] + uk + nk + vk + ak, ["pst"])
                mmg(pst[:, 3, :], [(Bpad[pset][0][:, cc, :], Upad[us][0][:]),
                                   (Bpad[pset][1][:, cc, :], Upad[us][1][:]),
                                   (Kpad[pset][0][:, cc, :], Vpad[pset][0][:, cc, :]),
                                   (Kpad[pset][1][:, cc, :], Vpad[pset][1][:, cc, :])],
                    bk + uk + kk_ + vk, ["pst"])
                tb_ = nxt("tmp128", 2)
                ts("dve", tmp128[tb_][:], pst[:, 3, :], relr[:, cc:cc + 1], ALU.mult, ["pst", relrk],
                   ["tmp128_%d" % tb_])
                stt(Hr[:, g, :], Hr[:, g, :], relast[:, cc:cc + 1], tmp128[tb_][:], ALU.mult, ALU.add,
                    [HK, relastk, "tmp128_%d" % tb_], [HK])
                cp("act", YT[:, csl], pst[:, 2, :], ["pst"], ["YT%d" % cc])
                yield

        def normR(g):
            YTK = ["YT%d" % c_ for c_ in range(NCC)]
            mm(pmi[:], bones[:], YT[:], ["bones"] + YTK, ["pmi"])
            yc, yck = ftmp[14][:], "ft14"
            stt(yc, pmi[:], -1.0 / 64, YT[:], ALU.mult, ALU.add, ["pmi"] + YTK, [yck])
            ysq, ysqk = ftmp[15][:], "ft15"
            act(ysq, yc, AF.Square, [yck], [ysqk])
            mm(pmi[:], bones[:], ysq, ["bones", ysqk], ["pmi"])
            ts("dve", ysq, pmi[:], 1.0 / 64, ALU.mult, ["pmi"], [ysqk], s2=GN_EPS, op1=ALU.add)
            act(ysq, ysq, AF.Sqrt, [ysqk], [ysqk])
            recip(ysq, ysq, [ysqk], [ysqk])
            tt("dve", yc, yc, ysq, ALU.mult, [yck, ysqk], [yck])
            ts("dve", yc, yc, col("lnw", g), ALU.mult, [yck, "cols"], [yck], s2=col("lnb", g), op1=ALU.add)
            tt("pool", yc, yc, BON[:], ALU.add, [yck, "BON"], [yck])
            tt("dve", OBT[:, g, :], yc, GG[:], ALU.mult, [yck, "GG"], ["A%d" % (8 + g)])
            yield

        def run_all(*gens):
            gens = list(gens)
            while gens:
                for gg in list(gens):
                    try:
                        next(gg)
                    except StopIteration:
                        gens.remove(gg)

        def seq(*gens):
            for gg in gens:
                yield from gg

        if tb == 0:
            load_Wg(0)
        def par(*gens):
            gens = list(gens)
            while gens:
                for gg in list(gens):
                    try:
                        next(gg)
                        yield
                    except StopIteration:
                        gens.remove(gg)

        P.phase = "G.AB0"
        run_all(stageA(0))
        run_all(stageB(0))
        for g in range(8):
            P.phase = "G.C"
            if g + 1 < 8:
                run_all(chainH(g), seq(stageB2(g), par(preR(g), stageA(g + 1))))
            else:
                run_all(chainH(g), seq(stageB2(g), preR(g)))
            P.phase = "G.D"
            if g + 1 < 8:
                run_all(chainR(g), stageB(g + 1))
            else:
                run_all(chainR(g))
            P.phase = "G.N"
            run_all(normR(g))

        P.phase = "M"
        OAK = ["A%d" % g for g in range(8)]
        OBK = ["A%d" % (8 + g) for g in range(8)]
        def load_Wm(m):
            W, WK = Wm[m % 2]
            ms = slice(m * 128, (m + 1) * 128)
            dma("pool", W[:, 0, :, :], wba_d[:, ms].rearrange("(c p) n -> p c n", p=128), [], [WK[0]])
            dma("pool", W[:, 1, :, :], wbb_d[:, ms].rearrange("(c p) n -> p c n", p=128), [], [WK[1]])
            dma("pool", W[:, 2, :, :], win_d[:, 7424 + m * 128:7424 + (m + 1) * 128].rearrange("(c p) n -> p c n", p=128),
                [], [WK[2]])
            dma("pool", W[:, 3, :, :], win_d[:, 8448 + m * 128:8448 + (m + 1) * 128].rearrange("(c p) n -> p c n", p=128),
                [], [WK[3]])

        load_Wm(0)
        dma("sp", gp[:], gpa_d.partition_broadcast(128), [], ["gp"])
        dma("pool", WO[:], wout_d[:, :].rearrange("(c p) n -> p c n", p=128), [], WOK)
        if tb + 1 < NB:
            dma("pool", Wl[:], win_d[:, 7168:7424].rearrange("(c p) n -> p c n", p=128), [], ["Wl"])
        for m in range(8):
            W, WK = Wm[m % 2]
            if m + 1 < 8:
                load_Wm(m + 1)

            def F():
                i = nxt("ft", NF)
                return ftmp[i][:], "ft%d" % i
            pa, pk = proj_fm(lambda c: W[:, 2, c, :], [WK[2]], xnT, XNT_KEYS, banks=B7)
            sga, sgak = F()
            act(sga, pa, AF.Sigmoid, [pk], [sgak])
            pa, pk = proj_fm(lambda c: W[:, 3, c, :], [WK[3]], xnT, XNT_KEYS, banks=B7)
            sgb, sgbk = F()
            act(sgb, pa, AF.Sigmoid, [pk], [sgbk])
            pa, pk = proj_fm(lambda c: W[:, 0, c, :], [WK[0]], OAT, OAK, banks=B7)
            tt("dve", sga, sga, pa, ALU.mult, [sgak, pk], [sgak])
            pa, pk = proj_fm(lambda c: W[:, 1, c, :], [WK[1]], OBT, OBK, banks=B7)
            tt("dve", sgb, sgb, pa, ALU.mult, [sgbk, pk], [sgbk])
            tt("dve", MT[:, m, :], sga, sgb, ALU.add, [sgak, sgbk], ["A%d" % (16 + m)])
        MTK = ["A%d" % (16 + m) for m in range(8)]
        P.phase = "W"

        def load_WU(j):
            W, WK = WU[j % 5]
            dma("pool", W[:, 0, :, :], wup_d[:, j * 128:(j + 1) * 128].rearrange("(c p) n -> p c n", p=128),
                [], [WK[0]])
            dma("pool", W[:, 1, :, :], wup_d[:, DFF + j * 128:DFF + (j + 1) * 128].rearrange("(c p) n -> p c n", p=128),
                [], [WK[1]])

        def load_WD(j):
            Wd, WdK = WDR[j % 4]
            dma("pool", Wd[:, 0, :], wdn_d[j * 128:(j + 1) * 128, :], [], WdK)

        load_WU(0)
        load_WU(1)
        load_WU(2)
        load_WU(3)

        def post_norm_residual(i, res_ap, res_key, gp, gpk, dst, dkey, bA=None, bB=None):
            if bA is None:
                bA, bB = (pj[0][:], ["pj0"]), (pj[1][:], ["pj1"])
            s = nxt("st1", 4)
            st = st1[s]
            stk = "st1_%d" % s
            act(junk[:, 0:512], bA[0], AF.Square, bA[1], ["junk", stk], accum=st[:, 0:1])
            act(junk[:, 512:1024], bB[0], AF.Square, bB[1], ["junk", stk], accum=st[:, 1:2])
            tt("dve", st[:, 2:3], st[:, 0:1], st[:, 1:2], ALU.add, [stk], [stk])
            ts("dve", st[:, 2:3], st[:, 2:3], 1.0 / D, ALU.mult, [stk], [stk], s2=EPS, op1=ALU.add)
            act(st[:, 2:3], st[:, 2:3], AF.Sqrt, [stk], [stk])
            recip(st[:, 3:4], st[:, 2:3], [stk], [stk])
            for hh, bk in enumerate((bA, bB)):
                sl = slice(hh * 512, (hh + 1) * 512)
                stt(dst[:, sl], bk[0], st[:, 3:4], gp[:, sl], ALU.mult, ALU.mult,
                    bk[1] + [stk, gpk], [dkey])
            tt("pool", dst, dst, res_ap, ALU.add, [dkey, res_key], [dkey])

        for i in range(NCC):
            isl = slice(i * 128, (i + 1) * 128)
            for hh in range(2):
                mmg(pj[hh][:], [(MT[:, c, isl], WO[:, c, hh * 512:(hh + 1) * 512]) for c in range(8)],
                    MTK + WOK, ["pj%d" % hh])
            b = nxt("xt", 2)
            dma("sp", xt[b][:], x_d[t0 + i * 128:t0 + (i + 1) * 128, :], [], ["xt%d" % b])
            post_norm_residual(i, xt[b][:], "xt%d" % b, gp, "gp", hblk[:, i, :], "hblk%d" % i)

        P.phase = "FN"
        norm_T(lambda i: (hblk[:, i, :], "hblk%d" % i), xn2T, "g3", "xnT")
        X2K = ["xnT_%d" % i for i in range(NCC)]
        P.phase = "FU"
        dma("sp", gp[:], gpf_d.partition_broadcast(128), [], ["gp"])
        for j in range(NJ):
            W, WK = WU[j % 5]
            if j + 4 < NJ:
                load_WU(j + 4)
            if j == NJ - 2:
                load_WD(0)
                load_WD(1)
            res = []
            for q in range(2):
                ci = q * NJ + j
                pa, pk = proj_fm(lambda c, q=q: W[:, q, c, :], [WK[q]], xn2T, X2K, banks=B7)
                hk = "halo%d" % ci
                f = nxt("ft", NF)
                A = ftmp[f][:]
                fk = "ft%d" % f
                cwo = CP["cw"]
                w0c = cols[:, cwo + ci:cwo + ci + 1]
                w1c = cols[:, cwo + 44 + ci:cwo + 44 + ci + 1]
                w2c = cols[:, cwo + 2 * 44 + ci:cwo + 2 * 44 + ci + 1]
                bc = cols[:, CP["cb"] + ci:CP["cb"] + ci + 1]
                act(A, pa, AF.Identity, [pk, "cols"], [fk], bias=bc, scale=w2c)
                stt(A[:, 1:TBS], pa[:, 0:TBS - 1], w1c, A[:, 1:TBS], ALU.mult, ALU.add, [pk, "cols", fk], [fk])
                stt(A[:, 2:TBS], pa[:, 0:TBS - 2], w0c, A[:, 2:TBS], ALU.mult, ALU.add, [pk, "cols", fk], [fk])
                stt(A[:, 0:1], halo[:, ci, 1:2], w1c, A[:, 0:1], ALU.mult, ALU.add, [hk, "cols", fk], [fk])
                stt(A[:, 0:1], halo[:, ci, 0:1], w0c, A[:, 0:1], ALU.mult, ALU.add, [hk, "cols", fk], [fk])
                stt(A[:, 1:2], halo[:, ci, 1:2], w0c, A[:, 1:2], ALU.mult, ALU.add, [hk, "cols", fk], [fk])
                cp("act", halo[:, ci, :], pa[:, TBS - 2:TBS], [pk], [hk])
                res.append((A, fk))
            (ga_, gak), (va_, vak) = res
            act(ga_, ga_, AF.Silu, [gak], [gak])
            tt("dve", ACTT[:, j, :], ga_, va_, ALU.mult, [gak, vak], ["A%d" % j])
        AK = ["A%d" % j for j in range(NJ)]
        P.phase = "FD"
        banks = [(pj[0][:], ["pj0"]), (pj[1][:], ["pj1"]), (psc[0][:], ["psc0"]),
                 (psc[1][:], ["psc1"]), (pmi[:], ["pmi"]),
                 (pst[:].rearrange("p a b -> p (a b)"), ["pst"]),
                 (phg[:].rearrange("p a b -> p (a b)"), ["phg"]),
                 (ptr[:].rearrange("p a b -> p (a b)").bitcast(F32), ["ptr"])]
        load_WD(2)
        for j in range(NJ):
            Wd, WdK = WDR[j % 4]
            if j + 3 < NJ:
                load_WD(j + 3)
            for i in range(NCC):
                isl = slice(i * 128, (i + 1) * 128)
                for hh in range(2):
                    bk = banks[2 * i + hh]
                    mm(bk[0], ACTT[:, j, isl], Wd[:, 0, hh * 512:(hh + 1) * 512], ["A%d" % j] + WdK, bk[1],
                       start=(j == 0), stop=(j == NJ - 1))
        if tb + 1 < NB:
            tb_next_g0 = True
            W_, WK_ = Wg[0]
            for j_, so_ in enumerate([0, 1024, 2048, 3072, 4096, 5120, 6144]):
                dma("pool", W_[:, j_, :, :], win_d[:, so_: so_ + 128].rearrange("(c p) n -> p c n", p=128),
                    [], [WK_[j_]])
        for i in range(NCC):
            b = nxt("xt", 2)
            post_norm_residual(i, hblk[:, i, :], "hblk%d" % i, gp, "gp", xt[b][:], "xt%d" % b,
                               bA=banks[2 * i], bB=banks[2 * i + 1])
            dma("sp", out_d[t0 + i * 128:t0 + (i + 1) * 128, :], xt[b][:], ["xt%d" % b], ["out%d_%d" % (tb, i)])

    sems = {}
    for e in ("pe", "act", "dve", "pool", "sp"):
        sems[e] = es.enter_context(nc.semaphore("s_" + e))
    dsems = {}
    for e in ("sp", "pool"):
        for i in range(Prog.NDS):
            dsems[(e, i)] = es.enter_context(nc.semaphore("d_%s%d" % (e, i)))
    allsems = [h.num for h in list(sems.values()) + list(dsems.values())]
    srange = range(min(allsems), max(allsems) + 1)
    nc.gpsimd.sem_clear(srange)
    nc.all_engine_barrier()
    with nc.Block() as block:
        P.emit(nc, block, sems, dsems)
    nc.all_engine_barrier()
    nc.gpsimd.sem_clear(srange)
    nc.all_engine_barrier()
    global _SBUF_USED
    _SBUF_USED = (nc.sbuf_base, nc.sbuf_top)
    es.close()
    return nc


def _colpack(v):
    v = np.asarray(v, dtype=np.float32).reshape(-1, 128)
    return np.ascontiguousarray(v.T)


_NC = None


def kernel(x, attn_pre_norm, w_in, hgrn_lb, hgrn_gnorm, w_branch_a, rwkv_mu, rwkv_w0, rwkv_w2, rwkv_a0,
           rwkv_a2, rwkv_g2, rwkv_k_k, rwkv_k_a, rwkv_r_k, rwkv_ln_w, rwkv_ln_b, w_branch_b, w_out,
           attn_post_norm, ffn_pre_norm, w_up, conv_w, conv_b, w_down, ffn_post_norm):
    global _NC
    f = lambda a: np.ascontiguousarray(np.asarray(a, dtype=np.float32))
    cw = np.asarray(conv_w, dtype=np.float32)[0]
    cols = np.concatenate([
        _colpack(attn_pre_norm[0]), _colpack(hgrn_lb[0]), _colpack(hgrn_lb[1]), _colpack(hgrn_gnorm[0]),
        _colpack(rwkv_mu[0]), _colpack(rwkv_w0[0]), _colpack(rwkv_a0[0]), _colpack(rwkv_k_k[0]),
        _colpack(rwkv_k_a[0]), _colpack(np.asarray(rwkv_r_k[0]).reshape(-1)), _colpack(rwkv_ln_w[0]),
        _colpack(rwkv_ln_b[0]), _colpack(ffn_pre_norm[0]),
        _colpack(cw[0]), _colpack(cw[1]), _colpack(cw[2]), _colpack(conv_b[0]),
    ], axis=1)
    assert cols.shape == (128, NCOL), cols.shape
    shared = {
        "cols": f(cols),
        "w_in": f(w_in[0]), "w_branch_a": f(w_branch_a[0]), "w_branch_b": f(w_branch_b[0]),
        "w_out": f(w_out[0]),
        "w2a2": f(np.concatenate([np.asarray(rwkv_w2[0]), np.asarray(rwkv_a2[0])], axis=0)),
        "g2": f(rwkv_g2[0]),
        "gpost_a": f(np.asarray(attn_post_norm[0]).reshape(1, D)),
        "gpost_f": f(np.asarray(ffn_post_norm[0]).reshape(1, D)),
        "w_up": f(w_up[0]), "w_down": f(w_down[0]),
    }
    if _NC is None:
        _NC = build()
    xs = np.asarray(x, dtype=np.float32)
    in_maps = [dict(shared, x=f(xs[b])) for b in range(8)]
    res = run_bass_kernel_spmd(_NC, in_maps, core_ids=list(range(8)))
    out = np.stack([np.asarray(r["out"]) for r in res.results], axis=0)
    kernel.last_results = res.results
    return out.astype(np.float32)
```

```python
import contextlib
import sys
import math
import numpy as np
import concourse.bass as bass
import concourse.mybir as mybir
from concourse.bass_utils import run_bass_kernel_spmd

F32 = mybir.dt.float32
BF16 = mybir.dt.bfloat16
AF = mybir.ActivationFunctionType
ALU = mybir.AluOpType

T = 2048
D = 1024
TBS = 512
NB = T // TBS
CH = 128
NCC = TBS // CH
DFF = 2816
NJ = DFF // 128
INC = 9472
EPS = 1e-6
GN_EPS = 1e-5 * 64
C0 = math.exp(-0.5)
HSCALE = 128 ** -0.5

CP = {}
_o = 0
for _n, _w in [("g1", 8), ("lb0", 8), ("lb1", 8), ("gn", 8), ("mu", 26), ("w0", 8), ("a0", 8),
               ("kk", 8), ("ka", 8), ("rk", 8), ("lnw", 8), ("lnb", 8), ("g3", 8),
               ("cw", 132), ("cb", 44)]:
    CP[_n] = _o
    _o += _w
NCOL = _o

DEBUG = {}
MAXOPS = None
PSUM_PREFIXES = ("pj", "ptr", "psc", "pst", "phg", "pmi")


class Op:
    __slots__ = ("eng", "fn", "deps", "sig", "seq", "dma", "dsem", "dval", "idx", "tag", "ph")


class Prog:
    NDS = 24

    def __init__(self):
        self.ops = []
        self.lastw = {}
        self.readers = {}
        self.dma_cnt = {}
        self.dma_last = {}
        self.phase = ''

    def add(self, eng, fn, r=(), w=(), dma=False):
        if MAXOPS is not None and len(self.ops) >= MAXOPS:
            return None
        op = Op()
        op.eng, op.fn, op.dma, op.sig, op.seq = eng, fn, dma, False, 0
        op.idx = len(self.ops)
        op.ph = self.phase
        fr = sys._getframe(1)
        tg = []
        while fr is not None and len(tg) < 3:
            tg.append(str(fr.f_lineno))
            fr = fr.f_back
        op.tag = "/".join(tg)
        deps = {}
        for k in r:
            d = self.lastw.get(k)
            if d is not None:
                deps[d.idx] = (d, True)
            if k.startswith(PSUM_PREFIXES):
                for d in self.readers.get(k, ()):
                    if d.eng != eng and d.idx not in deps:
                        deps[d.idx] = (d, False)
        for k in w:
            d = self.lastw.get(k)
            if d is not None and d.idx not in deps:
                deps[d.idx] = (d, False)
            for d in self.readers.get(k, ()):
                if d.idx not in deps:
                    deps[d.idx] = (d, False)
        keep = []
        for d, raw in deps.values():
            if d is op:
                continue
            if d.eng == eng and not d.dma and not dma:
                if eng == "pe":
                    continue
            keep.append(d)
            d.sig = True
        op.deps = keep
        for k in r:
            self.readers.setdefault(k, []).append(op)
        for k in w:
            self.lastw[k] = op
            self.readers[k] = []
        if dma:
            i = self.dma_cnt.get(eng, 0)
            self.dma_cnt[eng] = i + 1
            op.dsem = (eng, i % self.NDS)
            op.dval = 16 * (i // self.NDS + 1)
            prev = self.dma_last.get(op.dsem)
            if prev is not None and prev not in op.deps:
                op.deps.append(prev)
            self.dma_last[op.dsem] = op
        self.ops.append(op)
        return op

    def emit(self, nc, block, sems, dsems):
        cnt = {}
        for op in self.ops:
            if op.dma:
                continue
            if op.sig:
                cnt[op.eng] = cnt.get(op.eng, 0) + 1
                op.seq = cnt[op.eng]
        byeng = {}
        for op in self.ops:
            byeng.setdefault(op.eng, []).append(op)

        def run(e, ename):
            waited = {}
            for op in byeng.get(ename, []):
                need = {}
                for d in op.deps:
                    if d.dma:
                        key, val = ("d",) + d.dsem, d.dval
                    else:
                        key, val = ("e", d.eng), d.seq
                    if val > need.get(key, 0):
                        need[key] = val
                for key, val in need.items():
                    if waited.get(key, 0) >= val:
                        continue
                    waited[key] = val
                    s = dsems[key[1:]] if key[0] == "d" else sems[key[1]]
                    e.wait_ge(s, val)
                inst = op.fn(e)
                if op.dma:
                    inst.then_inc(dsems[op.dsem], 16)
                elif op.sig:
                    inst.then_inc(sems[ename], 1)
            n = self.dma_cnt.get(ename, 0)
            for i in range(min(n, self.NDS)):
                tot = (n - i + self.NDS - 1) // self.NDS
                e.wait_ge(dsems[(ename, i)], 16 * tot)

        @block.sync
        def _(e):
            run(e, "sp")

        @block.tensor
        def _(e):
            run(e, "pe")

        @block.scalar
        def _(e):
            run(e, "act")

        @block.vector
        def _(e):
            run(e, "dve")

        @block.gpsimd
        def _(e):
            run(e, "pool")


def build():
    nc = bass.Bass("TRN2", target_bir_lowering=False)
    global _P
    P = Prog()
    _P = P
    es = contextlib.ExitStack()

    def dram(name, shape, kind="ExternalInput"):
        return nc.dram_tensor(name, shape, F32, kind=kind).ap()

    x_d = dram("x", [T, D])
    cols_d = dram("cols", [128, NCOL])
    win_d = dram("w_in", [D, INC])
    wba_d = dram("w_branch_a", [D, D])
    wbb_d = dram("w_branch_b", [D, D])
    wout_d = dram("w_out", [D, D])
    w2a2_d = dram("w2a2", [128, D])
    g2_d = dram("g2", [128, D])
    gpa_d = dram("gpost_a", [1, D])
    gpf_d = dram("gpost_f", [1, D])
    wup_d = dram("w_up", [D, 2 * DFF])
    wdn_d = dram("w_down", [DFF, D])
    out_d = dram("out", [T, D], kind="ExternalOutput")
    dbg_d = {}
    for k, shp in DEBUG.items():
        dbg_d[k] = dram("dbg_" + k, shp, kind="ExternalOutput")

    def sb(name, shape, dt=F32):
        return es.enter_context(nc.sbuf_tensor("sb_" + name, shape, dt))

    def ps(name, shape, dt=F32):
        return es.enter_context(nc.psum_tensor("ps_" + name, shape, dt))

    def mm(out, lhsT, rhs, r, w, start=True, stop=True):
        P.add("pe", lambda e: e.matmul(out, lhsT, rhs, start=start, stop=stop), r, w)

    def mmg(out, pairs, r, w):
        n = len(pairs)
        for i, (l, rr) in enumerate(pairs):
            mm(out, l, rr, r, w, start=(i == 0), stop=(i == n - 1))

    def tr(out, in_, ident_ap, r, w):
        P.add("pe", lambda e: e.transpose(out, in_, ident_ap), r, w)

    def act(out, in_, func, r, w, bias=None, scale=None, accum=None, eng="act"):
        kw = {}
        if bias is not None:
            kw["bias"] = bias
        if scale is not None:
            kw["scale"] = scale
        if accum is not None:
            kw["accum_out"] = accum
        P.add("act", lambda e: e.activation(out=out, in_=in_, func=func, **kw), r, w)

    def tt(eng, out, a, b, op, r, w):
        P.add(eng, lambda e: e.tensor_tensor(out=out, in0=a, in1=b, op=op), r, w)

    def ts(eng, out, a, s1, op0, r, w, s2=None, op1=None):
        if op1 is None:
            P.add(eng, lambda e: e.tensor_scalar(out=out, in0=a, scalar1=s1, scalar2=None, op0=op0), r, w)
        else:
            P.add(eng, lambda e: e.tensor_scalar(out=out, in0=a, scalar1=s1, scalar2=s2, op0=op0, op1=op1), r, w)

    def stt(out, a, s, b, op0, op1, r, w):
        P.add("dve", lambda e: e.scalar_tensor_tensor(out=out, in0=a, scalar=s, in1=b, op0=op0, op1=op1), r, w)

    def cp(eng, out, in_, r, w):
        if eng == "act":
            P.add("act", lambda e: e.activation(out=out, in_=in_, func=AF.Copy), r, w)
        else:
            P.add(eng, lambda e: e.tensor_copy(out=out, in_=in_), r, w)

    def recip(out, in_, r, w):
        P.add("dve", lambda e: e.reciprocal(out=out, in_=in_), r, w)

    def scan(out, d0, d1, r, w):
        P.add("dve", lambda e: e.tensor_tensor_scan(out=out, data0=d0, data1=d1, initial=0.0,
                                                    op0=ALU.mult, op1=ALU.add), r, w)

    def memset(eng, ap, val, w):
        P.add(eng, lambda e: e.memset(ap, val), (), w)

    def dma(eng, out, in_, r, w):
        P.add(eng, lambda e: e.dma_start(out=out, in_=in_), r, w, dma=True)

    def dbg(name, ap, r):
        if name in dbg_d:
            dma("sp", dbg_d[name], ap, r, ["dbg_" + name])

    ident = sb("ident", [128, 128], BF16)
    identf = sb("identf", [128, 128], F32)
    mask2 = sb("mask2", [128, 256], F32)
    strictT = sb("strictT", [128, 128], F32)
    bones = sb("bones", [128, 128], F32)
    onesf = sb("onesf", [128, 128], F32)
    rmask = sb("rmask", [128, TBS], F32)
    cols = sb("cols", [128, NCOL], F32)
    lbc = sb("lbc", [128, 8], F32)
    omlc = sb("omlc", [128, 8], F32)
    omu = sb("omu", [128, 26], F32)
    gp = sb("gp", [128, D], F32)
    w2a2 = sb("w2a2", [128, D], BF16)
    g2 = sb("g2sb", [128, D], BF16)
    lnsc = sb("lnsc", [128, 1], F32)

    memset("pool", identf[:], 0.0, ["identf"])
    P.add("pool", lambda e: e.affine_select(out=identf[:], in_=identf[:], pattern=[[-1, 128]],
                                            compare_op=ALU.not_equal, fill=1.0, base=0,
                                            channel_multiplier=1), ["identf"], ["identf"])
    cp("pool", ident[:], identf[:], ["identf"], ["ident"])
    memset("pool", mask2[:], 1.0, ["mask2"])
    P.add("pool", lambda e: e.affine_select(out=mask2[:, 0:128], in_=mask2[:, 0:128], pattern=[[1, 128]],
                                            compare_op=ALU.is_gt, fill=0.0, base=0,
                                            channel_multiplier=-1), ["mask2"], ["mask2"])
    P.add("pool", lambda e: e.affine_select(out=mask2[:, 128:256], in_=mask2[:, 128:256], pattern=[[1, 128]],
                                            compare_op=ALU.is_ge, fill=0.0, base=0,
                                            channel_multiplier=-1), ["mask2"], ["mask2"])
    memset("pool", strictT[:], 1.0, ["strictT"])
    P.add("pool", lambda e: e.affine_select(out=strictT[:], in_=strictT[:], pattern=[[-1, 128]],
                                            compare_op=ALU.is_gt, fill=0.0, base=0,
                                            channel_multiplier=1), ["strictT"], ["strictT"])
    memset("pool", bones[:], 0.0, ["bones"])
    memset("pool", bones[0:64, 0:64], 1.0, ["bones"])
    memset("pool", bones[64:128, 64:128], 1.0, ["bones"])
    memset("pool", onesf[:], 1.0, ["onesf"])
    memset("pool", rmask[:], 1.0, ["rmask"])
    memset("pool", rmask[:].rearrange("p (c t) -> p c t", t=CH)[:, :, 0:1], 0.0, ["rmask"])
    memset("pool", lnsc[:], math.log(HSCALE), ["lnsc"])

    dma("sp", cols[:], cols_d[:, :], [], ["cols"])
    dma("pool", w2a2[:], w2a2_d[:, :], [], ["w2a2"])
    dma("pool", g2[:], g2_d[:, :], [], ["g2"])

    def col(name, i, n=1):
        o = CP[name] + i
        return cols[:, o:o + n]

    tt("dve", lbc[:], col("lb0", 0, 8), col("lb1", 0, 8), ALU.subtract, ["cols"], ["lbc"])
    act(lbc[:], lbc[:], AF.Sigmoid, ["lbc"], ["lbc"])
    ts("dve", omlc[:], lbc[:], -1.0, ALU.mult, ["lbc"], ["omlc"], s2=1.0, op1=ALU.add)
    ts("dve", omu[:], cols[:, CP["mu"]:CP["mu"] + 26], -1.0, ALU.mult, ["cols"], ["omu"], s2=1.0, op1=ALU.add)

    Sh = sb("Sh", [128, 8, 128], F32)
    Hr = sb("Hr", [128, 8, 128], F32)
    memset("pool", Sh[:], 0.0, ["Sh%d" % g for g in range(8)])
    memset("pool", Hr[:], 0.0, ["Hr%d" % g for g in range(8)])
    carry = sb("carry", [128, 26], F32)
    memset("pool", carry[:], 0.0, ["carry%d" % i for i in range(26)])
    halo = sb("halo", [128, 44, 2], F32)
    memset("pool", halo[:], 0.0, ["halo%d" % i for i in range(44)])

    NPAD = 1
    Vpad = [[sb("Vpad%d_%d" % (s, h), [128, NCC, 128], BF16) for h in range(2)] for s in range(NPAD)]
    Bpad = [[sb("Bpad%d_%d" % (s, h), [128, NCC, 128], BF16) for h in range(2)] for s in range(NPAD)]
    Kpad = [[sb("Kpad%d_%d" % (s, h), [128, NCC, 128], BF16) for h in range(2)] for s in range(NPAD)]
    Upad = [[sb("Upad%d_%d" % (s, h), [128, 128], BF16) for h in range(2)] for s in range(2)]
    for s in range(NPAD):
        for h in range(2):
            for nm, bufs in (("Vpad", Vpad), ("Bpad", Bpad), ("Kpad", Kpad)):
                memset("pool", bufs[s][h][:], 0.0, ["%s%d_%d_%d" % (nm, s, h, c) for c in range(NCC)])
    for s in range(2):
        for h in range(2):
            memset("pool", Upad[s][h][:], 0.0, ["Upad%d_%d" % (s, h)])

    xt = [sb("xt%d" % i, [128, D], F32) for i in range(2)]
    junk = sb("junk", [128, D], BF16)
    xnb = [sb("xnb%d" % i, [128, D], BF16) for i in range(2)]
    st1 = [sb("st1_%d" % i, [128, 4], F32) for i in range(4)]
    xnT = sb("xnT", [128, 8, TBS], BF16)
    xn2T = xnT
    AAR = sb("AAR", [128, 24, TBS], BF16)
    OAT = AAR[:, 0:8, :]
    OBT = AAR[:, 8:16, :]
    MT = AAR[:, 16:24, :]
    ACTT = AAR[:, 0:NJ, :]
    hblk = sb("hblk", [128, NCC, D], F32)
    LT = sb("LT", [128, TBS], BF16)
    LG = sb("LG", [128, TBS], BF16)
    Wl = sb("Wl", [128, 8, 256], BF16)
    WAR = sb("WAR", [128, 16 * 1024], BF16)

    def arena(slot0, nslots, pattern, **kw):
        ap = WAR[:, slot0 * 1024:(slot0 + nslots) * 1024].rearrange(pattern, **kw)
        return ap, ["ar%d" % i for i in range(slot0, slot0 + nslots)]
    Wg = [arena(7 * i, 7, "p (j c n) -> p j c n", c=8, j=7) for i in range(2)]
    Wm = [arena(4 * i, 4, "p (j c n) -> p j c n", c=8, j=4) for i in range(2)]
    WO, WOK = arena(8, 8, "p (c n) -> p c n", c=8)
    WU = [arena(2 * i, 2, "p (j c n) -> p j c n", c=8, j=2) for i in range(5)]
    WDR = [arena(10 + i, 1, "p (j n) -> p j n", j=1) for i in range(4)]
    NF = 16
    ftmp = [sb("ft%d" % i, [128, TBS], F32) for i in range(NF)]
    Vh2 = [sb("Vh%d" % i, [128, NCC, 128], BF16) for i in range(2)]
    QI2 = [sb("QI%d" % i, [128, TBS], BF16) for i in range(2)]
    KI2 = [sb("KI%d" % i, [128, TBS], BF16) for i in range(2)]
    KItm = [sb("KItm%d" % i, [128, 128], BF16) for i in range(2)]
    SCT = [sb("SCT%d" % i, [128, 128], BF16) for i in range(2)]
    Sg = [sb("Sg%d" % i, [128, 128], BF16) for i in range(2)]
    OT = sb("OT", [128, TBS], F32)
    SHG2 = [sb("SHG%d" % i, [128, TBS], BF16) for i in range(2)]
    sc4 = [sb("sc4_%d" % i, [128, 4], F32) for i in range(16)]
    ART = sb("ART", [128, NCC, 2, 128], BF16)
    BT = sb("BT", [128, TBS], BF16)
    KT = sb("KT", [128, TBS], BF16)
    RTlo = sb("RTlo", [128, TBS], BF16)
    KTlo = sb("KTlo", [128, TBS], BF16)
    VB = sb("VB", [128, TBS], BF16)
    GG = sb("GG", [128, TBS], F32)
    BON = sb("BON", [128, TBS], F32)
    YT = sb("YT", [128, TBS], F32)
    NTARB = [[sb("NTARB%d_%d" % (c, h), [128, 256], BF16) for h in range(2)] for c in range(NCC)]
    AKRK = [[sb("AKRK%d_%d" % (c, h), [128, 256], BF16) for h in range(2)] for c in range(NCC)]
    XT = [[sb("XT%d_%d" % (c, h), [128, 128], BF16) for h in range(2)] for c in range(NCC)]
    PX = [sb("PX%d" % i, [128, 256], BF16) for i in range(8)]
    Pb = [sb("Pb%d" % i, [128, 128], BF16) for i in range(8)]
    G0 = [sb("G0_%d" % i, [128, 128], BF16) for i in range(2)]
    RHSb = [sb("RHSb%d" % i, [128, 128], BF16) for i in range(2)]
    tmp128 = [sb("tmp128_%d" % i, [128, 128], F32) for i in range(2)]

    pj = [ps("pj%d" % i, [128, 512]) for i in range(2)]
    ptr = ps("ptr", [128, 8, 128], BF16)
    psc = [ps("psc%d" % i, [128, 512]) for i in range(2)]
    pst = ps("pst", [128, 4, 128])
    phg = ps("phg", [128, 4, 128])
    pmi = ps("pmi", [128, 512])

    cnt = {"pj": 0, "raw": 0, "ft": 0}

    def nxt(name, n):
        i = cnt.get(name, 0)
        cnt[name] = i + 1
        return i % n

    wcache = {}

    def wload(tag, dst, src, keys, ttb):
        if tag not in wcache:
            n = 1
            for d_ in dst.shape[1:]:
                n *= d_
            wcache[tag] = nc.dram_tensor("wc_" + tag, [128, n], BF16, kind="Internal").ap()
        cv = wcache[tag]
        if len(dst.shape) == 3:
            cv = cv.rearrange("p (a b) -> p a b", a=dst.shape[1])
        if ttb == 0:
            dma("pool", dst, src, [], keys)
            dma("sp", cv, dst, keys, ["wc_" + tag])
        else:
            dma("sp", dst, cv, ["wc_" + tag], keys)

    for tb in range(NB):
        t0 = tb * TBS

        def norm_T(src_tile_fn, dstT, gname, pre):
            for i in range(NCC):
                xb, xk = src_tile_fn(i)
                s = nxt("st1", 4)
                st = st1[s]
                stk = "st1_%d" % s
                act(junk[:], xb, AF.Square, [xk], ["junk", stk], accum=st[:, 0:1])
                ts("dve", st[:, 1:2], st[:, 0:1], 1.0 / D, ALU.mult, [stk], [stk], s2=EPS, op1=ALU.add)
                act(st[:, 1:2], st[:, 1:2], AF.Sqrt, [stk], [stk])
                b = nxt("xnb", 2)
                recip(st[:, 2:3], st[:, 1:2], [stk], [stk])
                ts("dve", xnb[b][:], xb, st[:, 2:3], ALU.mult, [xk, stk], ["xnb%d" % b])
                for c in range(8):
                    tr(ptr[:, c, :], xnb[b][:, c * 128:(c + 1) * 128], ident[:], ["xnb%d" % b, "ident"], ["ptr"])
                tt("dve", dstT[:, :, i * 128:(i + 1) * 128], ptr[:, :, :],
                   cols[:, CP[gname]:CP[gname] + 8].unsqueeze(2).to_broadcast([128, 8, 128]),
                   ALU.mult, ["ptr", "cols"], ["%s_%d" % (pre, i)])

        def xsrc(i):
            b = nxt("xt", 2)
            dma("sp", xt[b][:], x_d[t0 + i * 128:t0 + (i + 1) * 128, :], [], ["xt%d" % b])
            return xt[b][:], "xt%d" % b

        P.phase = "A"
        norm_T(xsrc, xnT, "g1", "xnT")
        XNT_KEYS = ["xnT_%d" % i for i in range(NCC)]

        def shift_lerp(psrc, pkey, muidx, dst, dkey, eng2="dve"):
            ck = "carry%d" % muidx
            muc = col("mu", muidx)
            act(dst, psrc, AF.Identity, [pkey, "omu"], [dkey], scale=omu[:, muidx:muidx + 1])
            stt(dst[:, 1:TBS], psrc[:, 0:TBS - 1], muc, dst[:, 1:TBS], ALU.mult, ALU.add, [pkey, "cols", dkey], [dkey])
            stt(dst[:, 0:1], carry[:, muidx:muidx + 1], muc, dst[:, 0:1], ALU.mult, ALU.add, [ck, "cols", dkey], [dkey])
            cp("act", carry[:, muidx:muidx + 1], psrc[:, TBS - 1:TBS], [pkey], [ck])

        B4 = [(pj[0][:], "pj0"), (pj[1][:], "pj1"), (psc[0][:], "psc0"), (psc[1][:], "psc1")]
        B7 = B4 + [(pst[:].rearrange("p a b -> p (a b)"), "pst"), (phg[:].rearrange("p a b -> p (a b)"), "phg"),
                   (pmi[:], "pmi")]

        def proj_fm(wtile_fn, wkeys, rhsT, rkeys, banks=B4):
            b = nxt("pjb", len(banks))
            bap, bkey = banks[b]
            mmg(bap, [(wtile_fn(c), rhsT[:, c, :]) for c in range(8)], wkeys + rkeys, [bkey])
            return bap, bkey

        P.phase = "L"
        if tb == 0:
            wload("Wl", Wl[:], win_d[:, 7168:7424].rearrange("(c p) n -> p c n", p=128), ["Wl"], 0)
        pa, pk = proj_fm(lambda c: Wl[:, c, 0:128], ["Wl"], xnT, XNT_KEYS)
        f = nxt("ft", NF)
        shift_lerp(pa, pk, 24, ftmp[f][:], "ft%d" % f)
        act(LT[0:64, :], ftmp[f][0:64, :], AF.Tanh, ["ft%d" % f], ["LT"])
        cp("act", LT[64:128, :], ftmp[f][64:128, :], ["ft%d" % f], ["LT"])
        pa, pk = proj_fm(lambda c: Wl[:, c, 128:256], ["Wl"], xnT, XNT_KEYS)
        f = nxt("ft", NF)
        shift_lerp(pa, pk, 25, ftmp[f][:], "ft%d" % f)
        act(LG[:], ftmp[f][:], AF.Sigmoid, ["ft%d" % f], ["LG"])

        segs = [0, 1024, 2048, 3072, 4096, 5120, 6144]

        def load_Wg(g):
            wb = g % 2
            W, WK = Wg[wb]
            for j, so in enumerate(segs):
                wload("Wg%d_%d" % (g, j), W[:, j, :, :],
                      win_d[:, so + g * 128: so + (g + 1) * 128].rearrange("(c p) n -> p c n", p=128),
                      [WK[j]], tb)

        def F():
            i = nxt("ft", NF)
            return ftmp[i][:], "ft%d" % i

        ctx = {}

        def stageA(g):
            c = ctx.setdefault(g, {})
            W, WK = Wg[g % 2]
            if g + 1 < 8:
                load_Wg(g + 1)
            cnt["ft"] = 0
            pp = g % 2
            QI, KI, Vh, SHG = QI2[pp], KI2[pp], Vh2[pp], SHG2[pp]
            QIk, KIk, SHGk = "QI%d" % pp, "KI%d" % pp, "SHG%d" % pp

            def wfn(j):
                return (lambda cc_: W[:, j, cc_, :]), [WK[j]]
            fn_, wkk = wfn(0)
            pa, pk = proj_fm(fn_, wkk, xnT, XNT_KEYS)
            qs, qsk = F()
            act(qs, pa, AF.Silu, [pk], [qsk])
            yield
            fn_, wkk = wfn(1)
            pa, pk = proj_fm(fn_, wkk, xnT, XNT_KEYS)
            fg, fgk = F()
            act(fg, pa, AF.Sigmoid, [pk], [fgk])
            yield
            ts("dve", fg, fg, omlc[:, g:g + 1], ALU.mult, [fgk, "omlc", "lbc"], [fgk], s2=lbc[:, g:g + 1], op1=ALU.add)
            lnf, lnfk = F()
            act(lnf, fg, AF.Ln, [fgk], [lnfk])
            kkh, kkhk = F()
            ts("pool", kkh, fg, -1.0, ALU.mult, [fgk], [kkhk], s2=1.0, op1=ALU.add)
            yield
            bb, bbk = F()
            scan(bb, rmask[:], lnf, ["rmask", lnfk], [bbk])
            b3 = bb.rearrange("p (c t) -> p c t", t=CH)
            dd, ddk = F()
            tt("dve", dd.rearrange("p (c t) -> p c t", t=CH), b3, b3[:, :, 63:64].to_broadcast([128, NCC, CH]),
               ALU.subtract, [bbk], [ddk])
            yield
            e1, e1k = F()
            act(e1, dd, AF.Exp, [ddk, "lnsc"], [e1k], bias=lnsc[:, 0:1])
            tt("dve", QI[:], qs, e1, ALU.mult, [qsk, e1k], [QIk])
            yield
            e2, e2k = F()
            act(e2, dd, AF.Exp, [ddk], [e2k], scale=-1.0)
            tt("pool", KI[:], kkh, e2, ALU.mult, [kkhk, e2k], [KIk])
            yield
            si = nxt("sc4", 16)
            eref, erefk = sc4[si], "sc4_%d" % si
            act(eref[:].unsqueeze(2), b3[:, :, 63:64], AF.Exp, [bbk], [erefk])
            si = nxt("sc4", 16)
            elast, elastk = sc4[si], "sc4_%d" % si
            act(elast[:].unsqueeze(2), b3[:, :, 127:128], AF.Exp, [bbk], [elastk])
            si = nxt("sc4", 16)
            elr, elrk = sc4[si], "sc4_%d" % si
            tt("dve", elr[:].unsqueeze(2), b3[:, :, 127:128], b3[:, :, 63:64], ALU.subtract, [bbk], [elrk])
            act(elr[:], elr[:], AF.Exp, [elrk], [elrk])
            c["h"] = (eref, erefk, elast, elastk, elr, elrk)
            yield
            fn_, wkk = wfn(3)
            pa, pk = proj_fm(fn_, wkk, xnT, XNT_KEYS)
            act(SHG[:], pa, AF.Silu, [pk], [SHGk])
            yield
            for i in range(NCC):
                b = nxt("pj", 2)
                mmg(pj[b][:, 0:128], [(xnT[:, c_, i * 128:(i + 1) * 128], W[:, 2, c_, :]) for c_ in range(8)],
                    [WK[2], "xnT_%d" % i], ["pj%d" % b])
                cp("act", Vh[:, i, :], pj[b][:, 0:128], ["pj%d" % b], ["Vh%d_%d" % (pp, i)])
                yield

        def chainH(g):
            eref, erefk, elast, elastk, elr, elrk = ctx[g]["h"]
            SK = "Sh%d" % g
            pp = g % 2
            QI, KI, Vh, SHG = QI2[pp], KI2[pp], Vh2[pp], SHG2[pp]
            QIk, KIk, SHGk = "QI%d" % pp, "KI%d" % pp, "SHG%d" % pp
            for cc in range(NCC):
                csl = slice(cc * CH, (cc + 1) * CH)
                kb = nxt("KItm", 2)
                tr(ptr[:, 0, :], KI[:, csl], ident[:], [KIk, "ident"], ["ptr"])
                cp("act", KItm[kb][:], ptr[:, 0, :], ["ptr"], ["KItm%d" % kb])
                yield
                mm(phg[:, 0, :], KI[:, csl], QI[:, csl], [KIk, QIk], ["phg"])
                mm(phg[:, 2, :], KItm[kb][:], Vh[:, cc, :], ["KItm%d" % kb, "Vh%d_%d" % (pp, cc)], ["phg"])
                sb_ = nxt("SCT", 2)
                tt("dve", SCT[sb_][:], phg[:, 0, :], mask2[:, 128:256], ALU.mult, ["phg", "mask2"], ["SCT%d" % sb_])
                tb_ = nxt("tmp128", 2)
                ts("dve", tmp128[tb_][:], phg[:, 2, :], elr[:, cc:cc + 1], ALU.mult, ["phg", elrk], ["tmp128_%d" % tb_])
                gb = nxt("Sg", 2)
                ts("dve", Sg[gb][:], Sh[:, g, :], eref[:, cc:cc + 1], ALU.mult, [SK, erefk], ["Sg%d" % gb])
                yield
                mmg(phg[:, 1, :], [(Vh[:, cc, :], SCT[sb_][:]), (Sg[gb][:], QI[:, csl])],
                    ["Vh%d_%d" % (pp, cc), "SCT%d" % sb_, "Sg%d" % gb, QIk], ["phg"])
                stt(Sh[:, g, :], Sh[:, g, :], elast[:, cc:cc + 1], tmp128[tb_][:], ALU.mult, ALU.add,
                    [SK, elastk, "tmp128_%d" % tb_], [SK])
                cp("act", OT[:, csl], phg[:, 1, :], ["phg"], ["OT%d" % cc])
                yield
            OTK = ["OT%d" % c_ for c_ in range(NCC)]
            osq, osqk = ftmp[13][:], "ft13"
            act(osq, OT[:], AF.Square, OTK, [osqk])
            mm(pmi[:], onesf[:], osq, ["onesf", osqk], ["pmi"])
            yield
            sd, sdk = osq, osqk
            ts("dve", sd, pmi[:], 1.0 / 128, ALU.mult, ["pmi"], [sdk], s2=EPS, op1=ALU.add)
            act(sd, sd, AF.Sqrt, [sdk], [sdk])
            recip(sd, sd, [sdk], [sdk])
            yield
            tt("dve", sd, OT[:], sd, ALU.mult, OTK + [sdk], [sdk])
            stt(OAT[:, g, :], sd, col("gn", g), SHG[:], ALU.mult, ALU.mult, [sdk, "cols", SHGk], ["A%d" % g])
            yield

        def stageB(g):
            c = ctx.setdefault(g, {})
            W, WK = Wg[g % 2]
            cnt["ft"] = 0

            def wfn(j):
                return (lambda cc_: W[:, j, cc_, :]), [WK[j]]
            fn_, wkk = wfn(4)
            pa, pk = proj_fm(fn_, wkk, xnT, XNT_KEYS)
            RP, RPk = F()
            shift_lerp(pa, pk, g, RP, RPk)
            yield
            fn_, wkk = wfn(5)
            pa, pk = proj_fm(fn_, wkk, xnT, XNT_KEYS)
            KP, KPk = F()
            shift_lerp(pa, pk, 8 + g, KP, KPk)
            yield
            fn_, wkk = wfn(6)
            pa, pk = proj_fm(fn_, wkk, xnT, XNT_KEYS)
            VP, VPk = F()
            shift_lerp(pa, pk, 16 + g, VP, VPk)
            cp("pool", VB[:], VP, [VPk], ["VB"])
            yield
            gs = slice(g * 128, (g + 1) * 128)
            b = nxt("pj", 2)
            mm(pj[b][:], w2a2[0:64, gs], LT[0:64, :], ["w2a2", "LT"], ["pj%d" % b])
            lwp, lwpk = F()
            act(lwp, pj[b][:], AF.Sigmoid, ["pj%d" % b, "cols"], [lwpk], bias=col("w0", g))
            b = nxt("pj", 2)
            mm(pj[b][:], w2a2[64:128, gs], LT[64:128, :], ["w2a2", "LT"], ["pj%d" % b])
            asg, asgk = F()
            act(asg, pj[b][:], AF.Sigmoid, ["pj%d" % b, "cols"], [asgk], bias=col("a0", g))
            yield
            cwp, cwpk = F()
            scan(cwp, rmask[:], lwp, ["rmask", lwpk], [cwpk])
            c3 = cwp.rearrange("p (c t) -> p c t", t=CH)
            dd, ddk = F()
            tt("dve", dd.rearrange("p (c t) -> p c t", t=CH), c3, c3[:, :, 63:64].to_broadcast([128, NCC, CH]),
               ALU.subtract, [cwpk], [ddk])
            da, dak = F()
            tt("pool", da, dd, lwp, ALU.subtract, [ddk, lwpk], [dak])
            yield
            epos, eposk = F()
            act(epos, dd, AF.Exp, [ddk], [eposk], scale=-C0)
            eneg, enegk = F()
            act(eneg, dd, AF.Exp, [ddk], [enegk], scale=C0)
            act(da, da, AF.Exp, [dak], [dak], scale=-C0)
            yield
            si = nxt("sc4", 16)
            reref, rerefk = sc4[si], "sc4_%d" % si
            act(reref[:].unsqueeze(2), c3[:, :, 63:64], AF.Exp, [cwpk], [rerefk], scale=-C0)
            si = nxt("sc4", 16)
            relast, relastk = sc4[si], "sc4_%d" % si
            act(relast[:].unsqueeze(2), c3[:, :, 127:128], AF.Exp, [cwpk], [relastk], scale=-C0)
            si = nxt("sc4", 16)
            relr, relrk = sc4[si], "sc4_%d" % si
            tt("dve", relr[:].unsqueeze(2), c3[:, :, 127:128], c3[:, :, 63:64], ALU.subtract, [cwpk], [relrk])
            act(relr[:], relr[:], AF.Exp, [relrk], [relrk], scale=-C0)
            c["r"] = (reref, rerefk, relast, relastk, relr, relrk)
            yield
            kkr, kkrk = F()
            ts("dve", kkr, KP, col("kk", g), ALU.mult, [KPk, "cols"], [kkrk])
            sq, sqk = F()
            act(sq, kkr, AF.Square, [kkrk], [sqk])
            mm(pmi[:], bones[:], sq, ["bones", sqk], ["pmi"])
            act(sq, pmi[:], AF.Sqrt, ["pmi"], [sqk])
            yield
            ts("dve", sq, sq, 1e-12, ALU.max, [sqk], [sqk])
            recip(sq, sq, [sqk], [sqk])
            tt("dve", kkr, kkr, sq, ALU.mult, [kkrk, sqk], [kkrk])
            k2, k2k = F()
            ts("pool", k2, asg, -1.0, ALU.add, [asgk, "cols"], [k2k], s2=col("ka", g), op1=ALU.mult)
            stt(k2, k2, 1.0, KP, ALU.add, ALU.mult, [k2k, KPk], [k2k])
            yield
            c["b1"] = dict(RP=RP, RPk=RPk, VP=VP, VPk=VPk, asg=asg, asgk=asgk, da=da, dak=dak, epos=epos,
                           eposk=eposk, eneg=eneg, enegk=enegk, kkr=kkr, kkrk=kkrk, sq=sq, sqk=sqk, k2=k2, k2k=k2k)

        def stageB2(g):
            c = ctx[g]
            d_ = c["b1"]
            RP, RPk, VP, VPk, asg, asgk = d_["RP"], d_["RPk"], d_["VP"], d_["VPk"], d_["asg"], d_["asgk"]
            da, dak, epos, eposk, eneg, enegk = d_["da"], d_["dak"], d_["epos"], d_["eposk"], d_["eneg"], d_["enegk"]
            kkr, kkrk, sq, sqk, k2, k2k = d_["kkr"], d_["kkrk"], d_["sq"], d_["sqk"], d_["k2"], d_["k2k"]
            gs = slice(g * 128, (g + 1) * 128)
            b = nxt("pj", 2)
            mm(pj[b][:], g2[:, gs], LG[:], ["g2", "LG"], ["pj%d" % b])
            cp("act", GG[:], pj[b][:], ["pj%d" % b], ["GG"])
            yield
            stt(sq, RP, col("rk", g), k2, ALU.mult, ALU.mult, [RPk, "cols", k2k], [sqk])
            mm(pmi[:], bones[:], sq, ["bones", sqk], ["pmi"])
            tt("dve", BON[:], pmi[:], VP, ALU.mult, ["pmi", VPk], ["BON"])
            yield
            tt("dve", epos, RP, epos, ALU.mult, [RPk, eposk], [eposk])
            cp("pool", ART[:, :, 1, :], epos.rearrange("p (c t) -> p c t", t=CH), [eposk], ["ART_R"])
            tt("pool", RTlo[:].rearrange("p (c t) -> p c t", t=CH), epos.rearrange("p (c t) -> p c t", t=CH),
               ART[:, :, 1, :], ALU.subtract, [eposk, "ART_R"], ["RTlo"])
            stt(ART[:, :, 0, :], kkr.rearrange("p (c t) -> p c t", t=CH), -1.0,
                da.rearrange("p (c t) -> p c t", t=CH), ALU.mult, ALU.mult, [kkrk, dak], ["ART_A"])
            tt("pool", sq, kkr, asg, ALU.mult, [kkrk, asgk], [sqk])
            tt("dve", BT[:], sq, eneg, ALU.mult, [sqk, enegk], ["BT"])
            tt("dve", k2, k2, eneg, ALU.mult, [k2k, enegk], [k2k])
            cp("pool", KT[:], k2, [k2k], ["KT"])
            tt("pool", KTlo[:], k2, KT[:], ALU.subtract, [k2k, "KT"], ["KTlo"])
            yield
            pset = 0
            trio = ((VB, "VB", Vpad, "Vpad"), (BT, "BT", Bpad, "Bpad"), (KT, "KT", Kpad, "Kpad"))
            for cc in range(NCC):
                csl = slice(cc * CH, (cc + 1) * CH)
                for q, (src, skey, bufs, nm) in enumerate(trio):
                    tr(ptr[:, 1 + q, :], src[:, csl], ident[:], [skey, "ident"], ["ptr"])
                for q, (src, skey, bufs, nm) in enumerate(trio):
                    for h in range(2):
                        hs = slice(64 * h, 64 * h + 64)
                        cp("act" if q != 1 else "dve", bufs[pset][h][:, cc, hs], ptr[:, 1 + q, hs], ["ptr"],
                           ["%s%d_%d_%d" % (nm, pset, h, cc)])
                yield

        def preR(g):
            chains = [(cc, h) for cc in range(NCC) for h in range(2)]
            IB = [(psc[0], "psc0"), (psc[1], "psc1"), (pst[:].rearrange("p a b -> p (a b)"), "pst")]
            for ci, (cc, h) in enumerate(chains):
                csl = slice(cc * CH, (cc + 1) * CH)
                ph = slice(64 * h, 64 * h + 64)
                pbank, pk2 = IB[nxt("ib", 3)]
                art2 = ART[ph, cc, :, :].rearrange("p a t -> p (a t)")
                mm(pbank[:, 0:256], BT[ph, csl], art2, ["BT", "ART_A", "ART_R"], [pk2])
                P.add("pe", lambda e, o=pbank[:, 256:512], l=KT[ph, csl], rr=art2:
                      e.matmul(o, l, rr, start=True, stop=False, skip_group_check=True),
                      ["KT", "ART_A", "ART_R"], [pk2])
                P.add("pe", lambda e, o=pbank[:, 384:512], l=KT[ph, csl], rr=RTlo[ph, csl]:
                      e.matmul(o, l, rr, start=False, stop=False, skip_group_check=True), ["KT", "RTlo"], [pk2])
                P.add("pe", lambda e, o=pbank[:, 384:512], l=KTlo[ph, csl], rr=ART[ph, cc, 1, :]:
                      e.matmul(o, l, rr, start=False, stop=True, skip_group_check=True), ["KTlo", "ART_R"], [pk2])
                tt("dve", NTARB[cc][h][:], pbank[:, 0:256], mask2[:], ALU.mult, [pk2, "mask2"],
                   ["NTARB%d_%d" % (cc, h)])
                tt("dve", AKRK[cc][h][:], pbank[:, 256:512], mask2[:], ALU.mult, [pk2, "mask2"],
                   ["AKRK%d_%d" % (cc, h)])
                yield
                pbank, pk2 = IB[nxt("ib", 3)]
                mm(pbank[:, 0:128], ART[ph, cc, 0, :], BT[ph, csl], ["BT", "ART_A"], [pk2])
                tt("dve", Pb[ci][:], pbank[:, 0:128], strictT[:], ALU.mult, [pk2, "strictT"], ["Pb%d" % ci])
                cp("pool", PX[ci][:, 0:128], NTARB[cc][h][:, 0:128], ["NTARB%d_%d" % (cc, h)], ["PX%d" % ci])
                tt("pool", PX[ci][:, 128:256], NTARB[cc][h][:, 0:128], ident[:], ALU.add,
                   ["NTARB%d_%d" % (cc, h), "ident"], ["PX%d" % ci])
                yield
            for j in range(0, 7):
                for ci, (cc, h) in enumerate(chains):
                    pxk, pbk = "PX%d" % ci, "Pb%d" % ci
                    ev = "act" if ci % 2 == 0 else "dve"
                    pbank, pk2 = IB[nxt("ib", 3)]
                    if j == 0:
                        mm(pbank[:, 0:128], Pb[ci][:], PX[ci][:, 0:128], [pbk, pxk], [pk2])
                        mm(pbank[:, 256:384], PX[ci][:, 0:128], Pb[ci][:], [pbk, pxk], [pk2])
                        cp(ev, PX[ci][:, 0:128], pbank[:, 0:128], [pk2], [pxk])
                        cp(ev, Pb[ci][:], pbank[:, 256:384], [pk2], [pbk])
                    elif j < 6:
                        P.add("pe", lambda e, o=pbank[:, 0:256], l=Pb[ci][:], rr=PX[ci][:]:
                              e.matmul(o, l, rr, start=True, stop=False, skip_group_check=True), [pbk, pxk], [pk2])
                        P.add("pe", lambda e, o=pbank[:, 128:256], l=ident[:], rr=PX[ci][:, 128:256]:
                              e.matmul(o, l, rr, start=False, stop=True, skip_group_check=True), [pxk, "ident"], [pk2])
                        P.add("pe", lambda e, o=pbank[:, 256:384], l=PX[ci][:, 0:128], rr=Pb[ci][:]:
                              e.matmul(o, l, rr, start=True, stop=True, skip_group_check=True), [pbk, pxk], [pk2])
                        cp(ev, PX[ci][:], pbank[:, 0:256], [pk2], [pxk])
                        cp(ev, Pb[ci][:], pbank[:, 256:384], [pk2], [pbk])
                    else:
                        mmg(pbank[:, 0:128], [(Pb[ci][:], PX[ci][:, 128:256]),
                                                 (ident[:], PX[ci][:, 128:256])], [pbk, pxk, "ident"], [pk2])
                        cp(ev, XT[cc][h][:], pbank[:, 0:128], [pk2], ["XT%d_%d" % (cc, h)])
                    yield

        def chainR(g):
            reref, rerefk, relast, relastk, relr, relrk = ctx[g]["r"]
            pset = 0
            HK = "Hr%d" % g
            for cc in range(NCC):
                csl = slice(cc * CH, (cc + 1) * CH)
                g0 = nxt("G0", 2)
                ts("dve", G0[g0][:], Hr[:, g, :], reref[:, cc:cc + 1], ALU.mult, [HK, rerefk], ["G0_%d" % g0])
                vk = ["Vpad%d_%d_%d" % (pset, h, cc) for h in range(2)]
                bk = ["Bpad%d_%d_%d" % (pset, h, cc) for h in range(2)]
                kk_ = ["Kpad%d_%d_%d" % (pset, h, cc) for h in range(2)]
                ak = ["AKRK%d_%d" % (cc, h) for h in range(2)]
                nk = ["NTARB%d_%d" % (cc, h) for h in range(2)]
                mmg(pst[:, 0, :], [(AKRK[cc][0][:, 0:128], Vpad[pset][0][:, cc, :]),
                                   (AKRK[cc][1][:, 0:128], Vpad[pset][1][:, cc, :]),
                                   (ART[:, cc, 0, :], G0[g0][:])],
                    ak + vk + ["ART_A", "G0_%d" % g0], ["pst"])
                rb = nxt("RHSb", 2)
                cp("act", RHSb[rb][:], pst[:, 0, :], ["pst"], ["RHSb%d" % rb])
                yield
                us = nxt("Upad", 2)
                for h in range(2):
                    hs = slice(64 * h, 64 * h + 64)
                    mm(pst[:, 1, hs], XT[cc][h][:], RHSb[rb][:, hs], ["XT%d_%d" % (cc, h), "RHSb%d" % rb], ["pst"])
                for h in range(2):
                    hs = slice(64 * h, 64 * h + 64)
                    cp("act", Upad[us][h][:, hs], pst[:, 1, hs], ["pst"], ["Upad%d_%d" % (us, h)])
                yield
                uk = ["Upad%d_%d" % (us, h) for h in range(2)]
                mmg(pst[:, 2, :], [(G0[g0][:], ART[:, cc, 1, :]),
                                   (Upad[us][0][:], NTARB[cc][0][:, 128:256]),
                                   (Upad[us][1][:], NTARB[cc][1][:, 128:256]),
                                   (Vpad[pset][0][:, cc, :], AKRK[cc][0][:, 128:256]),
                                   (Vpad[pset][1][:, cc, :], AKRK[cc][1][:, 128:256])],
                    ["G0_%d" % g0, "ART_R"] + uk + nk + vk + ak, ["pst"])
                mmg(pst[:, 3, :], [(Bpad[pset][0][:, cc, :], Upad[us][0][:]),
                                   (Bpad[pset][1][:, cc, :], Upad[us][1][:]),
                                   (Kpad[pset][0][:, cc, :], Vpad[pset][0][:, cc, :]),
                                   (Kpad[pset][1][:, cc, :], Vpad[pset][1][:, cc, :])],
                    bk + uk + kk_ + vk, ["pst"])
                tb_ = nxt("tmp128", 2)
                ts("dve", tmp128[tb_][:], pst[:, 3, :], relr[:, cc:cc + 1], ALU.mult, ["pst", relrk],
                   ["tmp128_%d" % tb_])
                stt(Hr[:, g, :], Hr[:, g, :], relast[:, cc:cc + 1], tmp128[tb_][:], ALU.mult, ALU.add,
                    [HK, relastk, "tmp128_%d" % tb_], [HK])
                cp("act", YT[:, csl], pst[:, 2, :], ["pst"], ["YT%d" % cc])
                yield

        def normR(g):
            YTK = ["YT%d" % c_ for c_ in range(NCC)]
            mm(pmi[:], bones[:], YT[:], ["bones"] + YTK, ["pmi"])
            yc, yck = ftmp[14][:], "ft14"
            stt(yc, pmi[:], -1.0 / 64, YT[:], ALU.mult, ALU.add, ["pmi"] + YTK, [yck])
            ysq, ysqk = ftmp[15][:], "ft15"
            act(ysq, yc, AF.Square, [yck], [ysqk])
            mm(pmi[:], bones[:], ysq, ["bones", ysqk], ["pmi"])
            ts("dve", ysq, pmi[:], 1.0 / 64, ALU.mult, ["pmi"], [ysqk], s2=GN_EPS, op1=ALU.add)
            act(ysq, ysq, AF.Sqrt, [ysqk], [ysqk])
            recip(ysq, ysq, [ysqk], [ysqk])
            tt("dve", yc, yc, ysq, ALU.mult, [yck, ysqk], [yck])
            ts("dve", yc, yc, col("lnw", g), ALU.mult, [yck, "cols"], [yck], s2=col("lnb", g), op1=ALU.add)
            tt("pool", yc, yc, BON[:], ALU.add, [yck, "BON"], [yck])
            tt("dve", OBT[:, g, :], yc, GG[:], ALU.mult, [yck, "GG"], ["A%d" % (8 + g)])
            yield

        def run_all(*gens):
            gens = list(gens)
            while gens:
                for gg in list(gens):
                    try:
                        next(gg)
                    except StopIteration:
                        gens.remove(gg)

        def seq(*gens):
            for gg in gens:
                yield from gg

        if tb == 0:
            load_Wg(0)
        def par(*gens):
            gens = list(gens)
            while gens:
                for gg in list(gens):
                    try:
                        next(gg)
                        yield
                    except StopIteration:
                        gens.remove(gg)

        P.phase = "G.AB0"
        run_all(stageA(0))
        run_all(stageB(0))
        for g in range(8):
            P.phase = "G.C"
            if g + 1 < 8:
                run_all(chainH(g), seq(stageB2(g), par(preR(g), stageA(g + 1))))
            else:
                run_all(chainH(g), seq(stageB2(g), preR(g)))
            P.phase = "G.D"
            if g + 1 < 8:
                run_all(chainR(g), stageB(g + 1))
            else:
                run_all(chainR(g))
            P.phase = "G.N"
            run_all(normR(g))

        P.phase = "M"
        OAK = ["A%d" % g for g in range(8)]
        OBK = ["A%d" % (8 + g) for g in range(8)]
        def load_Wm(m):
            W, WK = Wm[m % 2]
            ms = slice(m * 128, (m + 1) * 128)
            wload("Wm%d_0" % m, W[:, 0, :, :], wba_d[:, ms].rearrange("(c p) n -> p c n", p=128), [WK[0]], tb)
            wload("Wm%d_1" % m, W[:, 1, :, :], wbb_d[:, ms].rearrange("(c p) n -> p c n", p=128), [WK[1]], tb)
            wload("Wm%d_2" % m, W[:, 2, :, :],
                  win_d[:, 7424 + m * 128:7424 + (m + 1) * 128].rearrange("(c p) n -> p c n", p=128), [WK[2]], tb)
            wload("Wm%d_3" % m, W[:, 3, :, :],
                  win_d[:, 8448 + m * 128:8448 + (m + 1) * 128].rearrange("(c p) n -> p c n", p=128), [WK[3]], tb)

        load_Wm(0)
        dma("sp", gp[:], gpa_d.partition_broadcast(128), [], ["gp"])
        wload("WO", WO[:], wout_d[:, :].rearrange("(c p) n -> p c n", p=128), WOK, tb)
        if tb + 1 < NB:
            wload("Wl", Wl[:], win_d[:, 7168:7424].rearrange("(c p) n -> p c n", p=128), ["Wl"], tb + 1)
        for m in range(8):
            W, WK = Wm[m % 2]
            if m + 1 < 8:
                load_Wm(m + 1)

            def F():
                i = nxt("ft", NF)
                return ftmp[i][:], "ft%d" % i
            pa, pk = proj_fm(lambda c: W[:, 2, c, :], [WK[2]], xnT, XNT_KEYS, banks=B7)
            sga, sgak = F()
            act(sga, pa, AF.Sigmoid, [pk], [sgak])
            pa, pk = proj_fm(lambda c: W[:, 3, c, :], [WK[3]], xnT, XNT_KEYS, banks=B7)
            sgb, sgbk = F()
            act(sgb, pa, AF.Sigmoid, [pk], [sgbk])
            pa, pk = proj_fm(lambda c: W[:, 0, c, :], [WK[0]], OAT, OAK, banks=B7)
            tt("dve", sga, sga, pa, ALU.mult, [sgak, pk], [sgak])
            pa, pk = proj_fm(lambda c: W[:, 1, c, :], [WK[1]], OBT, OBK, banks=B7)
            tt("dve", sgb, sgb, pa, ALU.mult, [sgbk, pk], [sgbk])
            tt("dve", MT[:, m, :], sga, sgb, ALU.add, [sgak, sgbk], ["A%d" % (16 + m)])
        MTK = ["A%d" % (16 + m) for m in range(8)]
        P.phase = "W"

        def load_WU(j):
            W, WK = WU[j % 5]
            wload("WU%d_0" % j, W[:, 0, :, :], wup_d[:, j * 128:(j + 1) * 128].rearrange("(c p) n -> p c n", p=128),
                  [WK[0]], tb)
            wload("WU%d_1" % j, W[:, 1, :, :],
                  wup_d[:, DFF + j * 128:DFF + (j + 1) * 128].rearrange("(c p) n -> p c n", p=128), [WK[1]], tb)

        def load_WD(j):
            Wd, WdK = WDR[j % 4]
            wload("WD%d" % j, Wd[:, 0, :], wdn_d[j * 128:(j + 1) * 128, :], WdK, tb)

        load_WU(0)
        load_WU(1)
        load_WU(2)
        load_WU(3)

        def post_norm_residual(i, res_ap, res_key, gp, gpk, dst, dkey, bA=None, bB=None):
            if bA is None:
                bA, bB = (pj[0][:], ["pj0"]), (pj[1][:], ["pj1"])
            s = nxt("st1", 4)
            st = st1[s]
            stk = "st1_%d" % s
            act(junk[:, 0:512], bA[0], AF.Square, bA[1], ["junk", stk], accum=st[:, 0:1])
            act(junk[:, 512:1024], bB[0], AF.Square, bB[1], ["junk", stk], accum=st[:, 1:2])
            tt("dve", st[:, 2:3], st[:, 0:1], st[:, 1:2], ALU.add, [stk], [stk])
            ts("dve", st[:, 2:3], st[:, 2:3], 1.0 / D, ALU.mult, [stk], [stk], s2=EPS, op1=ALU.add)
            act(st[:, 2:3], st[:, 2:3], AF.Sqrt, [stk], [stk])
            recip(st[:, 3:4], st[:, 2:3], [stk], [stk])
            for hh, bk in enumerate((bA, bB)):
                sl = slice(hh * 512, (hh + 1) * 512)
                stt(dst[:, sl], bk[0], st[:, 3:4], gp[:, sl], ALU.mult, ALU.mult,
                    bk[1] + [stk, gpk], [dkey])
            tt("pool", dst, dst, res_ap, ALU.add, [dkey, res_key], [dkey])

        for i in range(NCC):
            isl = slice(i * 128, (i + 1) * 128)
            for hh in range(2):
                mmg(pj[hh][:], [(MT[:, c, isl], WO[:, c, hh * 512:(hh + 1) * 512]) for c in range(8)],
                    MTK + WOK, ["pj%d" % hh])
            b = nxt("xt", 2)
            dma("sp", xt[b][:], x_d[t0 + i * 128:t0 + (i + 1) * 128, :], [], ["xt%d" % b])
            post_norm_residual(i, xt[b][:], "xt%d" % b, gp, "gp", hblk[:, i, :], "hblk%d" % i)

        P.phase = "FN"
        norm_T(lambda i: (hblk[:, i, :], "hblk%d" % i), xn2T, "g3", "xnT")
        X2K = ["xnT_%d" % i for i in range(NCC)]
        P.phase = "FU"
        dma("sp", gp[:], gpf_d.partition_broadcast(128), [], ["gp"])
        for j in range(NJ):
            W, WK = WU[j % 5]
            if j + 4 < NJ:
                load_WU(j + 4)
            if j == NJ - 2:
                load_WD(0)
                load_WD(1)
            res = []
            for q in range(2):
                ci = q * NJ + j
                pa, pk = proj_fm(lambda c, q=q: W[:, q, c, :], [WK[q]], xn2T, X2K, banks=B7)
                hk = "halo%d" % ci
                f = nxt("ft", NF)
                A = ftmp[f][:]
                fk = "ft%d" % f
                cwo = CP["cw"]
                w0c = cols[:, cwo + ci:cwo + ci + 1]
                w1c = cols[:, cwo + 44 + ci:cwo + 44 + ci + 1]
                w2c = cols[:, cwo + 2 * 44 + ci:cwo + 2 * 44 + ci + 1]
                bc = cols[:, CP["cb"] + ci:CP["cb"] + ci + 1]
                act(A, pa, AF.Identity, [pk, "cols"], [fk], bias=bc, scale=w2c)
                stt(A[:, 1:TBS], pa[:, 0:TBS - 1], w1c, A[:, 1:TBS], ALU.mult, ALU.add, [pk, "cols", fk], [fk])
                stt(A[:, 2:TBS], pa[:, 0:TBS - 2], w0c, A[:, 2:TBS], ALU.mult, ALU.add, [pk, "cols", fk], [fk])
                stt(A[:, 0:1], halo[:, ci, 1:2], w1c, A[:, 0:1], ALU.mult, ALU.add, [hk, "cols", fk], [fk])
                stt(A[:, 0:1], halo[:, ci, 0:1], w0c, A[:, 0:1], ALU.mult, ALU.add, [hk, "cols", fk], [fk])
                stt(A[:, 1:2], halo[:, ci, 1:2], w0c, A[:, 1:2], ALU.mult, ALU.add, [hk, "cols", fk], [fk])
                cp("act", halo[:, ci, :], pa[:, TBS - 2:TBS], [pk], [hk])
                res.append((A, fk))
            (ga_, gak), (va_, vak) = res
            act(ga_, ga_, AF.Silu, [gak], [gak])
            tt("dve", ACTT[:, j, :], ga_, va_, ALU.mult, [gak, vak], ["A%d" % j])
        AK = ["A%d" % j for j in range(NJ)]
        P.phase = "FD"
        banks = [(pj[0][:], ["pj0"]), (pj[1][:], ["pj1"]), (psc[0][:], ["psc0"]),
                 (psc[1][:], ["psc1"]), (pmi[:], ["pmi"]),
                 (pst[:].rearrange("p a b -> p (a b)"), ["pst"]),
                 (phg[:].rearrange("p a b -> p (a b)"), ["phg"]),
                 (ptr[:].rearrange("p a b -> p (a b)").bitcast(F32), ["ptr"])]
        load_WD(2)
        for j in range(NJ):
            Wd, WdK = WDR[j % 4]
            if j + 3 < NJ:
                load_WD(j + 3)
            for i in range(NCC):
                isl = slice(i * 128, (i + 1) * 128)
                for hh in range(2):
                    bk = banks[2 * i + hh]
                    mm(bk[0], ACTT[:, j, isl], Wd[:, 0, hh * 512:(hh + 1) * 512], ["A%d" % j] + WdK, bk[1],
                       start=(j == 0), stop=(j == NJ - 1))
        if tb + 1 < NB:
            tb_next_g0 = True
            W_, WK_ = Wg[0]
            for j_, so_ in enumerate([0, 1024, 2048, 3072, 4096, 5120, 6144]):
                wload("Wg0_%d" % j_, W_[:, j_, :, :], win_d[:, so_: so_ + 128].rearrange("(c p) n -> p c n", p=128),
                      [WK_[j_]], tb + 1)
        for i in range(NCC):
            b = nxt("xt", 2)
            post_norm_residual(i, hblk[:, i, :], "hblk%d" % i, gp, "gp", xt[b][:], "xt%d" % b,
                               bA=banks[2 * i], bB=banks[2 * i + 1])
            dma("sp", out_d[t0 + i * 128:t0 + (i + 1) * 128, :], xt[b][:], ["xt%d" % b], ["out%d_%d" % (tb, i)])

    sems = {}
    for e in ("pe", "act", "dve", "pool", "sp"):
        sems[e] = es.enter_context(nc.semaphore("s_" + e))
    dsems = {}
    for e in ("sp", "pool"):
        for i in range(Prog.NDS):
            dsems[(e, i)] = es.enter_context(nc.semaphore("d_%s%d" % (e, i)))
    allsems = [h.num for h in list(sems.values()) + list(dsems.values())]
    srange = range(min(allsems), max(allsems) + 1)
    nc.gpsimd.sem_clear(srange)
    nc.all_engine_barrier()
    with nc.Block() as block:
        P.emit(nc, block, sems, dsems)
    nc.all_engine_barrier()
    nc.gpsimd.sem_clear(srange)
    nc.all_engine_barrier()
    global _SBUF_USED
    _SBUF_USED = (nc.sbuf_base, nc.sbuf_top)
    es.close()
    return nc


def _colpack(v):
    v = np.asarray(v, dtype=np.float32).reshape(-1, 128)
    return np.ascontiguousarray(v.T)


_NC = None


def kernel(x, attn_pre_norm, w_in, hgrn_lb, hgrn_gnorm, w_branch_a, rwkv_mu, rwkv_w0, rwkv_w2, rwkv_a0,
           rwkv_a2, rwkv_g2, rwkv_k_k, rwkv_k_a, rwkv_r_k, rwkv_ln_w, rwkv_ln_b, w_branch_b, w_out,
           attn_post_norm, ffn_pre_norm, w_up, conv_w, conv_b, w_down, ffn_post_norm):
    global _NC
    f = lambda a: np.ascontiguousarray(np.asarray(a, dtype=np.float32))
    cw = np.asarray(conv_w, dtype=np.float32)[0]
    cols = np.concatenate([
        _colpack(attn_pre_norm[0]), _colpack(hgrn_lb[0]), _colpack(hgrn_lb[1]), _colpack(hgrn_gnorm[0]),
        _colpack(rwkv_mu[0]), _colpack(rwkv_w0[0]), _colpack(rwkv_a0[0]), _colpack(rwkv_k_k[0]),
        _colpack(rwkv_k_a[0]), _colpack(np.asarray(rwkv_r_k[0]).reshape(-1)), _colpack(rwkv_ln_w[0]),
        _colpack(rwkv_ln_b[0]), _colpack(ffn_pre_norm[0]),
        _colpack(cw[0]), _colpack(cw[1]), _colpack(cw[2]), _colpack(conv_b[0]),
    ], axis=1)
    assert cols.shape == (128, NCOL), cols.shape
    shared = {
        "cols": f(cols),
        "w_in": f(w_in[0]), "w_branch_a": f(w_branch_a[0]), "w_branch_b": f(w_branch_b[0]),
        "w_out": f(w_out[0]),
        "w2a2": f(np.concatenate([np.asarray(rwkv_w2[0]), np.asarray(rwkv_a2[0])], axis=0)),
        "g2": f(rwkv_g2[0]),
        "gpost_a": f(np.asarray(attn_post_norm[0]).reshape(1, D)),
        "gpost_f": f(np.asarray(ffn_post_norm[0]).reshape(1, D)),
        "w_up": f(w_up[0]), "w_down": f(w_down[0]),
    }
    if _NC is None:
        _NC = build()
    xs = np.asarray(x, dtype=np.float32)
    in_maps = [dict(shared, x=f(xs[b])) for b in range(8)]
    res = run_bass_kernel_spmd(_NC, in_maps, core_ids=list(range(8)))
    out = np.stack([np.asarray(r["out"]) for r in res.results], axis=0)
    kernel.last_results = res.results
    return out.astype(np.float32)
```

```python
import contextlib
import sys
import math
import numpy as np
import concourse.bass as bass
import concourse.mybir as mybir
from concourse.bass_utils import run_bass_kernel_spmd

F32 = mybir.dt.float32
BF16 = mybir.dt.bfloat16
AF = mybir.ActivationFunctionType
ALU = mybir.AluOpType

T = 2048
D = 1024
TBS = 512
NB = T // TBS
CH = 128
NCC = TBS // CH
DFF = 2816
NJ = DFF // 128
INC = 9472
EPS = 1e-6
GN_EPS = 1e-5 * 64
C0 = math.exp(-0.5)
HSCALE = 128 ** -0.5

CP = {}
_o = 0
for _n, _w in [("g1", 8), ("lb0", 8), ("lb1", 8), ("gn", 8), ("mu", 26), ("w0", 8), ("a0", 8),
               ("kk", 8), ("ka", 8), ("rk", 8), ("lnw", 8), ("lnb", 8), ("g3", 8),
               ("cw", 132), ("cb", 44)]:
    CP[_n] = _o
    _o += _w
NCOL = _o

DEBUG = {}
MAXOPS = None
SCHED = True
CSTRIDE = 5
PSUM_PREFIXES = ("pj", "ptr", "psc", "pst", "phg", "pmi")


class Op:
    __slots__ = ("eng", "fn", "deps", "sig", "seq", "dma", "dsem", "dval", "idx", "tag", "ph", "alldeps", "cost", "lat")


class Prog:
    NDS = 24

    def __init__(self):
        self.ops = []
        self.lastw = {}
        self.readers = {}
        self.dma_cnt = {}
        self.dma_last = {}
        self.dma_prevq = {}
        self.phase = ''

    def add(self, eng, fn, r=(), w=(), dma=False, cost=0.3):
        if MAXOPS is not None and len(self.ops) >= MAXOPS:
            return None
        op = Op()
        op.eng, op.fn, op.dma, op.sig, op.seq = eng, fn, dma, False, 0
        op.idx = len(self.ops)
        op.ph = self.phase
        fr = sys._getframe(1)
        tg = []
        while fr is not None and len(tg) < 3:
            tg.append(str(fr.f_lineno))
            fr = fr.f_back
        op.tag = "/".join(tg)
        deps = {}
        for k in r:
            d = self.lastw.get(k)
            if d is not None:
                deps[d.idx] = (d, True)
            if k.startswith(PSUM_PREFIXES):
                for d in self.readers.get(k, ()):
                    if d.eng != eng and d.idx not in deps:
                        deps[d.idx] = (d, False)
        for k in w:
            d = self.lastw.get(k)
            if d is not None and d.idx not in deps:
                deps[d.idx] = (d, False)
            for d in self.readers.get(k, ()):
                if d.idx not in deps:
                    deps[d.idx] = (d, False)
        keep = []
        op.alldeps = [d for d, raw in deps.values() if d is not op]
        op.cost = cost
        op.lat = cost
        if dma:
            op.cost = 1.0 if eng == "pool" else 0.15
            op.lat = cost
            prevq = self.dma_prevq.get(eng)
            if prevq is not None:
                op.alldeps.append(prevq)
            self.dma_prevq[eng] = op
        for d, raw in deps.values():
            if d is op:
                continue
            if d.eng == eng and not d.dma and not dma:
                if eng == "pe":
                    continue
            keep.append(d)
            d.sig = True
        op.deps = keep
        for k in r:
            self.readers.setdefault(k, []).append(op)
        for k in w:
            self.lastw[k] = op
            self.readers[k] = []
        if dma:
            i = self.dma_cnt.get(eng, 0)
            self.dma_cnt[eng] = i + 1
            op.dsem = (eng, i % self.NDS)
            op.dval = 16 * (i // self.NDS + 1)
            prev = self.dma_last.get(op.dsem)
            if prev is not None and prev not in op.deps:
                op.deps.append(prev)
            if prev is not None and prev not in op.alldeps:
                op.alldeps.append(prev)
            self.dma_last[op.dsem] = op
        self.ops.append(op)
        return op

    def schedule(self):
        import heapq
        ops = self.ops
        n = len(ops)
        ndep = [0] * n
        users = [[] for _ in range(n)]
        for op in ops:
            ds = {d.idx for d in op.alldeps}
            ndep[op.idx] = len(ds)
            for di in ds:
                users[di].append(op.idx)
        finish = [0.0] * n
        ready = [0.0] * n
        efree = {}
        heap = []
        for op in ops:
            if ndep[op.idx] == 0:
                heapq.heappush(heap, (0.0, op.idx))
        order = []
        HOP = 0.3
        while heap:
            key, i = heapq.heappop(heap)
            op = ops[i]
            st = max(ready[i], efree.get(op.eng, 0.0))
            if st > key + 1e-9:
                heapq.heappush(heap, (st, i))
                continue
            efree[op.eng] = st + op.cost
            finish[i] = st + op.lat
            order.append((st, i))
            for u in users[i]:
                uo = ops[u]
                lat = HOP if uo.eng != op.eng else 0.1
                if uo.eng == "pe" and op.eng == "pe":
                    lat = 0.0
                ready[u] = max(ready[u], finish[i] + lat)
                ndep[u] -= 1
                if ndep[u] == 0:
                    heapq.heappush(heap, (ready[u], u))
        assert len(order) == n, (len(order), n)
        order.sort()
        self.ops = [ops[i] for _, i in order]
        self.est_total = max(finish)

    def emit(self, nc, block, sems, dsems):
        if SCHED:
            self.schedule()
        cnt = {}
        for op in self.ops:
            if op.dma:
                continue
            if op.sig:
                cnt[op.eng] = cnt.get(op.eng, 0) + 1
                op.seq = cnt[op.eng]
        byeng = {}
        for op in self.ops:
            byeng.setdefault(op.eng, []).append(op)

        def run(e, ename):
            waited = {}
            for op in byeng.get(ename, []):
                need = {}
                for d in op.deps:
                    if d.dma:
                        key, val = ("d",) + d.dsem, d.dval
                    else:
                        key, val = ("e", d.eng), d.seq
                    if val > need.get(key, 0):
                        need[key] = val
                for key, val in need.items():
                    if waited.get(key, 0) >= val:
                        continue
                    waited[key] = val
                    s = dsems[key[1:]] if key[0] == "d" else sems[key[1]]
                    e.wait_ge(s, val)
                inst = op.fn(e)
                if op.dma:
                    inst.then_inc(dsems[op.dsem], 16)
                elif op.sig:
                    inst.then_inc(sems[ename], 1)
            n = self.dma_cnt.get(ename, 0)
            for i in range(min(n, self.NDS)):
                tot = (n - i + self.NDS - 1) // self.NDS
                e.wait_ge(dsems[(ename, i)], 16 * tot)

        @block.sync
        def _(e):
            run(e, "sp")

        @block.tensor
        def _(e):
            run(e, "pe")

        @block.scalar
        def _(e):
            run(e, "act")

        @block.vector
        def _(e):
            run(e, "dve")

        @block.gpsimd
        def _(e):
            run(e, "pool")


def build():
    nc = bass.Bass("TRN2", target_bir_lowering=False)
    global _P
    P = Prog()
    _P = P
    es = contextlib.ExitStack()

    def dram(name, shape, kind="ExternalInput"):
        return nc.dram_tensor(name, shape, F32, kind=kind).ap()

    x_d = dram("x", [T, D])
    cols_d = dram("cols", [128, NCOL])
    win_d = dram("w_in", [D, INC])
    wba_d = dram("w_branch_a", [D, D])
    wbb_d = dram("w_branch_b", [D, D])
    wout_d = dram("w_out", [D, D])
    w2a2_d = dram("w2a2", [128, D])
    g2_d = dram("g2", [128, D])
    gpa_d = dram("gpost_a", [1, D])
    gpf_d = dram("gpost_f", [1, D])
    wup_d = dram("w_up", [D, 2 * DFF])
    wdn_d = dram("w_down", [DFF, D])
    out_d = dram("out", [T, D], kind="ExternalOutput")
    dbg_d = {}
    for k, shp in DEBUG.items():
        dbg_d[k] = dram("dbg_" + k, shp, kind="ExternalOutput")

    def sb(name, shape, dt=F32):
        return es.enter_context(nc.sbuf_tensor("sb_" + name, shape, dt))

    def ps(name, shape, dt=F32):
        return es.enter_context(nc.psum_tensor("ps_" + name, shape, dt))

    def fsz(ap):
        n_ = 1
        for d_ in ap.shape[1:]:
            n_ *= d_
        return n_

    def mm(out, lhsT, rhs, r, w, start=True, stop=True):
        f32 = (rhs.dtype == F32)
        c_ = 0.11 + fsz(rhs) / 2000.0 * (4.0 if f32 else 1.0)
        P.add("pe", lambda e: e.matmul(out, lhsT, rhs, start=start, stop=stop), r, w, cost=c_)

    def mmg(out, pairs, r, w):
        n = len(pairs)
        for i, (l, rr) in enumerate(pairs):
            mm(out, l, rr, r, w, start=(i == 0), stop=(i == n - 1))

    def tr(out, in_, ident_ap, r, w):
        P.add("pe", lambda e: e.transpose(out, in_, ident_ap), r, w, cost=0.2)

    def act(out, in_, func, r, w, bias=None, scale=None, accum=None, eng="act"):
        kw = {}
        if bias is not None:
            kw["bias"] = bias
        if scale is not None:
            kw["scale"] = scale
        if accum is not None:
            kw["accum_out"] = accum
        P.add("act", lambda e: e.activation(out=out, in_=in_, func=func, **kw), r, w,
              cost=0.25 + fsz(in_) * 0.00085 + (0.1 if accum is not None else 0.0))

    def tt(eng, out, a, b, op, r, w):
        P.add(eng, lambda e: e.tensor_tensor(out=out, in0=a, in1=b, op=op), r, w,
              cost=(0.1 + fsz(out) / 900.0) if eng == "dve" else (0.3 + fsz(out) * 0.0022))

    def ts(eng, out, a, s1, op0, r, w, s2=None, op1=None):
        if op1 is None:
            P.add(eng, lambda e: e.tensor_scalar(out=out, in0=a, scalar1=s1, scalar2=None, op0=op0), r, w,
                  cost=(0.1 + fsz(out) / 900.0) if eng == "dve" else (0.3 + fsz(out) * 0.0022))
        else:
            P.add(eng, lambda e: e.tensor_scalar(out=out, in0=a, scalar1=s1, scalar2=s2, op0=op0, op1=op1), r, w,
                  cost=(0.1 + fsz(out) / 900.0) if eng == "dve" else (0.3 + fsz(out) * 0.0022))

    def stt(out, a, s, b, op0, op1, r, w):
        P.add("dve", lambda e: e.scalar_tensor_tensor(out=out, in0=a, scalar=s, in1=b, op0=op0, op1=op1), r, w,
              cost=0.1 + fsz(out) / 900.0)

    def cp(eng, out, in_, r, w):
        if eng == "act":
            P.add("act", lambda e: e.activation(out=out, in_=in_, func=AF.Copy), r, w,
                  cost=0.25 + fsz(in_) * 0.00085)
        else:
            P.add(eng, lambda e: e.tensor_copy(out=out, in_=in_), r, w,
                  cost=(0.08 + fsz(out) / 1100.0) if eng == "dve" else (0.3 + fsz(out) * 0.003))

    def recip(out, in_, r, w):
        P.add("dve", lambda e: e.reciprocal(out=out, in_=in_), r, w, cost=0.1 + fsz(out) * 0.004)

    def scan(out, d0, d1, r, w):
        P.add("dve", lambda e: e.tensor_tensor_scan(out=out, data0=d0, data1=d1, initial=0.0,
                                                    op0=ALU.mult, op1=ALU.add), r, w, cost=0.1 + fsz(out) / 450.0)

    def memset(eng, ap, val, w):
        P.add(eng, lambda e: e.memset(ap, val), (), w)

    def dma(eng, out, in_, r, w):
        P.add(eng, lambda e: e.dma_start(out=out, in_=in_), r, w, dma=True,
              cost=(3.0 + fsz(out) * 128 * 2 / 150e3) if eng == "pool" else (2.5 + fsz(out) * 128 * 3 / 200e3))

    def dbg(name, ap, r):
        if name in dbg_d:
            dma("sp", dbg_d[name], ap, r, ["dbg_" + name])

    ident = sb("ident", [128, 128], BF16)
    identf = sb("identf", [128, 128], F32)
    mask2 = sb("mask2", [128, 256], F32)
    strictT = sb("strictT", [128, 128], F32)
    bones = sb("bones", [128, 128], F32)
    onesf = sb("onesf", [128, 128], F32)
    rmask = sb("rmask", [128, TBS], F32)
    cols = sb("cols", [128, NCOL], F32)
    lbc = sb("lbc", [128, 8], F32)
    omlc = sb("omlc", [128, 8], F32)
    omu = sb("omu", [128, 26], F32)
    gp = sb("gp", [128, D], F32)
    w2a2 = sb("w2a2", [128, D], BF16)
    g2 = sb("g2sb", [128, D], BF16)
    lnsc = sb("lnsc", [128, 1], F32)

    memset("pool", identf[:], 0.0, ["identf"])
    P.add("pool", lambda e: e.affine_select(out=identf[:], in_=identf[:], pattern=[[-1, 128]],
                                            compare_op=ALU.not_equal, fill=1.0, base=0,
                                            channel_multiplier=1), ["identf"], ["identf"])
    cp("pool", ident[:], identf[:], ["identf"], ["ident"])
    memset("pool", mask2[:], 1.0, ["mask2"])
    P.add("pool", lambda e: e.affine_select(out=mask2[:, 0:128], in_=mask2[:, 0:128], pattern=[[1, 128]],
                                            compare_op=ALU.is_gt, fill=0.0, base=0,
                                            channel_multiplier=-1), ["mask2"], ["mask2"])
    P.add("pool", lambda e: e.affine_select(out=mask2[:, 128:256], in_=mask2[:, 128:256], pattern=[[1, 128]],
                                            compare_op=ALU.is_ge, fill=0.0, base=0,
                                            channel_multiplier=-1), ["mask2"], ["mask2"])
    memset("pool", strictT[:], 1.0, ["strictT"])
    P.add("pool", lambda e: e.affine_select(out=strictT[:], in_=strictT[:], pattern=[[-1, 128]],
                                            compare_op=ALU.is_gt, fill=0.0, base=0,
                                            channel_multiplier=1), ["strictT"], ["strictT"])
    memset("pool", bones[:], 0.0, ["bones"])
    memset("pool", bones[0:64, 0:64], 1.0, ["bones"])
    memset("pool", bones[64:128, 64:128], 1.0, ["bones"])
    memset("pool", onesf[:], 1.0, ["onesf"])
    memset("pool", rmask[:], 1.0, ["rmask"])
    memset("pool", rmask[:].rearrange("p (c t) -> p c t", t=CH)[:, :, 0:1], 0.0, ["rmask"])
    memset("pool", lnsc[:], math.log(HSCALE), ["lnsc"])

    dma("sp", cols[:], cols_d[:, :], [], ["cols"])
    dma("pool", w2a2[:], w2a2_d[:, :], [], ["w2a2"])
    dma("pool", g2[:], g2_d[:, :], [], ["g2"])

    def col(name, i, n=1):
        o = CP[name] + i
        return cols[:, o:o + n]

    tt("dve", lbc[:], col("lb0", 0, 8), col("lb1", 0, 8), ALU.subtract, ["cols"], ["lbc"])
    act(lbc[:], lbc[:], AF.Sigmoid, ["lbc"], ["lbc"])
    ts("dve", omlc[:], lbc[:], -1.0, ALU.mult, ["lbc"], ["omlc"], s2=1.0, op1=ALU.add)
    ts("dve", omu[:], cols[:, CP["mu"]:CP["mu"] + 26], -1.0, ALU.mult, ["cols"], ["omu"], s2=1.0, op1=ALU.add)

    Sh = sb("Sh", [128, 8, 128], F32)
    Hr = sb("Hr", [128, 8, 128], F32)
    memset("pool", Sh[:], 0.0, ["Sh%d" % g for g in range(8)])
    memset("pool", Hr[:], 0.0, ["Hr%d" % g for g in range(8)])
    carry = sb("carry", [128, 26], F32)
    memset("pool", carry[:], 0.0, ["carry%d" % i for i in range(26)])
    halo = sb("halo", [128, 44, 2], F32)
    memset("pool", halo[:], 0.0, ["halo%d" % i for i in range(44)])

    NPAD = 1
    Vpad = [[sb("Vpad%d_%d" % (s, h), [128, NCC, 128], BF16) for h in range(2)] for s in range(NPAD)]
    Bpad = [[sb("Bpad%d_%d" % (s, h), [128, NCC, 128], BF16) for h in range(2)] for s in range(NPAD)]
    Kpad = [[sb("Kpad%d_%d" % (s, h), [128, NCC, 128], BF16) for h in range(2)] for s in range(NPAD)]
    Upad = [[sb("Upad%d_%d" % (s, h), [128, 128], BF16) for h in range(2)] for s in range(2)]
    for s in range(NPAD):
        for h in range(2):
            for nm, bufs in (("Vpad", Vpad), ("Bpad", Bpad), ("Kpad", Kpad)):
                memset("pool", bufs[s][h][:], 0.0, ["%s%d_%d_%d" % (nm, s, h, c) for c in range(NCC)])
    for s in range(2):
        for h in range(2):
            memset("pool", Upad[s][h][:], 0.0, ["Upad%d_%d" % (s, h)])

    xt = [sb("xt%d" % i, [128, D], F32) for i in range(2)]
    junk = sb("junk", [128, D], BF16)
    xnb = [sb("xnb%d" % i, [128, D], BF16) for i in range(2)]
    st1 = [sb("st1_%d" % i, [128, 4], F32) for i in range(4)]
    xnT = sb("xnT", [128, 8, TBS], BF16)
    xn2T = xnT
    AAR = sb("AAR", [128, 24, TBS], BF16)
    OAT = AAR[:, 0:8, :]
    OBT = AAR[:, 8:16, :]
    MT = AAR[:, 16:24, :]
    ACTT = AAR[:, 0:NJ, :]
    hblk = sb("hblk", [128, NCC, D], F32)
    LT = sb("LT", [128, TBS], BF16)
    LG = sb("LG", [128, TBS], BF16)
    Wl = sb("Wl", [128, 8, 256], BF16)
    WAR = sb("WAR", [128, 16 * 1024], BF16)

    def arena(slot0, nslots, pattern, **kw):
        ap = WAR[:, slot0 * 1024:(slot0 + nslots) * 1024].rearrange(pattern, **kw)
        return ap, ["ar%d" % i for i in range(slot0, slot0 + nslots)]
    Wg = [arena(7 * i, 7, "p (j c n) -> p j c n", c=8, j=7) for i in range(2)]
    Wm = [arena(4 * i, 4, "p (j c n) -> p j c n", c=8, j=4) for i in range(2)]
    WO, WOK = arena(8, 8, "p (c n) -> p c n", c=8)
    WU = [arena(2 * i, 2, "p (j c n) -> p j c n", c=8, j=2) for i in range(5)]
    WDR = [arena(10 + i, 1, "p (j n) -> p j n", j=1) for i in range(4)]
    NF = 16
    ftmp = [sb("ft%d" % i, [128, TBS], F32) for i in range(NF)]
    Vh2 = [sb("Vh%d" % i, [128, NCC, 128], BF16) for i in range(2)]
    QI2 = [sb("QI%d" % i, [128, TBS], BF16) for i in range(2)]
    KI2 = [sb("KI%d" % i, [128, TBS], BF16) for i in range(2)]
    KItm = [sb("KItm%d" % i, [128, 128], BF16) for i in range(2)]
    SCT = [sb("SCT%d" % i, [128, 128], BF16) for i in range(2)]
    Sg = [sb("Sg%d" % i, [128, 128], BF16) for i in range(2)]
    OT = sb("OT", [128, TBS], F32)
    SHG2 = [sb("SHG%d" % i, [128, TBS], BF16) for i in range(2)]
    sc4 = [sb("sc4_%d" % i, [128, 4], F32) for i in range(16)]
    ART = sb("ART", [128, NCC, 2, 128], BF16)
    BT = sb("BT", [128, TBS], BF16)
    KT = sb("KT", [128, TBS], BF16)
    RTlo = sb("RTlo", [128, TBS], BF16)
    KTlo = sb("KTlo", [128, TBS], BF16)
    VB = sb("VB", [128, TBS], BF16)
    GG = sb("GG", [128, TBS], F32)
    BON = sb("BON", [128, TBS], F32)
    YT = sb("YT", [128, TBS], F32)
    NTARB = [[sb("NTARB%d_%d" % (c, h), [128, 256], BF16) for h in range(2)] for c in range(NCC)]
    AKRK = [[sb("AKRK%d_%d" % (c, h), [128, 256], BF16) for h in range(2)] for c in range(NCC)]
    XT = [[sb("XT%d_%d" % (c, h), [128, 128], BF16) for h in range(2)] for c in range(NCC)]
    PX = [sb("PX%d" % i, [128, 256], BF16) for i in range(8)]
    Pb = [sb("Pb%d" % i, [128, 128], BF16) for i in range(8)]
    G0 = [sb("G0_%d" % i, [128, 128], BF16) for i in range(2)]
    RHSb = [sb("RHSb%d" % i, [128, 128], BF16) for i in range(2)]
    tmp128 = [sb("tmp128_%d" % i, [128, 128], F32) for i in range(2)]

    pj = [ps("pj%d" % i, [128, 512]) for i in range(2)]
    ptr = ps("ptr", [128, 8, 128], BF16)
    psc = [ps("psc%d" % i, [128, 512]) for i in range(2)]
    pst = ps("pst", [128, 4, 128])
    phg = ps("phg", [128, 4, 128])
    pmi = ps("pmi", [128, 512])

    cnt = {"pj": 0, "raw": 0, "ft": 0}

    def nxt(name, n):
        i = cnt.get(name, 0)
        cnt[name] = i + 1
        return i % n

    wcache = {}

    def wload(tag, dst, src, keys, ttb):
        if tag not in wcache:
            n = 1
            for d_ in dst.shape[1:]:
                n *= d_
            wcache[tag] = nc.dram_tensor("wc_" + tag, [128, n], BF16, kind="Internal").ap()
        cv = wcache[tag]
        if len(dst.shape) == 3:
            cv = cv.rearrange("p (a b) -> p a b", a=dst.shape[1])
        if ttb == 0:
            dma("pool", dst, src, [], keys)
            dma("sp", cv, dst, keys, ["wc_" + tag])
        else:
            dma("sp", dst, cv, ["wc_" + tag], keys)

    for tb in range(NB):
        t0 = tb * TBS

        def norm_T(src_tile_fn, dstT, gname, pre):
            for i in range(NCC):
                xb, xk = src_tile_fn(i)
                s = nxt("st1", 4)
                st = st1[s]
                stk = "st1_%d" % s
                act(junk[:], xb, AF.Square, [xk], ["junk", stk], accum=st[:, 0:1])
                ts("dve", st[:, 1:2], st[:, 0:1], 1.0 / D, ALU.mult, [stk], [stk], s2=EPS, op1=ALU.add)
                act(st[:, 1:2], st[:, 1:2], AF.Sqrt, [stk], [stk])
                b = nxt("xnb", 2)
                recip(st[:, 2:3], st[:, 1:2], [stk], [stk])
                ts("dve", xnb[b][:], xb, st[:, 2:3], ALU.mult, [xk, stk], ["xnb%d" % b])
                for c in range(8):
                    tr(ptr[:, c, :], xnb[b][:, c * 128:(c + 1) * 128], ident[:], ["xnb%d" % b, "ident"], ["ptr"])
                tt("dve", dstT[:, :, i * 128:(i + 1) * 128], ptr[:, :, :],
                   cols[:, CP[gname]:CP[gname] + 8].unsqueeze(2).to_broadcast([128, 8, 128]),
                   ALU.mult, ["ptr", "cols"], ["%s_%d" % (pre, i)])

        def xsrc(i):
            b = nxt("xt", 2)
            dma("sp", xt[b][:], x_d[t0 + i * 128:t0 + (i + 1) * 128, :], [], ["xt%d" % b])
            return xt[b][:], "xt%d" % b

        P.phase = "A"
        norm_T(xsrc, xnT, "g1", "xnT")
        XNT_KEYS = ["xnT_%d" % i for i in range(NCC)]

        def shift_lerp(psrc, pkey, muidx, dst, dkey, eng2="dve"):
            ck = "carry%d" % muidx
            muc = col("mu", muidx)
            act(dst, psrc, AF.Identity, [pkey, "omu"], [dkey], scale=omu[:, muidx:muidx + 1])
            stt(dst[:, 1:TBS], psrc[:, 0:TBS - 1], muc, dst[:, 1:TBS], ALU.mult, ALU.add, [pkey, "cols", dkey], [dkey])
            stt(dst[:, 0:1], carry[:, muidx:muidx + 1], muc, dst[:, 0:1], ALU.mult, ALU.add, [ck, "cols", dkey], [dkey])
            cp("act", carry[:, muidx:muidx + 1], psrc[:, TBS - 1:TBS], [pkey], [ck])

        B4 = [(pj[0][:], "pj0"), (pj[1][:], "pj1"), (psc[0][:], "psc0"), (psc[1][:], "psc1")]
        B7 = B4 + [(pst[:].rearrange("p a b -> p (a b)"), "pst"), (phg[:].rearrange("p a b -> p (a b)"), "phg"),
                   (pmi[:], "pmi")]

        def proj_fm(wtile_fn, wkeys, rhsT, rkeys, banks=B4):
            b = nxt("pjb", len(banks))
            bap, bkey = banks[b]
            mmg(bap, [(wtile_fn(c), rhsT[:, c, :]) for c in range(8)], wkeys + rkeys, [bkey])
            return bap, bkey

        P.phase = "L"
        if tb == 0:
            wload("Wl", Wl[:], win_d[:, 7168:7424].rearrange("(c p) n -> p c n", p=128), ["Wl"], 0)
        pa, pk = proj_fm(lambda c: Wl[:, c, 0:128], ["Wl"], xnT, XNT_KEYS)
        f = nxt("ft", NF)
        shift_lerp(pa, pk, 24, ftmp[f][:], "ft%d" % f)
        act(LT[0:64, :], ftmp[f][0:64, :], AF.Tanh, ["ft%d" % f], ["LT"])
        cp("act", LT[64:128, :], ftmp[f][64:128, :], ["ft%d" % f], ["LT"])
        pa, pk = proj_fm(lambda c: Wl[:, c, 128:256], ["Wl"], xnT, XNT_KEYS)
        f = nxt("ft", NF)
        shift_lerp(pa, pk, 25, ftmp[f][:], "ft%d" % f)
        act(LG[:], ftmp[f][:], AF.Sigmoid, ["ft%d" % f], ["LG"])

        segs = [0, 1024, 2048, 3072, 4096, 5120, 6144]

        def load_Wg(g):
            wb = g % 2
            W, WK = Wg[wb]
            for j, so in enumerate(segs):
                wload("Wg%d_%d" % (g, j), W[:, j, :, :],
                      win_d[:, so + g * 128: so + (g + 1) * 128].rearrange("(c p) n -> p c n", p=128),
                      [WK[j]], tb)

        def F():
            i = nxt("ft", NF)
            return ftmp[i][:], "ft%d" % i

        ctx = {}

        def stageA(g):
            c = ctx.setdefault(g, {})
            W, WK = Wg[g % 2]
            if g + 1 < 8:
                load_Wg(g + 1)
            cnt["ft"] = 0
            pp = g % 2
            QI, KI, Vh, SHG = QI2[pp], KI2[pp], Vh2[pp], SHG2[pp]
            QIk, KIk, SHGk = "QI%d" % pp, "KI%d" % pp, "SHG%d" % pp

            def wfn(j):
                return (lambda cc_: W[:, j, cc_, :]), [WK[j]]
            fn_, wkk = wfn(0)
            pa, pk = proj_fm(fn_, wkk, xnT, XNT_KEYS)
            qs, qsk = F()
            act(qs, pa, AF.Silu, [pk], [qsk])
            yield
            fn_, wkk = wfn(1)
            pa, pk = proj_fm(fn_, wkk, xnT, XNT_KEYS)
            fg, fgk = F()
            act(fg, pa, AF.Sigmoid, [pk], [fgk])
            yield
            ts("dve", fg, fg, omlc[:, g:g + 1], ALU.mult, [fgk, "omlc", "lbc"], [fgk], s2=lbc[:, g:g + 1], op1=ALU.add)
            lnf, lnfk = F()
            act(lnf, fg, AF.Ln, [fgk], [lnfk])
            kkh, kkhk = F()
            ts("pool", kkh, fg, -1.0, ALU.mult, [fgk], [kkhk], s2=1.0, op1=ALU.add)
            yield
            bb, bbk = F()
            scan(bb, rmask[:], lnf, ["rmask", lnfk], [bbk])
            b3 = bb.rearrange("p (c t) -> p c t", t=CH)
            dd, ddk = F()
            tt("dve", dd.rearrange("p (c t) -> p c t", t=CH), b3, b3[:, :, 63:64].to_broadcast([128, NCC, CH]),
               ALU.subtract, [bbk], [ddk])
            yield
            e1, e1k = F()
            act(e1, dd, AF.Exp, [ddk, "lnsc"], [e1k], bias=lnsc[:, 0:1])
            tt("dve", QI[:], qs, e1, ALU.mult, [qsk, e1k], [QIk])
            yield
            e2, e2k = F()
            act(e2, dd, AF.Exp, [ddk], [e2k], scale=-1.0)
            tt("pool", KI[:], kkh, e2, ALU.mult, [kkhk, e2k], [KIk])
            yield
            si = nxt("sc4", 16)
            eref, erefk = sc4[si], "sc4_%d" % si
            act(eref[:].unsqueeze(2), b3[:, :, 63:64], AF.Exp, [bbk], [erefk])
            si = nxt("sc4", 16)
            elast, elastk = sc4[si], "sc4_%d" % si
            act(elast[:].unsqueeze(2), b3[:, :, 127:128], AF.Exp, [bbk], [elastk])
            si = nxt("sc4", 16)
            elr, elrk = sc4[si], "sc4_%d" % si
            tt("dve", elr[:].unsqueeze(2), b3[:, :, 127:128], b3[:, :, 63:64], ALU.subtract, [bbk], [elrk])
            act(elr[:], elr[:], AF.Exp, [elrk], [elrk])
            c["h"] = (eref, erefk, elast, elastk, elr, elrk)
            yield
            fn_, wkk = wfn(3)
            pa, pk = proj_fm(fn_, wkk, xnT, XNT_KEYS)
            act(SHG[:], pa, AF.Silu, [pk], [SHGk])
            yield
            for i in range(NCC):
                b = nxt("pj", 2)
                mmg(pj[b][:, 0:128], [(xnT[:, c_, i * 128:(i + 1) * 128], W[:, 2, c_, :]) for c_ in range(8)],
                    [WK[2], "xnT_%d" % i], ["pj%d" % b])
                cp("act", Vh[:, i, :], pj[b][:, 0:128], ["pj%d" % b], ["Vh%d_%d" % (pp, i)])
                yield

        def chainH(g):
            eref, erefk, elast, elastk, elr, elrk = ctx[g]["h"]
            SK = "Sh%d" % g
            pp = g % 2
            QI, KI, Vh, SHG = QI2[pp], KI2[pp], Vh2[pp], SHG2[pp]
            QIk, KIk, SHGk = "QI%d" % pp, "KI%d" % pp, "SHG%d" % pp
            for cc in range(NCC):
                csl = slice(cc * CH, (cc + 1) * CH)
                kb = nxt("KItm", 2)
                tr(ptr[:, 0, :], KI[:, csl], ident[:], [KIk, "ident"], ["ptr"])
                cp("act", KItm[kb][:], ptr[:, 0, :], ["ptr"], ["KItm%d" % kb])
                yield
                mm(phg[:, 0, :], KI[:, csl], QI[:, csl], [KIk, QIk], ["phg"])
                mm(phg[:, 2, :], KItm[kb][:], Vh[:, cc, :], ["KItm%d" % kb, "Vh%d_%d" % (pp, cc)], ["phg"])
                sb_ = nxt("SCT", 2)
                tt("dve", SCT[sb_][:], phg[:, 0, :], mask2[:, 128:256], ALU.mult, ["phg", "mask2"], ["SCT%d" % sb_])
                tb_ = nxt("tmp128", 2)
                ts("dve", tmp128[tb_][:], phg[:, 2, :], elr[:, cc:cc + 1], ALU.mult, ["phg", elrk], ["tmp128_%d" % tb_])
                gb = nxt("Sg", 2)
                ts("dve", Sg[gb][:], Sh[:, g, :], eref[:, cc:cc + 1], ALU.mult, [SK, erefk], ["Sg%d" % gb])
                yield
                mmg(phg[:, 1, :], [(Vh[:, cc, :], SCT[sb_][:]), (Sg[gb][:], QI[:, csl])],
                    ["Vh%d_%d" % (pp, cc), "SCT%d" % sb_, "Sg%d" % gb, QIk], ["phg"])
                stt(Sh[:, g, :], Sh[:, g, :], elast[:, cc:cc + 1], tmp128[tb_][:], ALU.mult, ALU.add,
                    [SK, elastk, "tmp128_%d" % tb_], [SK])
                cp("act", OT[:, csl], phg[:, 1, :], ["phg"], ["OT%d" % cc])
                yield
            OTK = ["OT%d" % c_ for c_ in range(NCC)]
            osq, osqk = ftmp[13][:], "ft13"
            act(osq, OT[:], AF.Square, OTK, [osqk])
            mm(pmi[:], onesf[:], osq, ["onesf", osqk], ["pmi"])
            yield
            sd, sdk = osq, osqk
            ts("dve", sd, pmi[:], 1.0 / 128, ALU.mult, ["pmi"], [sdk], s2=EPS, op1=ALU.add)
            act(sd, sd, AF.Sqrt, [sdk], [sdk])
            recip(sd, sd, [sdk], [sdk])
            yield
            tt("dve", sd, OT[:], sd, ALU.mult, OTK + [sdk], [sdk])
            stt(OAT[:, g, :], sd, col("gn", g), SHG[:], ALU.mult, ALU.mult, [sdk, "cols", SHGk], ["A%d" % g])
            yield

        def stageB(g):
            c = ctx.setdefault(g, {})
            W, WK = Wg[g % 2]
            cnt["ft"] = 0

            def wfn(j):
                return (lambda cc_: W[:, j, cc_, :]), [WK[j]]
            fn_, wkk = wfn(4)
            pa, pk = proj_fm(fn_, wkk, xnT, XNT_KEYS)
            RP, RPk = F()
            shift_lerp(pa, pk, g, RP, RPk)
            yield
            fn_, wkk = wfn(5)
            pa, pk = proj_fm(fn_, wkk, xnT, XNT_KEYS)
            KP, KPk = F()
            shift_lerp(pa, pk, 8 + g, KP, KPk)
            yield
            fn_, wkk = wfn(6)
            pa, pk = proj_fm(fn_, wkk, xnT, XNT_KEYS)
            VP, VPk = F()
            shift_lerp(pa, pk, 16 + g, VP, VPk)
            cp("pool", VB[:], VP, [VPk], ["VB"])
            yield
            gs = slice(g * 128, (g + 1) * 128)
            b = nxt("pj", 2)
            mm(pj[b][:], w2a2[0:64, gs], LT[0:64, :], ["w2a2", "LT"], ["pj%d" % b])
            lwp, lwpk = F()
            act(lwp, pj[b][:], AF.Sigmoid, ["pj%d" % b, "cols"], [lwpk], bias=col("w0", g))
            b = nxt("pj", 2)
            mm(pj[b][:], w2a2[64:128, gs], LT[64:128, :], ["w2a2", "LT"], ["pj%d" % b])
            asg, asgk = F()
            act(asg, pj[b][:], AF.Sigmoid, ["pj%d" % b, "cols"], [asgk], bias=col("a0", g))
            yield
            cwp, cwpk = F()
            scan(cwp, rmask[:], lwp, ["rmask", lwpk], [cwpk])
            c3 = cwp.rearrange("p (c t) -> p c t", t=CH)
            dd, ddk = F()
            tt("dve", dd.rearrange("p (c t) -> p c t", t=CH), c3, c3[:, :, 63:64].to_broadcast([128, NCC, CH]),
               ALU.subtract, [cwpk], [ddk])
            da, dak = F()
            tt("pool", da, dd, lwp, ALU.subtract, [ddk, lwpk], [dak])
            yield
            epos, eposk = F()
            act(epos, dd, AF.Exp, [ddk], [eposk], scale=-C0)
            eneg, enegk = F()
            act(eneg, dd, AF.Exp, [ddk], [enegk], scale=C0)
            act(da, da, AF.Exp, [dak], [dak], scale=-C0)
            yield
            si = nxt("sc4", 16)
            reref, rerefk = sc4[si], "sc4_%d" % si
            act(reref[:].unsqueeze(2), c3[:, :, 63:64], AF.Exp, [cwpk], [rerefk], scale=-C0)
            si = nxt("sc4", 16)
            relast, relastk = sc4[si], "sc4_%d" % si
            act(relast[:].unsqueeze(2), c3[:, :, 127:128], AF.Exp, [cwpk], [relastk], scale=-C0)
            si = nxt("sc4", 16)
            relr, relrk = sc4[si], "sc4_%d" % si
            tt("dve", relr[:].unsqueeze(2), c3[:, :, 127:128], c3[:, :, 63:64], ALU.subtract, [cwpk], [relrk])
            act(relr[:], relr[:], AF.Exp, [relrk], [relrk], scale=-C0)
            c["r"] = (reref, rerefk, relast, relastk, relr, relrk)
            yield
            kkr, kkrk = F()
            ts("dve", kkr, KP, col("kk", g), ALU.mult, [KPk, "cols"], [kkrk])
            sq, sqk = F()
            act(sq, kkr, AF.Square, [kkrk], [sqk])
            mm(pmi[:], bones[:], sq, ["bones", sqk], ["pmi"])
            act(sq, pmi[:], AF.Sqrt, ["pmi"], [sqk])
            yield
            ts("dve", sq, sq, 1e-12, ALU.max, [sqk], [sqk])
            recip(sq, sq, [sqk], [sqk])
            tt("dve", kkr, kkr, sq, ALU.mult, [kkrk, sqk], [kkrk])
            k2, k2k = F()
            ts("pool", k2, asg, -1.0, ALU.add, [asgk, "cols"], [k2k], s2=col("ka", g), op1=ALU.mult)
            stt(k2, k2, 1.0, KP, ALU.add, ALU.mult, [k2k, KPk], [k2k])
            yield
            c["b1"] = dict(RP=RP, RPk=RPk, VP=VP, VPk=VPk, asg=asg, asgk=asgk, da=da, dak=dak, epos=epos,
                           eposk=eposk, eneg=eneg, enegk=enegk, kkr=kkr, kkrk=kkrk, sq=sq, sqk=sqk, k2=k2, k2k=k2k)

        def stageB2(g):
            c = ctx[g]
            d_ = c["b1"]
            RP, RPk, VP, VPk, asg, asgk = d_["RP"], d_["RPk"], d_["VP"], d_["VPk"], d_["asg"], d_["asgk"]
            da, dak, epos, eposk, eneg, enegk = d_["da"], d_["dak"], d_["epos"], d_["eposk"], d_["eneg"], d_["enegk"]
            kkr, kkrk, sq, sqk, k2, k2k = d_["kkr"], d_["kkrk"], d_["sq"], d_["sqk"], d_["k2"], d_["k2k"]
            gs = slice(g * 128, (g + 1) * 128)
            b = nxt("pj", 2)
            mm(pj[b][:], g2[:, gs], LG[:], ["g2", "LG"], ["pj%d" % b])
            cp("act", GG[:], pj[b][:], ["pj%d" % b], ["GG"])
            yield
            stt(sq, RP, col("rk", g), k2, ALU.mult, ALU.mult, [RPk, "cols", k2k], [sqk])
            mm(pmi[:], bones[:], sq, ["bones", sqk], ["pmi"])
            tt("dve", BON[:], pmi[:], VP, ALU.mult, ["pmi", VPk], ["BON"])
            yield
            tt("dve", epos, RP, epos, ALU.mult, [RPk, eposk], [eposk])
            cp("pool", ART[:, :, 1, :], epos.rearrange("p (c t) -> p c t", t=CH), [eposk], ["ART_R"])
            tt("pool", RTlo[:].rearrange("p (c t) -> p c t", t=CH), epos.rearrange("p (c t) -> p c t", t=CH),
               ART[:, :, 1, :], ALU.subtract, [eposk, "ART_R"], ["RTlo"])
            stt(ART[:, :, 0, :], kkr.rearrange("p (c t) -> p c t", t=CH), -1.0,
                da.rearrange("p (c t) -> p c t", t=CH), ALU.mult, ALU.mult, [kkrk, dak], ["ART_A"])
            tt("pool", sq, kkr, asg, ALU.mult, [kkrk, asgk], [sqk])
            tt("dve", BT[:], sq, eneg, ALU.mult, [sqk, enegk], ["BT"])
            tt("dve", k2, k2, eneg, ALU.mult, [k2k, enegk], [k2k])
            cp("pool", KT[:], k2, [k2k], ["KT"])
            tt("pool", KTlo[:], k2, KT[:], ALU.subtract, [k2k, "KT"], ["KTlo"])
            yield
            pset = 0
            trio = ((VB, "VB", Vpad, "Vpad"), (BT, "BT", Bpad, "Bpad"), (KT, "KT", Kpad, "Kpad"))
            for cc in range(NCC):
                csl = slice(cc * CH, (cc + 1) * CH)
                for q, (src, skey, bufs, nm) in enumerate(trio):
                    tr(ptr[:, 1 + q, :], src[:, csl], ident[:], [skey, "ident"], ["ptr"])
                for q, (src, skey, bufs, nm) in enumerate(trio):
                    for h in range(2):
                        hs = slice(64 * h, 64 * h + 64)
                        cp("act" if q != 1 else "dve", bufs[pset][h][:, cc, hs], ptr[:, 1 + q, hs], ["ptr"],
                           ["%s%d_%d_%d" % (nm, pset, h, cc)])
                yield

        def preR(g):
            chains = [(cc, h) for cc in range(NCC) for h in range(2)]
            IB = [(psc[0], "psc0"), (psc[1], "psc1"), (pst[:].rearrange("p a b -> p (a b)"), "pst")]
            for ci, (cc, h) in enumerate(chains):
                csl = slice(cc * CH, (cc + 1) * CH)
                ph = slice(64 * h, 64 * h + 64)
                pbank, pk2 = IB[nxt("ib", 3)]
                art2 = ART[ph, cc, :, :].rearrange("p a t -> p (a t)")
                mm(pbank[:, 0:256], BT[ph, csl], art2, ["BT", "ART_A", "ART_R"], [pk2])
                P.add("pe", lambda e, o=pbank[:, 256:512], l=KT[ph, csl], rr=art2:
                      e.matmul(o, l, rr, start=True, stop=False, skip_group_check=True),
                      ["KT", "ART_A", "ART_R"], [pk2])
                P.add("pe", lambda e, o=pbank[:, 384:512], l=KT[ph, csl], rr=RTlo[ph, csl]:
                      e.matmul(o, l, rr, start=False, stop=False, skip_group_check=True), ["KT", "RTlo"], [pk2])
                P.add("pe", lambda e, o=pbank[:, 384:512], l=KTlo[ph, csl], rr=ART[ph, cc, 1, :]:
                      e.matmul(o, l, rr, start=False, stop=True, skip_group_check=True), ["KTlo", "ART_R"], [pk2])
                tt("dve", NTARB[cc][h][:], pbank[:, 0:256], mask2[:], ALU.mult, [pk2, "mask2"],
                   ["NTARB%d_%d" % (cc, h)])
                tt("dve", AKRK[cc][h][:], pbank[:, 256:512], mask2[:], ALU.mult, [pk2, "mask2"],
                   ["AKRK%d_%d" % (cc, h)])
                yield
                pbank, pk2 = IB[nxt("ib", 3)]
                mm(pbank[:, 0:128], ART[ph, cc, 0, :], BT[ph, csl], ["BT", "ART_A"], [pk2])
                tt("dve", Pb[ci][:], pbank[:, 0:128], strictT[:], ALU.mult, [pk2, "strictT"], ["Pb%d" % ci])
                cp("pool", PX[ci][:, 0:128], NTARB[cc][h][:, 0:128], ["NTARB%d_%d" % (cc, h)], ["PX%d" % ci])
                tt("pool", PX[ci][:, 128:256], NTARB[cc][h][:, 0:128], ident[:], ALU.add,
                   ["NTARB%d_%d" % (cc, h), "ident"], ["PX%d" % ci])
                yield
            for j in range(0, 7):
                for ci, (cc, h) in enumerate(chains):
                    pxk, pbk = "PX%d" % ci, "Pb%d" % ci
                    ev = "act" if ci % 2 == 0 else "dve"
                    pbank, pk2 = IB[nxt("ib", 3)]
                    if j == 0:
                        mm(pbank[:, 0:128], Pb[ci][:], PX[ci][:, 0:128], [pbk, pxk], [pk2])
                        mm(pbank[:, 256:384], PX[ci][:, 0:128], Pb[ci][:], [pbk, pxk], [pk2])
                        cp(ev, PX[ci][:, 0:128], pbank[:, 0:128], [pk2], [pxk])
                        cp(ev, Pb[ci][:], pbank[:, 256:384], [pk2], [pbk])
                    elif j < 6:
                        P.add("pe", lambda e, o=pbank[:, 0:256], l=Pb[ci][:], rr=PX[ci][:]:
                              e.matmul(o, l, rr, start=True, stop=False, skip_group_check=True), [pbk, pxk], [pk2])
                        P.add("pe", lambda e, o=pbank[:, 128:256], l=ident[:], rr=PX[ci][:, 128:256]:
                              e.matmul(o, l, rr, start=False, stop=True, skip_group_check=True), [pxk, "ident"], [pk2])
                        P.add("pe", lambda e, o=pbank[:, 256:384], l=PX[ci][:, 0:128], rr=Pb[ci][:]:
                              e.matmul(o, l, rr, start=True, stop=True, skip_group_check=True), [pbk, pxk], [pk2])
                        cp(ev, PX[ci][:], pbank[:, 0:256], [pk2], [pxk])
                        cp(ev, Pb[ci][:], pbank[:, 256:384], [pk2], [pbk])
                    else:
                        mmg(pbank[:, 0:128], [(Pb[ci][:], PX[ci][:, 128:256]),
                                                 (ident[:], PX[ci][:, 128:256])], [pbk, pxk, "ident"], [pk2])
                        cp(ev, XT[cc][h][:], pbank[:, 0:128], [pk2], ["XT%d_%d" % (cc, h)])
                    yield

        def chainR(g):
            reref, rerefk, relast, relastk, relr, relrk = ctx[g]["r"]
            pset = 0
            HK = "Hr%d" % g
            for cc in range(NCC):
                csl = slice(cc * CH, (cc + 1) * CH)
                g0 = nxt("G0", 2)
                ts("dve", G0[g0][:], Hr[:, g, :], reref[:, cc:cc + 1], ALU.mult, [HK, rerefk], ["G0_%d" % g0])
                vk = ["Vpad%d_%d_%d" % (pset, h, cc) for h in range(2)]
                bk = ["Bpad%d_%d_%d" % (pset, h, cc) for h in range(2)]
                kk_ = ["Kpad%d_%d_%d" % (pset, h, cc) for h in range(2)]
                ak = ["AKRK%d_%d" % (cc, h) for h in range(2)]
                nk = ["NTARB%d_%d" % (cc, h) for h in range(2)]
                mmg(pst[:, 0, :], [(AKRK[cc][0][:, 0:128], Vpad[pset][0][:, cc, :]),
                                   (AKRK[cc][1][:, 0:128], Vpad[pset][1][:, cc, :]),
                                   (ART[:, cc, 0, :], G0[g0][:])],
                    ak + vk + ["ART_A", "G0_%d" % g0], ["pst"])
                rb = nxt("RHSb", 2)
                cp("act", RHSb[rb][:], pst[:, 0, :], ["pst"], ["RHSb%d" % rb])
                yield
                us = nxt("Upad", 2)
                for h in range(2):
                    hs = slice(64 * h, 64 * h + 64)
                    mm(pst[:, 1, hs], XT[cc][h][:], RHSb[rb][:, hs], ["XT%d_%d" % (cc, h), "RHSb%d" % rb], ["pst"])
                for h in range(2):
                    hs = slice(64 * h, 64 * h + 64)
                    cp("act", Upad[us][h][:, hs], pst[:, 1, hs], ["pst"], ["Upad%d_%d" % (us, h)])
                yield
                uk = ["Upad%d_%d" % (us, h) for h in range(2)]
                mmg(pst[:, 2, :], [(G0[g0][:], ART[:, cc, 1, :]),
                                   (Upad[us][0][:], NTARB[cc][0][:, 128:256]),
                                   (Upad[us][1][:], NTARB[cc][1][:, 128:256]),
                                   (Vpad[pset][0][:, cc, :], AKRK[cc][0][:, 128:256]),
                                   (Vpad[pset][1][:, cc, :], AKRK[cc][1][:, 128:256])],
                    ["G0_%d" % g0, "ART_R"] + uk + nk + vk + ak, ["pst"])
                mmg(pst[:, 3, :], [(Bpad[pset][0][:, cc, :], Upad[us][0][:]),
                                   (Bpad[pset][1][:, cc, :], Upad[us][1][:]),
                                   (Kpad[pset][0][:, cc, :], Vpad[pset][0][:, cc, :]),
                                   (Kpad[pset][1][:, cc, :], Vpad[pset][1][:, cc, :])],
                    bk + uk + kk_ + vk, ["pst"])
                tb_ = nxt("tmp128", 2)
                ts("dve", tmp128[tb_][:], pst[:, 3, :], relr[:, cc:cc + 1], ALU.mult, ["pst", relrk],
                   ["tmp128_%d" % tb_])
                stt(Hr[:, g, :], Hr[:, g, :], relast[:, cc:cc + 1], tmp128[tb_][:], ALU.mult, ALU.add,
                    [HK, relastk, "tmp128_%d" % tb_], [HK])
                cp("act", YT[:, csl], pst[:, 2, :], ["pst"], ["YT%d" % cc])
                yield

        def normR(g):
            YTK = ["YT%d" % c_ for c_ in range(NCC)]
            mm(pmi[:], bones[:], YT[:], ["bones"] + YTK, ["pmi"])
            yc, yck = ftmp[14][:], "ft14"
            stt(yc, pmi[:], -1.0 / 64, YT[:], ALU.mult, ALU.add, ["pmi"] + YTK, [yck])
            ysq, ysqk = ftmp[15][:], "ft15"
            act(ysq, yc, AF.Square, [yck], [ysqk])
            mm(pmi[:], bones[:], ysq, ["bones", ysqk], ["pmi"])
            ts("dve", ysq, pmi[:], 1.0 / 64, ALU.mult, ["pmi"], [ysqk], s2=GN_EPS, op1=ALU.add)
            act(ysq, ysq, AF.Sqrt, [ysqk], [ysqk])
            recip(ysq, ysq, [ysqk], [ysqk])
            tt("dve", yc, yc, ysq, ALU.mult, [yck, ysqk], [yck])
            ts("dve", yc, yc, col("lnw", g), ALU.mult, [yck, "cols"], [yck], s2=col("lnb", g), op1=ALU.add)
            tt("pool", yc, yc, BON[:], ALU.add, [yck, "BON"], [yck])
            tt("dve", OBT[:, g, :], yc, GG[:], ALU.mult, [yck, "GG"], ["A%d" % (8 + g)])
            yield

        def run_all(*gens, strides=None):
            gens = list(gens)
            strides = list(strides) if strides else [1] * len(gens)
            rnd = 0
            while gens:
                for gg, st_ in list(zip(gens, strides)):
                    if rnd % st_ != 0 and len(gens) > 1:
                        continue
                    try:
                        next(gg)
                    except StopIteration:
                        i_ = gens.index(gg)
                        gens.pop(i_)
                        strides.pop(i_)
                rnd += 1

        def seq(*gens):
            for gg in gens:
                yield from gg

        if tb == 0:
            load_Wg(0)
        def par(*gens):
            gens = list(gens)
            while gens:
                for gg in list(gens):
                    try:
                        next(gg)
                        yield
                    except StopIteration:
                        gens.remove(gg)

        P.phase = "G.AB0"
        run_all(stageA(0))
        run_all(stageB(0))
        for g in range(8):
            P.phase = "G.C"
            if g + 1 < 8:
                run_all(chainH(g), seq(stageB2(g), par(preR(g), stageA(g + 1))), strides=[CSTRIDE, 1])
            else:
                run_all(chainH(g), seq(stageB2(g), preR(g)), strides=[CSTRIDE, 1])
            P.phase = "G.D"
            if g + 1 < 8:
                run_all(chainR(g), stageB(g + 1))
            else:
                run_all(chainR(g))
            P.phase = "G.N"
            run_all(normR(g))

        P.phase = "M"
        OAK = ["A%d" % g for g in range(8)]
        OBK = ["A%d" % (8 + g) for g in range(8)]
        def load_Wm(m):
            W, WK = Wm[m % 2]
            ms = slice(m * 128, (m + 1) * 128)
            wload("Wm%d_0" % m, W[:, 0, :, :], wba_d[:, ms].rearrange("(c p) n -> p c n", p=128), [WK[0]], tb)
            wload("Wm%d_1" % m, W[:, 1, :, :], wbb_d[:, ms].rearrange("(c p) n -> p c n", p=128), [WK[1]], tb)
            wload("Wm%d_2" % m, W[:, 2, :, :],
                  win_d[:, 7424 + m * 128:7424 + (m + 1) * 128].rearrange("(c p) n -> p c n", p=128), [WK[2]], tb)
            wload("Wm%d_3" % m, W[:, 3, :, :],
                  win_d[:, 8448 + m * 128:8448 + (m + 1) * 128].rearrange("(c p) n -> p c n", p=128), [WK[3]], tb)

        load_Wm(0)
        dma("sp", gp[:], gpa_d.partition_broadcast(128), [], ["gp"])
        wload("WO", WO[:], wout_d[:, :].rearrange("(c p) n -> p c n", p=128), WOK, tb)
        if tb + 1 < NB:
            wload("Wl", Wl[:], win_d[:, 7168:7424].rearrange("(c p) n -> p c n", p=128), ["Wl"], tb + 1)
        for m in range(8):
            W, WK = Wm[m % 2]
            if m + 1 < 8:
                load_Wm(m + 1)

            def F():
                i = nxt("ft", NF)
                return ftmp[i][:], "ft%d" % i
            pa, pk = proj_fm(lambda c: W[:, 2, c, :], [WK[2]], xnT, XNT_KEYS, banks=B7)
            sga, sgak = F()
            act(sga, pa, AF.Sigmoid, [pk], [sgak])
            pa, pk = proj_fm(lambda c: W[:, 3, c, :], [WK[3]], xnT, XNT_KEYS, banks=B7)
            sgb, sgbk = F()
            act(sgb, pa, AF.Sigmoid, [pk], [sgbk])
            pa, pk = proj_fm(lambda c: W[:, 0, c, :], [WK[0]], OAT, OAK, banks=B7)
            tt("dve", sga, sga, pa, ALU.mult, [sgak, pk], [sgak])
            pa, pk = proj_fm(lambda c: W[:, 1, c, :], [WK[1]], OBT, OBK, banks=B7)
            tt("dve", sgb, sgb, pa, ALU.mult, [sgbk, pk], [sgbk])
            tt("dve", MT[:, m, :], sga, sgb, ALU.add, [sgak, sgbk], ["A%d" % (16 + m)])
        MTK = ["A%d" % (16 + m) for m in range(8)]
        P.phase = "W"

        def load_WU(j):
            W, WK = WU[j % 5]
            wload("WU%d_0" % j, W[:, 0, :, :], wup_d[:, j * 128:(j + 1) * 128].rearrange("(c p) n -> p c n", p=128),
                  [WK[0]], tb)
            wload("WU%d_1" % j, W[:, 1, :, :],
                  wup_d[:, DFF + j * 128:DFF + (j + 1) * 128].rearrange("(c p) n -> p c n", p=128), [WK[1]], tb)

        def load_WD(j):
            Wd, WdK = WDR[j % 4]
            wload("WD%d" % j, Wd[:, 0, :], wdn_d[j * 128:(j + 1) * 128, :], WdK, tb)

        load_WU(0)
        load_WU(1)
        load_WU(2)
        load_WU(3)

        def post_norm_residual(i, res_ap, res_key, gp, gpk, dst, dkey, bA=None, bB=None):
            if bA is None:
                bA, bB = (pj[0][:], ["pj0"]), (pj[1][:], ["pj1"])
            s = nxt("st1", 4)
            st = st1[s]
            stk = "st1_%d" % s
            act(junk[:, 0:512], bA[0], AF.Square, bA[1], ["junk", stk], accum=st[:, 0:1])
            act(junk[:, 512:1024], bB[0], AF.Square, bB[1], ["junk", stk], accum=st[:, 1:2])
            tt("dve", st[:, 2:3], st[:, 0:1], st[:, 1:2], ALU.add, [stk], [stk])
            ts("dve", st[:, 2:3], st[:, 2:3], 1.0 / D, ALU.mult, [stk], [stk], s2=EPS, op1=ALU.add)
            act(st[:, 2:3], st[:, 2:3], AF.Sqrt, [stk], [stk])
            recip(st[:, 3:4], st[:, 2:3], [stk], [stk])
            for hh, bk in enumerate((bA, bB)):
                sl = slice(hh * 512, (hh + 1) * 512)
                stt(dst[:, sl], bk[0], st[:, 3:4], gp[:, sl], ALU.mult, ALU.mult,
                    bk[1] + [stk, gpk], [dkey])
            tt("pool", dst, dst, res_ap, ALU.add, [dkey, res_key], [dkey])

        for i in range(NCC):
            isl = slice(i * 128, (i + 1) * 128)
            for hh in range(2):
                mmg(pj[hh][:], [(MT[:, c, isl], WO[:, c, hh * 512:(hh + 1) * 512]) for c in range(8)],
                    MTK + WOK, ["pj%d" % hh])
            b = nxt("xt", 2)
            dma("sp", xt[b][:], x_d[t0 + i * 128:t0 + (i + 1) * 128, :], [], ["xt%d" % b])
            post_norm_residual(i, xt[b][:], "xt%d" % b, gp, "gp", hblk[:, i, :], "hblk%d" % i)

        P.phase = "FN"
        norm_T(lambda i: (hblk[:, i, :], "hblk%d" % i), xn2T, "g3", "xnT")
        X2K = ["xnT_%d" % i for i in range(NCC)]
        P.phase = "FU"
        dma("sp", gp[:], gpf_d.partition_broadcast(128), [], ["gp"])
        for j in range(NJ):
            W, WK = WU[j % 5]
            if j + 4 < NJ:
                load_WU(j + 4)
            if j == NJ - 2:
                load_WD(0)
                load_WD(1)
            res = []
            for q in range(2):
                ci = q * NJ + j
                pa, pk = proj_fm(lambda c, q=q: W[:, q, c, :], [WK[q]], xn2T, X2K, banks=B7)
                hk = "halo%d" % ci
                f = nxt("ft", NF)
                A = ftmp[f][:]
                fk = "ft%d" % f
                cwo = CP["cw"]
                w0c = cols[:, cwo + ci:cwo + ci + 1]
                w1c = cols[:, cwo + 44 + ci:cwo + 44 + ci + 1]
                w2c = cols[:, cwo + 2 * 44 + ci:cwo + 2 * 44 + ci + 1]
                bc = cols[:, CP["cb"] + ci:CP["cb"] + ci + 1]
                act(A, pa, AF.Identity, [pk, "cols"], [fk], bias=bc, scale=w2c)
                stt(A[:, 1:TBS], pa[:, 0:TBS - 1], w1c, A[:, 1:TBS], ALU.mult, ALU.add, [pk, "cols", fk], [fk])
                stt(A[:, 2:TBS], pa[:, 0:TBS - 2], w0c, A[:, 2:TBS], ALU.mult, ALU.add, [pk, "cols", fk], [fk])
                stt(A[:, 0:1], halo[:, ci, 1:2], w1c, A[:, 0:1], ALU.mult, ALU.add, [hk, "cols", fk], [fk])
                stt(A[:, 0:1], halo[:, ci, 0:1], w0c, A[:, 0:1], ALU.mult, ALU.add, [hk, "cols", fk], [fk])
                stt(A[:, 1:2], halo[:, ci, 1:2], w0c, A[:, 1:2], ALU.mult, ALU.add, [hk, "cols", fk], [fk])
                cp("act", halo[:, ci, :], pa[:, TBS - 2:TBS], [pk], [hk])
                res.append((A, fk))
            (ga_, gak), (va_, vak) = res
            act(ga_, ga_, AF.Silu, [gak], [gak])
            tt("dve", ACTT[:, j, :], ga_, va_, ALU.mult, [gak, vak], ["A%d" % j])
        AK = ["A%d" % j for j in range(NJ)]
        P.phase = "FD"
        banks = [(pj[0][:], ["pj0"]), (pj[1][:], ["pj1"]), (psc[0][:], ["psc0"]),
                 (psc[1][:], ["psc1"]), (pmi[:], ["pmi"]),
                 (pst[:].rearrange("p a b -> p (a b)"), ["pst"]),
                 (phg[:].rearrange("p a b -> p (a b)"), ["phg"]),
                 (ptr[:].rearrange("p a b -> p (a b)").bitcast(F32), ["ptr"])]
        load_WD(2)
        for j in range(NJ):
            Wd, WdK = WDR[j % 4]
            if j + 3 < NJ:
                load_WD(j + 3)
            for i in range(NCC):
                isl = slice(i * 128, (i + 1) * 128)
                for hh in range(2):
                    bk = banks[2 * i + hh]
                    mm(bk[0], ACTT[:, j, isl], Wd[:, 0, hh * 512:(hh + 1) * 512], ["A%d" % j] + WdK, bk[1],
                       start=(j == 0), stop=(j == NJ - 1))
        if tb + 1 < NB:
            tb_next_g0 = True
            W_, WK_ = Wg[0]
            for j_, so_ in enumerate([0, 1024, 2048, 3072, 4096, 5120, 6144]):
                wload("Wg0_%d" % j_, W_[:, j_, :, :], win_d[:, so_: so_ + 128].rearrange("(c p) n -> p c n", p=128),
                      [WK_[j_]], tb + 1)
        for i in range(NCC):
            b = nxt("xt", 2)
            post_norm_residual(i, hblk[:, i, :], "hblk%d" % i, gp, "gp", xt[b][:], "xt%d" % b,
                               bA=banks[2 * i], bB=banks[2 * i + 1])
            dma("sp", out_d[t0 + i * 128:t0 + (i + 1) * 128, :], xt[b][:], ["xt%d" % b], ["out%d_%d" % (tb, i)])

    sems = {}
    for e in ("pe", "act", "dve", "pool", "sp"):
        sems[e] = es.enter_context(nc.semaphore("s_" + e))
    dsems = {}
    for e in ("sp", "pool"):
        for i in range(Prog.NDS):
            dsems[(e, i)] = es.enter_context(nc.semaphore("d_%s%d" % (e, i)))
    allsems = [h.num for h in list(sems.values()) + list(dsems.values())]
    srange = range(min(allsems), max(allsems) + 1)
    nc.gpsimd.sem_clear(srange)
    nc.all_engine_barrier()
    with nc.Block() as block:
        P.emit(nc, block, sems, dsems)
    nc.all_engine_barrier()
    nc.gpsimd.sem_clear(srange)
    nc.all_engine_barrier()
    global _SBUF_USED
    _SBUF_USED = (nc.sbuf_base, nc.sbuf_top)
    es.close()
    return nc


def _colpack(v):
    v = np.asarray(v, dtype=np.float32).reshape(-1, 128)
    return np.ascontiguousarray(v.T)


_NC = None


def kernel(x, attn_pre_norm, w_in, hgrn_lb, hgrn_gnorm, w_branch_a, rwkv_mu, rwkv_w0, rwkv_w2, rwkv_a0,
           rwkv_a2, rwkv_g2, rwkv_k_k, rwkv_k_a, rwkv_r_k, rwkv_ln_w, rwkv_ln_b, w_branch_b, w_out,
           attn_post_norm, ffn_pre_norm, w_up, conv_w, conv_b, w_down, ffn_post_norm):
    global _NC
    f = lambda a: np.ascontiguousarray(np.asarray(a, dtype=np.float32))
    cw = np.asarray(conv_w, dtype=np.float32)[0]
    cols = np.concatenate([
        _colpack(attn_pre_norm[0]), _colpack(hgrn_lb[0]), _colpack(hgrn_lb[1]), _colpack(hgrn_gnorm[0]),
        _colpack(rwkv_mu[0]), _colpack(rwkv_w0[0]), _colpack(rwkv_a0[0]), _colpack(rwkv_k_k[0]),
        _colpack(rwkv_k_a[0]), _colpack(np.asarray(rwkv_r_k[0]).reshape(-1)), _colpack(rwkv_ln_w[0]),
        _colpack(rwkv_ln_b[0]), _colpack(ffn_pre_norm[0]),
        _colpack(cw[0]), _colpack(cw[1]), _colpack(cw[2]), _colpack(conv_b[0]),
    ], axis=1)
    assert cols.shape == (128, NCOL), cols.shape
    shared = {
        "cols": f(cols),
        "w_in": f(w_in[0]), "w_branch_a": f(w_branch_a[0]), "w_branch_b": f(w_branch_b[0]),
        "w_out": f(w_out[0]),
        "w2a2": f(np.concatenate([np.asarray(rwkv_w2[0]), np.asarray(rwkv_a2[0])], axis=0)),
        "g2": f(rwkv_g2[0]),
        "gpost_a": f(np.asarray(attn_post_norm[0]).reshape(1, D)),
        "gpost_f": f(np.asarray(ffn_post_norm[0]).reshape(1, D)),
        "w_up": f(w_up[0]), "w_down": f(w_down[0]),
    }
    if _NC is None:
        _NC = build()
    xs = np.asarray(x, dtype=np.float32)
    in_maps = [dict(shared, x=f(xs[b])) for b in range(8)]
    res = run_bass_kernel_spmd(_NC, in_maps, core_ids=list(range(8)))
    out = np.stack([np.asarray(r["out"]) for r in res.results], axis=0)
    kernel.last_results = res.results
    return out.astype(np.float32)
```

```python
import contextlib
import sys
import math
import numpy as np
import concourse.bass as bass
import concourse.mybir as mybir
from concourse.bass_utils import run_bass_kernel_spmd

F32 = mybir.dt.float32
BF16 = mybir.dt.bfloat16
AF = mybir.ActivationFunctionType
ALU = mybir.AluOpType

T = 2048
D = 1024
TBS = 512
NB = T // TBS
CH = 128
NCC = TBS // CH
DFF = 2816
NJ = DFF // 128
INC = 9472
EPS = 1e-6
GN_EPS = 1e-5 * 64
C0 = math.exp(-0.5)
HSCALE = 128 ** -0.5

CP = {}
_o = 0
for _n, _w in [("g1", 8), ("lb0", 8), ("lb1", 8), ("gn", 8), ("mu", 26), ("w0", 8), ("a0", 8),
               ("kk", 8), ("ka", 8), ("rk", 8), ("lnw", 8), ("lnb", 8), ("g3", 8),
               ("cw", 132), ("cb", 44)]:
    CP[_n] = _o
    _o += _w
NCOL = _o

DEBUG = {}
MAXOPS = None
SCHED = True
SHOP = 0.3
NIB = 5
CSTRIDE = 5
PSUM_PREFIXES = ("pj", "ptr", "psc", "pst", "phg", "pmi")


class Op:
    __slots__ = ("eng", "fn", "deps", "sig", "seq", "dma", "dsem", "dval", "idx", "tag", "ph", "alldeps", "cost", "lat")


class Prog:
    NDS = 24

    def __init__(self):
        self.ops = []
        self.lastw = {}
        self.readers = {}
        self.dma_cnt = {}
        self.dma_last = {}
        self.dma_prevq = {}
        self.phase = ''

    def add(self, eng, fn, r=(), w=(), dma=False, cost=0.3):
        if MAXOPS is not None and len(self.ops) >= MAXOPS:
            return None
        op = Op()
        op.eng, op.fn, op.dma, op.sig, op.seq = eng, fn, dma, False, 0
        op.idx = len(self.ops)
        op.ph = self.phase
        fr = sys._getframe(1)
        tg = []
        while fr is not None and len(tg) < 3:
            tg.append(str(fr.f_lineno))
            fr = fr.f_back
        op.tag = "/".join(tg)
        deps = {}
        for k in r:
            d = self.lastw.get(k)
            if d is not None:
                deps[d.idx] = (d, True)
            if k.startswith(PSUM_PREFIXES):
                for d in self.readers.get(k, ()):
                    if d.eng != eng and d.idx not in deps:
                        deps[d.idx] = (d, False)
        for k in w:
            d = self.lastw.get(k)
            if d is not None and d.idx not in deps:
                deps[d.idx] = (d, False)
            for d in self.readers.get(k, ()):
                if d.idx not in deps:
                    deps[d.idx] = (d, False)
        keep = []
        op.alldeps = [d for d, raw in deps.values() if d is not op]
        op.cost = cost
        op.lat = cost
        if dma:
            op.cost = 1.0 if eng == "pool" else 0.15
            op.lat = cost
            prevq = self.dma_prevq.get(eng)
            if prevq is not None:
                op.alldeps.append(prevq)
            self.dma_prevq[eng] = op
        for d, raw in deps.values():
            if d is op:
                continue
            if d.eng == eng and not d.dma and not dma:
                if eng == "pe":
                    continue
            keep.append(d)
            d.sig = True
        op.deps = keep
        for k in r:
            self.readers.setdefault(k, []).append(op)
        for k in w:
            self.lastw[k] = op
            self.readers[k] = []
        if dma:
            i = self.dma_cnt.get(eng, 0)
            self.dma_cnt[eng] = i + 1
            op.dsem = (eng, i % self.NDS)
            op.dval = 16 * (i // self.NDS + 1)
            prev = self.dma_last.get(op.dsem)
            if prev is not None and prev not in op.deps:
                op.deps.append(prev)
            if prev is not None and prev not in op.alldeps:
                op.alldeps.append(prev)
            self.dma_last[op.dsem] = op
        self.ops.append(op)
        return op

    def schedule(self):
        import heapq
        ops = self.ops
        n = len(ops)
        ndep = [0] * n
        users = [[] for _ in range(n)]
        for op in ops:
            ds = {d.idx for d in op.alldeps}
            ndep[op.idx] = len(ds)
            for di in ds:
                users[di].append(op.idx)
        finish = [0.0] * n
        ready = [0.0] * n
        efree = {}
        heap = []
        for op in ops:
            if ndep[op.idx] == 0:
                heapq.heappush(heap, (0.0, op.idx))
        order = []
        HOP = SHOP
        while heap:
            key, i = heapq.heappop(heap)
            op = ops[i]
            st = max(ready[i], efree.get(op.eng, 0.0))
            if st > key + 1e-9:
                heapq.heappush(heap, (st, i))
                continue
            efree[op.eng] = st + op.cost
            finish[i] = st + op.lat
            order.append((st, i))
            for u in users[i]:
                uo = ops[u]
                lat = HOP if uo.eng != op.eng else 0.1
                if uo.eng == "pe" and op.eng == "pe":
                    lat = 0.0
                ready[u] = max(ready[u], finish[i] + lat)
                ndep[u] -= 1
                if ndep[u] == 0:
                    heapq.heappush(heap, (ready[u], u))
        assert len(order) == n, (len(order), n)
        order.sort()
        self.ops = [ops[i] for _, i in order]
        self.est_total = max(finish)

    def emit(self, nc, block, sems, dsems):
        if SCHED:
            self.schedule()
        cnt = {}
        for op in self.ops:
            if op.dma:
                continue
            if op.sig:
                cnt[op.eng] = cnt.get(op.eng, 0) + 1
                op.seq = cnt[op.eng]
        byeng = {}
        for op in self.ops:
            byeng.setdefault(op.eng, []).append(op)

        def run(e, ename):
            waited = {}
            for op in byeng.get(ename, []):
                need = {}
                for d in op.deps:
                    if d.dma:
                        key, val = ("d",) + d.dsem, d.dval
                    else:
                        key, val = ("e", d.eng), d.seq
                    if val > need.get(key, 0):
                        need[key] = val
                for key, val in need.items():
                    if waited.get(key, 0) >= val:
                        continue
                    waited[key] = val
                    s = dsems[key[1:]] if key[0] == "d" else sems[key[1]]
                    e.wait_ge(s, val)
                inst = op.fn(e)
                if op.dma:
                    inst.then_inc(dsems[op.dsem], 16)
                elif op.sig:
                    inst.then_inc(sems[ename], 1)
            n = self.dma_cnt.get(ename, 0)
            for i in range(min(n, self.NDS)):
                tot = (n - i + self.NDS - 1) // self.NDS
                e.wait_ge(dsems[(ename, i)], 16 * tot)

        @block.sync
        def _(e):
            run(e, "sp")

        @block.tensor
        def _(e):
            run(e, "pe")

        @block.scalar
        def _(e):
            run(e, "act")

        @block.vector
        def _(e):
            run(e, "dve")

        @block.gpsimd
        def _(e):
            run(e, "pool")


def build():
    nc = bass.Bass("TRN2", target_bir_lowering=False)
    global _P
    P = Prog()
    _P = P
    es = contextlib.ExitStack()

    def dram(name, shape, kind="ExternalInput"):
        return nc.dram_tensor(name, shape, F32, kind=kind).ap()

    x_d = dram("x", [T, D])
    cols_d = dram("cols", [128, NCOL])
    win_d = dram("w_in", [D, INC])
    wba_d = dram("w_branch_a", [D, D])
    wbb_d = dram("w_branch_b", [D, D])
    wout_d = dram("w_out", [D, D])
    w2a2_d = dram("w2a2", [128, D])
    g2_d = dram("g2", [128, D])
    gpa_d = dram("gpost_a", [1, D])
    gpf_d = dram("gpost_f", [1, D])
    wup_d = dram("w_up", [D, 2 * DFF])
    wdn_d = dram("w_down", [DFF, D])
    out_d = dram("out", [T, D], kind="ExternalOutput")
    dbg_d = {}
    for k, shp in DEBUG.items():
        dbg_d[k] = dram("dbg_" + k, shp, kind="ExternalOutput")

    def sb(name, shape, dt=F32):
        return es.enter_context(nc.sbuf_tensor("sb_" + name, shape, dt))

    def ps(name, shape, dt=F32):
        return es.enter_context(nc.psum_tensor("ps_" + name, shape, dt))

    def fsz(ap):
        n_ = 1
        for d_ in ap.shape[1:]:
            n_ *= d_
        return n_

    def mm(out, lhsT, rhs, r, w, start=True, stop=True):
        f32 = (rhs.dtype == F32)
        c_ = 0.11 + fsz(rhs) / 2000.0 * (4.0 if f32 else 1.0)
        P.add("pe", lambda e: e.matmul(out, lhsT, rhs, start=start, stop=stop), r, w, cost=c_)

    def mmg(out, pairs, r, w):
        n = len(pairs)
        for i, (l, rr) in enumerate(pairs):
            mm(out, l, rr, r, w, start=(i == 0), stop=(i == n - 1))

    def tr(out, in_, ident_ap, r, w):
        P.add("pe", lambda e: e.transpose(out, in_, ident_ap), r, w, cost=0.2)

    def act(out, in_, func, r, w, bias=None, scale=None, accum=None, eng="act"):
        kw = {}
        if bias is not None:
            kw["bias"] = bias
        if scale is not None:
            kw["scale"] = scale
        if accum is not None:
            kw["accum_out"] = accum
        P.add("act", lambda e: e.activation(out=out, in_=in_, func=func, **kw), r, w,
              cost=0.25 + fsz(in_) * 0.00085 + (0.1 if accum is not None else 0.0))

    def tt(eng, out, a, b, op, r, w):
        P.add(eng, lambda e: e.tensor_tensor(out=out, in0=a, in1=b, op=op), r, w,
              cost=(0.1 + fsz(out) / 900.0) if eng == "dve" else (0.3 + fsz(out) * 0.0022))

    def ts(eng, out, a, s1, op0, r, w, s2=None, op1=None):
        if op1 is None:
            P.add(eng, lambda e: e.tensor_scalar(out=out, in0=a, scalar1=s1, scalar2=None, op0=op0), r, w,
                  cost=(0.1 + fsz(out) / 900.0) if eng == "dve" else (0.3 + fsz(out) * 0.0022))
        else:
            P.add(eng, lambda e: e.tensor_scalar(out=out, in0=a, scalar1=s1, scalar2=s2, op0=op0, op1=op1), r, w,
                  cost=(0.1 + fsz(out) / 900.0) if eng == "dve" else (0.3 + fsz(out) * 0.0022))

    def stt(out, a, s, b, op0, op1, r, w):
        P.add("dve", lambda e: e.scalar_tensor_tensor(out=out, in0=a, scalar=s, in1=b, op0=op0, op1=op1), r, w,
              cost=0.1 + fsz(out) / 900.0)

    def cp(eng, out, in_, r, w):
        if eng == "act":
            P.add("act", lambda e: e.activation(out=out, in_=in_, func=AF.Copy), r, w,
                  cost=0.25 + fsz(in_) * 0.00085)
        else:
            P.add(eng, lambda e: e.tensor_copy(out=out, in_=in_), r, w,
                  cost=(0.08 + fsz(out) / 1100.0) if eng == "dve" else (0.3 + fsz(out) * 0.003))

    def recip(out, in_, r, w):
        P.add("dve", lambda e: e.reciprocal(out=out, in_=in_), r, w, cost=0.1 + fsz(out) * 0.004)

    def scan(out, d0, d1, r, w):
        P.add("dve", lambda e: e.tensor_tensor_scan(out=out, data0=d0, data1=d1, initial=0.0,
                                                    op0=ALU.mult, op1=ALU.add), r, w, cost=0.1 + fsz(out) / 450.0)

    def memset(eng, ap, val, w):
        P.add(eng, lambda e: e.memset(ap, val), (), w)

    def dma(eng, out, in_, r, w):
        P.add(eng, lambda e: e.dma_start(out=out, in_=in_), r, w, dma=True,
              cost=(3.0 + fsz(out) * 128 * 2 / 150e3) if eng == "pool" else (2.5 + fsz(out) * 128 * 3 / 200e3))

    def dbg(name, ap, r):
        if name in dbg_d:
            dma("sp", dbg_d[name], ap, r, ["dbg_" + name])

    ident = sb("ident", [128, 128], BF16)
    identf = sb("identf", [128, 128], F32)
    mask2 = sb("mask2", [128, 256], F32)
    strictT = sb("strictT", [128, 128], F32)
    bones = sb("bones", [128, 128], F32)
    onesf = sb("onesf", [128, 128], F32)
    rmask = sb("rmask", [128, TBS], F32)
    cols = sb("cols", [128, NCOL], F32)
    lbc = sb("lbc", [128, 8], F32)
    omlc = sb("omlc", [128, 8], F32)
    omu = sb("omu", [128, 26], F32)
    gp = sb("gp", [128, D], F32)
    w2a2 = sb("w2a2", [128, D], BF16)
    g2 = sb("g2sb", [128, D], BF16)
    lnsc = sb("lnsc", [128, 1], F32)

    memset("pool", identf[:], 0.0, ["identf"])
    P.add("pool", lambda e: e.affine_select(out=identf[:], in_=identf[:], pattern=[[-1, 128]],
                                            compare_op=ALU.not_equal, fill=1.0, base=0,
                                            channel_multiplier=1), ["identf"], ["identf"])
    cp("pool", ident[:], identf[:], ["identf"], ["ident"])
    memset("pool", mask2[:], 1.0, ["mask2"])
    P.add("pool", lambda e: e.affine_select(out=mask2[:, 0:128], in_=mask2[:, 0:128], pattern=[[1, 128]],
                                            compare_op=ALU.is_gt, fill=0.0, base=0,
                                            channel_multiplier=-1), ["mask2"], ["mask2"])
    P.add("pool", lambda e: e.affine_select(out=mask2[:, 128:256], in_=mask2[:, 128:256], pattern=[[1, 128]],
                                            compare_op=ALU.is_ge, fill=0.0, base=0,
                                            channel_multiplier=-1), ["mask2"], ["mask2"])
    memset("pool", strictT[:], 1.0, ["strictT"])
    P.add("pool", lambda e: e.affine_select(out=strictT[:], in_=strictT[:], pattern=[[-1, 128]],
                                            compare_op=ALU.is_gt, fill=0.0, base=0,
                                            channel_multiplier=1), ["strictT"], ["strictT"])
    memset("pool", bones[:], 0.0, ["bones"])
    memset("pool", bones[0:64, 0:64], 1.0, ["bones"])
    memset("pool", bones[64:128, 64:128], 1.0, ["bones"])
    memset("pool", onesf[:], 1.0, ["onesf"])
    memset("pool", rmask[:], 1.0, ["rmask"])
    memset("pool", rmask[:].rearrange("p (c t) -> p c t", t=CH)[:, :, 0:1], 0.0, ["rmask"])
    memset("pool", lnsc[:], math.log(HSCALE), ["lnsc"])

    dma("sp", cols[:], cols_d[:, :], [], ["cols"])
    dma("pool", w2a2[:], w2a2_d[:, :], [], ["w2a2"])
    dma("pool", g2[:], g2_d[:, :], [], ["g2"])

    def col(name, i, n=1):
        o = CP[name] + i
        return cols[:, o:o + n]

    tt("dve", lbc[:], col("lb0", 0, 8), col("lb1", 0, 8), ALU.subtract, ["cols"], ["lbc"])
    act(lbc[:], lbc[:], AF.Sigmoid, ["lbc"], ["lbc"])
    ts("dve", omlc[:], lbc[:], -1.0, ALU.mult, ["lbc"], ["omlc"], s2=1.0, op1=ALU.add)
    ts("dve", omu[:], cols[:, CP["mu"]:CP["mu"] + 26], -1.0, ALU.mult, ["cols"], ["omu"], s2=1.0, op1=ALU.add)

    Sh = sb("Sh", [128, 8, 128], F32)
    Hr = sb("Hr", [128, 8, 128], F32)
    memset("pool", Sh[:], 0.0, ["Sh%d" % g for g in range(8)])
    memset("pool", Hr[:], 0.0, ["Hr%d" % g for g in range(8)])
    carry = sb("carry", [128, 26], F32)
    memset("pool", carry[:], 0.0, ["carry%d" % i for i in range(26)])
    halo = sb("halo", [128, 44, 2], F32)
    memset("pool", halo[:], 0.0, ["halo%d" % i for i in range(44)])

    NPAD = 1
    Vpad = [[sb("Vpad%d_%d" % (s, h), [128, NCC, 128], BF16) for h in range(2)] for s in range(NPAD)]
    Bpad = [[sb("Bpad%d_%d" % (s, h), [128, NCC, 128], BF16) for h in range(2)] for s in range(NPAD)]
    Kpad = [[sb("Kpad%d_%d" % (s, h), [128, NCC, 128], BF16) for h in range(2)] for s in range(NPAD)]
    Upad = [[sb("Upad%d_%d" % (s, h), [128, 128], BF16) for h in range(2)] for s in range(2)]
    for s in range(NPAD):
        for h in range(2):
            for nm, bufs in (("Vpad", Vpad), ("Bpad", Bpad), ("Kpad", Kpad)):
                memset("pool", bufs[s][h][:], 0.0, ["%s%d_%d_%d" % (nm, s, h, c) for c in range(NCC)])
    for s in range(2):
        for h in range(2):
            memset("pool", Upad[s][h][:], 0.0, ["Upad%d_%d" % (s, h)])

    xt = [sb("xt%d" % i, [128, D], F32) for i in range(2)]
    junk = sb("junk", [128, D], BF16)
    xnb = [sb("xnb%d" % i, [128, D], BF16) for i in range(2)]
    st1 = [sb("st1_%d" % i, [128, 4], F32) for i in range(4)]
    xnT = sb("xnT", [128, 8, TBS], BF16)
    xn2T = xnT
    AAR = sb("AAR", [128, 24, TBS], BF16)
    OAT = AAR[:, 0:8, :]
    OBT = AAR[:, 8:16, :]
    MT = AAR[:, 16:24, :]
    ACTT = AAR[:, 0:NJ, :]
    hblk = sb("hblk", [128, NCC, D], F32)
    LT = sb("LT", [128, TBS], BF16)
    LG = sb("LG", [128, TBS], BF16)
    Wl = sb("Wl", [128, 8, 256], BF16)
    WAR = sb("WAR", [128, 16 * 1024], BF16)

    def arena(slot0, nslots, pattern, **kw):
        ap = WAR[:, slot0 * 1024:(slot0 + nslots) * 1024].rearrange(pattern, **kw)
        return ap, ["ar%d" % i for i in range(slot0, slot0 + nslots)]
    Wg = [arena(7 * i, 7, "p (j c n) -> p j c n", c=8, j=7) for i in range(2)]
    Wm = [arena(4 * i, 4, "p (j c n) -> p j c n", c=8, j=4) for i in range(2)]
    WO, WOK = arena(8, 8, "p (c n) -> p c n", c=8)
    WU = [arena(2 * i, 2, "p (j c n) -> p j c n", c=8, j=2) for i in range(5)]
    WDR = [arena(10 + i, 1, "p (j n) -> p j n", j=1) for i in range(4)]
    NF = 16
    ftmp = [sb("ft%d" % i, [128, TBS], F32) for i in range(NF)]
    Vh2 = [sb("Vh%d" % i, [128, NCC, 128], BF16) for i in range(2)]
    QI2 = [sb("QI%d" % i, [128, TBS], BF16) for i in range(2)]
    KI2 = [sb("KI%d" % i, [128, TBS], BF16) for i in range(2)]
    KItm = [sb("KItm%d" % i, [128, 128], BF16) for i in range(2)]
    SCT = [sb("SCT%d" % i, [128, 128], BF16) for i in range(2)]
    Sg = [sb("Sg%d" % i, [128, 128], BF16) for i in range(2)]
    OT = sb("OT", [128, TBS], F32)
    SHG2 = [sb("SHG%d" % i, [128, TBS], BF16) for i in range(2)]
    sc4 = [sb("sc4_%d" % i, [128, 4], F32) for i in range(16)]
    ART = sb("ART", [128, NCC, 2, 128], BF16)
    BT = sb("BT", [128, TBS], BF16)
    KT = sb("KT", [128, TBS], BF16)
    RTlo = sb("RTlo", [128, TBS], BF16)
    KTlo = sb("KTlo", [128, TBS], BF16)
    VB = sb("VB", [128, TBS], BF16)
    GG = sb("GG", [128, TBS], F32)
    BON = sb("BON", [128, TBS], F32)
    YT = sb("YT", [128, TBS], F32)
    NTARB = [[sb("NTARB%d_%d" % (c, h), [128, 256], BF16) for h in range(2)] for c in range(NCC)]
    AKRK = [[sb("AKRK%d_%d" % (c, h), [128, 256], BF16) for h in range(2)] for c in range(NCC)]
    XT = [[sb("XT%d_%d" % (c, h), [128, 128], BF16) for h in range(2)] for c in range(NCC)]
    PXb = [sb("PXb%d" % i, [128, 384], BF16) for i in range(8)]
    PX = [t_[:, 0:256] for t_ in PXb]
    Pb = [t_[:, 256:384] for t_ in PXb]
    G0 = [sb("G0_%d" % i, [128, 128], BF16) for i in range(2)]
    RHSb = [sb("RHSb%d" % i, [128, 128], BF16) for i in range(2)]
    tmp128 = [sb("tmp128_%d" % i, [128, 128], F32) for i in range(2)]

    pj = [ps("pj%d" % i, [128, 512]) for i in range(2)]
    ptr = ps("ptr", [128, 8, 128], BF16)
    psc = [ps("psc%d" % i, [128, 512]) for i in range(2)]
    pst = ps("pst", [128, 4, 128])
    phg = ps("phg", [128, 4, 128])
    pmi = ps("pmi", [128, 512])

    cnt = {"pj": 0, "raw": 0, "ft": 0}

    def nxt(name, n):
        i = cnt.get(name, 0)
        cnt[name] = i + 1
        return i % n

    wcache = {}

    def wload(tag, dst, src, keys, ttb):
        if tag not in wcache:
            n = 1
            for d_ in dst.shape[1:]:
                n *= d_
            wcache[tag] = nc.dram_tensor("wc_" + tag, [128, n], BF16, kind="Internal").ap()
        cv = wcache[tag]
        if len(dst.shape) == 3:
            cv = cv.rearrange("p (a b) -> p a b", a=dst.shape[1])
        if ttb == 0:
            dma("pool", dst, src, [], keys)
            dma("sp", cv, dst, keys, ["wc_" + tag])
        else:
            dma("sp", dst, cv, ["wc_" + tag], keys)

    for tb in range(NB):
        t0 = tb * TBS

        def norm_T(src_tile_fn, dstT, gname, pre):
            for i in range(NCC):
                xb, xk = src_tile_fn(i)
                s = nxt("st1", 4)
                st = st1[s]
                stk = "st1_%d" % s
                act(junk[:], xb, AF.Square, [xk], ["junk", stk], accum=st[:, 0:1])
                ts("dve", st[:, 1:2], st[:, 0:1], 1.0 / D, ALU.mult, [stk], [stk], s2=EPS, op1=ALU.add)
                act(st[:, 1:2], st[:, 1:2], AF.Sqrt, [stk], [stk])
                b = nxt("xnb", 2)
                recip(st[:, 2:3], st[:, 1:2], [stk], [stk])
                ts("dve", xnb[b][:], xb, st[:, 2:3], ALU.mult, [xk, stk], ["xnb%d" % b])
                for c in range(8):
                    tr(ptr[:, c, :], xnb[b][:, c * 128:(c + 1) * 128], ident[:], ["xnb%d" % b, "ident"], ["ptr"])
                tt("dve", dstT[:, :, i * 128:(i + 1) * 128], ptr[:, :, :],
                   cols[:, CP[gname]:CP[gname] + 8].unsqueeze(2).to_broadcast([128, 8, 128]),
                   ALU.mult, ["ptr", "cols"], ["%s_%d" % (pre, i)])

        def xsrc(i):
            b = nxt("xt", 2)
            dma("sp", xt[b][:], x_d[t0 + i * 128:t0 + (i + 1) * 128, :], [], ["xt%d" % b])
            return xt[b][:], "xt%d" % b

        P.phase = "A"
        norm_T(xsrc, xnT, "g1", "xnT")
        XNT_KEYS = ["xnT_%d" % i for i in range(NCC)]

        def shift_lerp(psrc, pkey, muidx, dst, dkey, eng2="dve"):
            ck = "carry%d" % muidx
            muc = col("mu", muidx)
            act(dst, psrc, AF.Identity, [pkey, "omu"], [dkey], scale=omu[:, muidx:muidx + 1])
            stt(dst[:, 1:TBS], psrc[:, 0:TBS - 1], muc, dst[:, 1:TBS], ALU.mult, ALU.add, [pkey, "cols", dkey], [dkey])
            stt(dst[:, 0:1], carry[:, muidx:muidx + 1], muc, dst[:, 0:1], ALU.mult, ALU.add, [ck, "cols", dkey], [dkey])
            cp("act", carry[:, muidx:muidx + 1], psrc[:, TBS - 1:TBS], [pkey], [ck])

        B4 = [(pj[0][:], "pj0"), (pj[1][:], "pj1"), (psc[0][:], "psc0"), (psc[1][:], "psc1")]
        B7 = B4 + [(pst[:].rearrange("p a b -> p (a b)"), "pst"), (phg[:].rearrange("p a b -> p (a b)"), "phg"),
                   (pmi[:], "pmi")]

        def proj_fm(wtile_fn, wkeys, rhsT, rkeys, banks=B4):
            b = nxt("pjb", len(banks))
            bap, bkey = banks[b]
            mmg(bap, [(wtile_fn(c), rhsT[:, c, :]) for c in range(8)], wkeys + rkeys, [bkey])
            return bap, bkey

        P.phase = "L"
        if tb == 0:
            wload("Wl", Wl[:], win_d[:, 7168:7424].rearrange("(c p) n -> p c n", p=128), ["Wl"], 0)
        pa, pk = proj_fm(lambda c: Wl[:, c, 0:128], ["Wl"], xnT, XNT_KEYS)
        f = nxt("ft", NF)
        shift_lerp(pa, pk, 24, ftmp[f][:], "ft%d" % f)
        act(LT[0:64, :], ftmp[f][0:64, :], AF.Tanh, ["ft%d" % f], ["LT"])
        cp("act", LT[64:128, :], ftmp[f][64:128, :], ["ft%d" % f], ["LT"])
        pa, pk = proj_fm(lambda c: Wl[:, c, 128:256], ["Wl"], xnT, XNT_KEYS)
        f = nxt("ft", NF)
        shift_lerp(pa, pk, 25, ftmp[f][:], "ft%d" % f)
        act(LG[:], ftmp[f][:], AF.Sigmoid, ["ft%d" % f], ["LG"])

        segs = [0, 1024, 2048, 3072, 4096, 5120, 6144]

        def load_Wg(g):
            wb = g % 2
            W, WK = Wg[wb]
            for j, so in enumerate(segs):
                wload("Wg%d_%d" % (g, j), W[:, j, :, :],
                      win_d[:, so + g * 128: so + (g + 1) * 128].rearrange("(c p) n -> p c n", p=128),
                      [WK[j]], tb)

        def F():
            i = nxt("ft", NF)
            return ftmp[i][:], "ft%d" % i

        ctx = {}

        def stageA(g):
            c = ctx.setdefault(g, {})
            W, WK = Wg[g % 2]
            if g + 1 < 8:
                load_Wg(g + 1)
            cnt["ft"] = 0
            pp = g % 2
            QI, KI, Vh, SHG = QI2[pp], KI2[pp], Vh2[pp], SHG2[pp]
            QIk, KIk, SHGk = "QI%d" % pp, "KI%d" % pp, "SHG%d" % pp

            def wfn(j):
                return (lambda cc_: W[:, j, cc_, :]), [WK[j]]
            fn_, wkk = wfn(0)
            pa, pk = proj_fm(fn_, wkk, xnT, XNT_KEYS)
            qs, qsk = F()
            act(qs, pa, AF.Silu, [pk], [qsk])
            yield
            fn_, wkk = wfn(1)
            pa, pk = proj_fm(fn_, wkk, xnT, XNT_KEYS)
            fg, fgk = F()
            act(fg, pa, AF.Sigmoid, [pk], [fgk])
            yield
            ts("dve", fg, fg, omlc[:, g:g + 1], ALU.mult, [fgk, "omlc", "lbc"], [fgk], s2=lbc[:, g:g + 1], op1=ALU.add)
            lnf, lnfk = F()
            act(lnf, fg, AF.Ln, [fgk], [lnfk])
            kkh, kkhk = F()
            ts("pool", kkh, fg, -1.0, ALU.mult, [fgk], [kkhk], s2=1.0, op1=ALU.add)
            yield
            bb, bbk = F()
            scan(bb, rmask[:], lnf, ["rmask", lnfk], [bbk])
            b3 = bb.rearrange("p (c t) -> p c t", t=CH)
            dd, ddk = F()
            tt("dve", dd.rearrange("p (c t) -> p c t", t=CH), b3, b3[:, :, 63:64].to_broadcast([128, NCC, CH]),
               ALU.subtract, [bbk], [ddk])
            yield
            e1, e1k = F()
            act(e1, dd, AF.Exp, [ddk, "lnsc"], [e1k], bias=lnsc[:, 0:1])
            tt("dve", QI[:], qs, e1, ALU.mult, [qsk, e1k], [QIk])
            yield
            e2, e2k = F()
            act(e2, dd, AF.Exp, [ddk], [e2k], scale=-1.0)
            tt("pool", KI[:], kkh, e2, ALU.mult, [kkhk, e2k], [KIk])
            yield
            si = nxt("sc4", 16)
            eref, erefk = sc4[si], "sc4_%d" % si
            act(eref[:].unsqueeze(2), b3[:, :, 63:64], AF.Exp, [bbk], [erefk])
            si = nxt("sc4", 16)
            elast, elastk = sc4[si], "sc4_%d" % si
            act(elast[:].unsqueeze(2), b3[:, :, 127:128], AF.Exp, [bbk], [elastk])
            si = nxt("sc4", 16)
            elr, elrk = sc4[si], "sc4_%d" % si
            tt("dve", elr[:].unsqueeze(2), b3[:, :, 127:128], b3[:, :, 63:64], ALU.subtract, [bbk], [elrk])
            act(elr[:], elr[:], AF.Exp, [elrk], [elrk])
            c["h"] = (eref, erefk, elast, elastk, elr, elrk)
            yield
            fn_, wkk = wfn(3)
            pa, pk = proj_fm(fn_, wkk, xnT, XNT_KEYS)
            act(SHG[:], pa, AF.Silu, [pk], [SHGk])
            yield
            for i in range(NCC):
                b = nxt("pj", 2)
                mmg(pj[b][:, 0:128], [(xnT[:, c_, i * 128:(i + 1) * 128], W[:, 2, c_, :]) for c_ in range(8)],
                    [WK[2], "xnT_%d" % i], ["pj%d" % b])
                cp("act", Vh[:, i, :], pj[b][:, 0:128], ["pj%d" % b], ["Vh%d_%d" % (pp, i)])
                yield

        def chainH(g):
            eref, erefk, elast, elastk, elr, elrk = ctx[g]["h"]
            SK = "Sh%d" % g
            pp = g % 2
            QI, KI, Vh, SHG = QI2[pp], KI2[pp], Vh2[pp], SHG2[pp]
            QIk, KIk, SHGk = "QI%d" % pp, "KI%d" % pp, "SHG%d" % pp
            for cc in range(NCC):
                csl = slice(cc * CH, (cc + 1) * CH)
                kb = nxt("KItm", 2)
                tr(ptr[:, 0, :], KI[:, csl], ident[:], [KIk, "ident"], ["ptr"])
                cp("act", KItm[kb][:], ptr[:, 0, :], ["ptr"], ["KItm%d" % kb])
                yield
                mm(phg[:, 0, :], KI[:, csl], QI[:, csl], [KIk, QIk], ["phg"])
                mm(phg[:, 2, :], KItm[kb][:], Vh[:, cc, :], ["KItm%d" % kb, "Vh%d_%d" % (pp, cc)], ["phg"])
                sb_ = nxt("SCT", 2)
                tt("dve", SCT[sb_][:], phg[:, 0, :], mask2[:, 128:256], ALU.mult, ["phg", "mask2"], ["SCT%d" % sb_])
                tb_ = nxt("tmp128", 2)
                ts("dve", tmp128[tb_][:], phg[:, 2, :], elr[:, cc:cc + 1], ALU.mult, ["phg", elrk], ["tmp128_%d" % tb_])
                gb = nxt("Sg", 2)
                ts("dve", Sg[gb][:], Sh[:, g, :], eref[:, cc:cc + 1], ALU.mult, [SK, erefk], ["Sg%d" % gb])
                yield
                mmg(phg[:, 1, :], [(Vh[:, cc, :], SCT[sb_][:]), (Sg[gb][:], QI[:, csl])],
                    ["Vh%d_%d" % (pp, cc), "SCT%d" % sb_, "Sg%d" % gb, QIk], ["phg"])
                stt(Sh[:, g, :], Sh[:, g, :], elast[:, cc:cc + 1], tmp128[tb_][:], ALU.mult, ALU.add,
                    [SK, elastk, "tmp128_%d" % tb_], [SK])
                cp("act", OT[:, csl], phg[:, 1, :], ["phg"], ["OT%d" % cc])
                yield
            OTK = ["OT%d" % c_ for c_ in range(NCC)]
            osq, osqk = ftmp[13][:], "ft13"
            act(osq, OT[:], AF.Square, OTK, [osqk])
            mm(pmi[:], onesf[:], osq, ["onesf", osqk], ["pmi"])
            yield
            sd, sdk = osq, osqk
            ts("dve", sd, pmi[:], 1.0 / 128, ALU.mult, ["pmi"], [sdk], s2=EPS, op1=ALU.add)
            act(sd, sd, AF.Sqrt, [sdk], [sdk])
            recip(sd, sd, [sdk], [sdk])
            yield
            tt("dve", sd, OT[:], sd, ALU.mult, OTK + [sdk], [sdk])
            stt(OAT[:, g, :], sd, col("gn", g), SHG[:], ALU.mult, ALU.mult, [sdk, "cols", SHGk], ["A%d" % g])
            yield

        def stageB(g):
            c = ctx.setdefault(g, {})
            W, WK = Wg[g % 2]
            cnt["ft"] = 0

            def wfn(j):
                return (lambda cc_: W[:, j, cc_, :]), [WK[j]]
            fn_, wkk = wfn(4)
            pa, pk = proj_fm(fn_, wkk, xnT, XNT_KEYS)
            RP, RPk = F()
            shift_lerp(pa, pk, g, RP, RPk)
            yield
            fn_, wkk = wfn(5)
            pa, pk = proj_fm(fn_, wkk, xnT, XNT_KEYS)
            KP, KPk = F()
            shift_lerp(pa, pk, 8 + g, KP, KPk)
            yield
            fn_, wkk = wfn(6)
            pa, pk = proj_fm(fn_, wkk, xnT, XNT_KEYS)
            VP, VPk = F()
            shift_lerp(pa, pk, 16 + g, VP, VPk)
            cp("pool", VB[:], VP, [VPk], ["VB"])
            yield
            gs = slice(g * 128, (g + 1) * 128)
            b = nxt("pj", 2)
            mm(pj[b][:], w2a2[0:64, gs], LT[0:64, :], ["w2a2", "LT"], ["pj%d" % b])
            lwp, lwpk = F()
            act(lwp, pj[b][:], AF.Sigmoid, ["pj%d" % b, "cols"], [lwpk], bias=col("w0", g))
            b = nxt("pj", 2)
            mm(pj[b][:], w2a2[64:128, gs], LT[64:128, :], ["w2a2", "LT"], ["pj%d" % b])
            asg, asgk = F()
            act(asg, pj[b][:], AF.Sigmoid, ["pj%d" % b, "cols"], [asgk], bias=col("a0", g))
            yield
            cwp, cwpk = F()
            scan(cwp, rmask[:], lwp, ["rmask", lwpk], [cwpk])
            c3 = cwp.rearrange("p (c t) -> p c t", t=CH)
            dd, ddk = F()
            tt("dve", dd.rearrange("p (c t) -> p c t", t=CH), c3, c3[:, :, 63:64].to_broadcast([128, NCC, CH]),
               ALU.subtract, [cwpk], [ddk])
            da, dak = F()
            tt("pool", da, dd, lwp, ALU.subtract, [ddk, lwpk], [dak])
            yield
            epos, eposk = F()
            act(epos, dd, AF.Exp, [ddk], [eposk], scale=-C0)
            eneg, enegk = F()
            act(eneg, dd, AF.Exp, [ddk], [enegk], scale=C0)
            act(da, da, AF.Exp, [dak], [dak], scale=-C0)
            yield
            si = nxt("sc4", 16)
            reref, rerefk = sc4[si], "sc4_%d" % si
            act(reref[:].unsqueeze(2), c3[:, :, 63:64], AF.Exp, [cwpk], [rerefk], scale=-C0)
            si = nxt("sc4", 16)
            relast, relastk = sc4[si], "sc4_%d" % si
            act(relast[:].unsqueeze(2), c3[:, :, 127:128], AF.Exp, [cwpk], [relastk], scale=-C0)
            si = nxt("sc4", 16)
            relr, relrk = sc4[si], "sc4_%d" % si
            tt("dve", relr[:].unsqueeze(2), c3[:, :, 127:128], c3[:, :, 63:64], ALU.subtract, [cwpk], [relrk])
            act(relr[:], relr[:], AF.Exp, [relrk], [relrk], scale=-C0)
            c["r"] = (reref, rerefk, relast, relastk, relr, relrk)
            yield
            kkr, kkrk = F()
            ts("dve", kkr, KP, col("kk", g), ALU.mult, [KPk, "cols"], [kkrk])
            sq, sqk = F()
            act(sq, kkr, AF.Square, [kkrk], [sqk])
            mm(pmi[:], bones[:], sq, ["bones", sqk], ["pmi"])
            act(sq, pmi[:], AF.Sqrt, ["pmi"], [sqk])
            yield
            ts("dve", sq, sq, 1e-12, ALU.max, [sqk], [sqk])
            recip(sq, sq, [sqk], [sqk])
            tt("dve", kkr, kkr, sq, ALU.mult, [kkrk, sqk], [kkrk])
            k2, k2k = F()
            ts("pool", k2, asg, -1.0, ALU.add, [asgk, "cols"], [k2k], s2=col("ka", g), op1=ALU.mult)
            stt(k2, k2, 1.0, KP, ALU.add, ALU.mult, [k2k, KPk], [k2k])
            yield
            c["b1"] = dict(RP=RP, RPk=RPk, VP=VP, VPk=VPk, asg=asg, asgk=asgk, da=da, dak=dak, epos=epos,
                           eposk=eposk, eneg=eneg, enegk=enegk, kkr=kkr, kkrk=kkrk, sq=sq, sqk=sqk, k2=k2, k2k=k2k)

        def stageB2(g):
            c = ctx[g]
            d_ = c["b1"]
            RP, RPk, VP, VPk, asg, asgk = d_["RP"], d_["RPk"], d_["VP"], d_["VPk"], d_["asg"], d_["asgk"]
            da, dak, epos, eposk, eneg, enegk = d_["da"], d_["dak"], d_["epos"], d_["eposk"], d_["eneg"], d_["enegk"]
            kkr, kkrk, sq, sqk, k2, k2k = d_["kkr"], d_["kkrk"], d_["sq"], d_["sqk"], d_["k2"], d_["k2k"]
            gs = slice(g * 128, (g + 1) * 128)
            b = nxt("pj", 2)
            mm(pj[b][:], g2[:, gs], LG[:], ["g2", "LG"], ["pj%d" % b])
            cp("act", GG[:], pj[b][:], ["pj%d" % b], ["GG"])
            yield
            stt(sq, RP, col("rk", g), k2, ALU.mult, ALU.mult, [RPk, "cols", k2k], [sqk])
            mm(pmi[:], bones[:], sq, ["bones", sqk], ["pmi"])
            tt("dve", BON[:], pmi[:], VP, ALU.mult, ["pmi", VPk], ["BON"])
            yield
            tt("dve", epos, RP, epos, ALU.mult, [RPk, eposk], [eposk])
            cp("pool", ART[:, :, 1, :], epos.rearrange("p (c t) -> p c t", t=CH), [eposk], ["ART_R"])
            tt("pool", RTlo[:].rearrange("p (c t) -> p c t", t=CH), epos.rearrange("p (c t) -> p c t", t=CH),
               ART[:, :, 1, :], ALU.subtract, [eposk, "ART_R"], ["RTlo"])
            stt(ART[:, :, 0, :], kkr.rearrange("p (c t) -> p c t", t=CH), -1.0,
                da.rearrange("p (c t) -> p c t", t=CH), ALU.mult, ALU.mult, [kkrk, dak], ["ART_A"])
            tt("pool", sq, kkr, asg, ALU.mult, [kkrk, asgk], [sqk])
            tt("dve", BT[:], sq, eneg, ALU.mult, [sqk, enegk], ["BT"])
            tt("dve", k2, k2, eneg, ALU.mult, [k2k, enegk], [k2k])
            cp("pool", KT[:], k2, [k2k], ["KT"])
            tt("pool", KTlo[:], k2, KT[:], ALU.subtract, [k2k, "KT"], ["KTlo"])
            yield
            pset = 0
            trio = ((VB, "VB", Vpad, "Vpad"), (BT, "BT", Bpad, "Bpad"), (KT, "KT", Kpad, "Kpad"))
            for cc in range(NCC):
                csl = slice(cc * CH, (cc + 1) * CH)
                for q, (src, skey, bufs, nm) in enumerate(trio):
                    tr(ptr[:, 1 + q, :], src[:, csl], ident[:], [skey, "ident"], ["ptr"])
                for q, (src, skey, bufs, nm) in enumerate(trio):
                    for h in range(2):
                        hs = slice(64 * h, 64 * h + 64)
                        cp("act" if q != 1 else "dve", bufs[pset][h][:, cc, hs], ptr[:, 1 + q, hs], ["ptr"],
                           ["%s%d_%d_%d" % (nm, pset, h, cc)])
                yield

        def preR(g):
            chains = [(cc, h) for cc in range(NCC) for h in range(2)]
            IB = [(psc[0], "psc0"), (psc[1], "psc1"), (pst[:].rearrange("p a b -> p (a b)"), "pst"),
                  (pj[0], "pj0"), (pj[1], "pj1")][:NIB]
            for ci, (cc, h) in enumerate(chains):
                csl = slice(cc * CH, (cc + 1) * CH)
                ph = slice(64 * h, 64 * h + 64)
                pbank, pk2 = IB[nxt("ib", NIB)]
                art2 = ART[ph, cc, :, :].rearrange("p a t -> p (a t)")
                mm(pbank[:, 0:256], BT[ph, csl], art2, ["BT", "ART_A", "ART_R"], [pk2])
                P.add("pe", lambda e, o=pbank[:, 256:512], l=KT[ph, csl], rr=art2:
                      e.matmul(o, l, rr, start=True, stop=False, skip_group_check=True),
                      ["KT", "ART_A", "ART_R"], [pk2])
                P.add("pe", lambda e, o=pbank[:, 384:512], l=KT[ph, csl], rr=RTlo[ph, csl]:
                      e.matmul(o, l, rr, start=False, stop=False, skip_group_check=True), ["KT", "RTlo"], [pk2])
                P.add("pe", lambda e, o=pbank[:, 384:512], l=KTlo[ph, csl], rr=ART[ph, cc, 1, :]:
                      e.matmul(o, l, rr, start=False, stop=True, skip_group_check=True), ["KTlo", "ART_R"], [pk2])
                tt("dve", NTARB[cc][h][:], pbank[:, 0:256], mask2[:], ALU.mult, [pk2, "mask2"],
                   ["NTARB%d_%d" % (cc, h)])
                tt("dve", AKRK[cc][h][:], pbank[:, 256:512], mask2[:], ALU.mult, [pk2, "mask2"],
                   ["AKRK%d_%d" % (cc, h)])
                yield
                pbank, pk2 = IB[nxt("ib", NIB)]
                mm(pbank[:, 0:128], ART[ph, cc, 0, :], BT[ph, csl], ["BT", "ART_A"], [pk2])
                tt("dve", Pb[ci][:], pbank[:, 0:128], strictT[:], ALU.mult, [pk2, "strictT"], ["Pb%d" % ci])
                cp("pool", PX[ci][:, 0:128], NTARB[cc][h][:, 0:128], ["NTARB%d_%d" % (cc, h)], ["PX%d" % ci])
                tt("pool", PX[ci][:, 128:256], NTARB[cc][h][:, 0:128], ident[:], ALU.add,
                   ["NTARB%d_%d" % (cc, h), "ident"], ["PX%d" % ci])
                yield
            for j in range(0, 7):
                for ci, (cc, h) in enumerate(chains):
                    pxk, pbk = "PX%d" % ci, "Pb%d" % ci
                    ev = "act" if ci % 2 == 0 else "dve"
                    pbank, pk2 = IB[nxt("ib", NIB)]
                    if j == 0:
                        mm(pbank[:, 0:128], Pb[ci][:], PX[ci][:, 0:128], [pbk, pxk], [pk2])
                        mm(pbank[:, 256:384], PX[ci][:, 0:128], Pb[ci][:], [pbk, pxk], [pk2])
                        cp(ev, PX[ci][:, 0:128], pbank[:, 0:128], [pk2], [pxk])
                        cp(ev, Pb[ci][:], pbank[:, 256:384], [pk2], [pbk])
                    elif j < 6:
                        P.add("pe", lambda e, o=pbank[:, 0:256], l=Pb[ci][:], rr=PX[ci][:]:
                              e.matmul(o, l, rr, start=True, stop=False, skip_group_check=True), [pbk, pxk], [pk2])
                        P.add("pe", lambda e, o=pbank[:, 128:256], l=ident[:], rr=PX[ci][:, 128:256]:
                              e.matmul(o, l, rr, start=False, stop=True, skip_group_check=True), [pxk, "ident"], [pk2])
                        P.add("pe", lambda e, o=pbank[:, 256:384], l=PX[ci][:, 0:128], rr=Pb[ci][:]:
                              e.matmul(o, l, rr, start=True, stop=True, skip_group_check=True), [pbk, pxk], [pk2])
                        cp(ev, PXb[ci][:], pbank[:, 0:384], [pk2], [pxk, pbk])
                    else:
                        mmg(pbank[:, 0:128], [(Pb[ci][:], PX[ci][:, 128:256]),
                                                 (ident[:], PX[ci][:, 128:256])], [pbk, pxk, "ident"], [pk2])
                        cp(ev, XT[cc][h][:], pbank[:, 0:128], [pk2], ["XT%d_%d" % (cc, h)])
                    yield

        def chainR(g):
            reref, rerefk, relast, relastk, relr, relrk = ctx[g]["r"]
            pset = 0
            HK = "Hr%d" % g
            for cc in range(NCC):
                csl = slice(cc * CH, (cc + 1) * CH)
                g0 = nxt("G0", 2)
                ts("dve", G0[g0][:], Hr[:, g, :], reref[:, cc:cc + 1], ALU.mult, [HK, rerefk], ["G0_%d" % g0])
                vk = ["Vpad%d_%d_%d" % (pset, h, cc) for h in range(2)]
                bk = ["Bpad%d_%d_%d" % (pset, h, cc) for h in range(2)]
                kk_ = ["Kpad%d_%d_%d" % (pset, h, cc) for h in range(2)]
                ak = ["AKRK%d_%d" % (cc, h) for h in range(2)]
                nk = ["NTARB%d_%d" % (cc, h) for h in range(2)]
                mmg(pst[:, 0, :], [(AKRK[cc][0][:, 0:128], Vpad[pset][0][:, cc, :]),
                                   (AKRK[cc][1][:, 0:128], Vpad[pset][1][:, cc, :]),
                                   (ART[:, cc, 0, :], G0[g0][:])],
                    ak + vk + ["ART_A", "G0_%d" % g0], ["pst"])
                rb = nxt("RHSb", 2)
                cp("act", RHSb[rb][:], pst[:, 0, :], ["pst"], ["RHSb%d" % rb])
                yield
                us = nxt("Upad", 2)
                for h in range(2):
                    hs = slice(64 * h, 64 * h + 64)
                    mm(pst[:, 1, hs], XT[cc][h][:], RHSb[rb][:, hs], ["XT%d_%d" % (cc, h), "RHSb%d" % rb], ["pst"])
                for h in range(2):
                    hs = slice(64 * h, 64 * h + 64)
                    cp("act", Upad[us][h][:, hs], pst[:, 1, hs], ["pst"], ["Upad%d_%d" % (us, h)])
                yield
                uk = ["Upad%d_%d" % (us, h) for h in range(2)]
                mmg(pst[:, 2, :], [(G0[g0][:], ART[:, cc, 1, :]),
                                   (Upad[us][0][:], NTARB[cc][0][:, 128:256]),
                                   (Upad[us][1][:], NTARB[cc][1][:, 128:256]),
                                   (Vpad[pset][0][:, cc, :], AKRK[cc][0][:, 128:256]),
                                   (Vpad[pset][1][:, cc, :], AKRK[cc][1][:, 128:256])],
                    ["G0_%d" % g0, "ART_R"] + uk + nk + vk + ak, ["pst"])
                mmg(pst[:, 3, :], [(Bpad[pset][0][:, cc, :], Upad[us][0][:]),
                                   (Bpad[pset][1][:, cc, :], Upad[us][1][:]),
                                   (Kpad[pset][0][:, cc, :], Vpad[pset][0][:, cc, :]),
                                   (Kpad[pset][1][:, cc, :], Vpad[pset][1][:, cc, :])],
                    bk + uk + kk_ + vk, ["pst"])
                tb_ = nxt("tmp128", 2)
                ts("dve", tmp128[tb_][:], pst[:, 3, :], relr[:, cc:cc + 1], ALU.mult, ["pst", relrk],
                   ["tmp128_%d" % tb_])
                stt(Hr[:, g, :], Hr[:, g, :], relast[:, cc:cc + 1], tmp128[tb_][:], ALU.mult, ALU.add,
                    [HK, relastk, "tmp128_%d" % tb_], [HK])
                cp("act", YT[:, csl], pst[:, 2, :], ["pst"], ["YT%d" % cc])
                yield

        def normR(g):
            YTK = ["YT%d" % c_ for c_ in range(NCC)]
            mm(pmi[:], bones[:], YT[:], ["bones"] + YTK, ["pmi"])
            yc, yck = ftmp[14][:], "ft14"
            stt(yc, pmi[:], -1.0 / 64, YT[:], ALU.mult, ALU.add, ["pmi"] + YTK, [yck])
            ysq, ysqk = ftmp[15][:], "ft15"
            act(ysq, yc, AF.Square, [yck], [ysqk])
            mm(pmi[:], bones[:], ysq, ["bones", ysqk], ["pmi"])
            ts("dve", ysq, pmi[:], 1.0 / 64, ALU.mult, ["pmi"], [ysqk], s2=GN_EPS, op1=ALU.add)
            act(ysq, ysq, AF.Sqrt, [ysqk], [ysqk])
            recip(ysq, ysq, [ysqk], [ysqk])
            tt("dve", yc, yc, ysq, ALU.mult, [yck, ysqk], [yck])
            ts("dve", yc, yc, col("lnw", g), ALU.mult, [yck, "cols"], [yck], s2=col("lnb", g), op1=ALU.add)
            tt("pool", yc, yc, BON[:], ALU.add, [yck, "BON"], [yck])
            tt("dve", OBT[:, g, :], yc, GG[:], ALU.mult, [yck, "GG"], ["A%d" % (8 + g)])
            yield

        def run_all(*gens, strides=None):
            gens = list(gens)
            strides = list(strides) if strides else [1] * len(gens)
            rnd = 0
            while gens:
                for gg, st_ in list(zip(gens, strides)):
                    if rnd % st_ != 0 and len(gens) > 1:
                        continue
                    try:
                        next(gg)
                    except StopIteration:
                        i_ = gens.index(gg)
                        gens.pop(i_)
                        strides.pop(i_)
                rnd += 1

        def seq(*gens):
            for gg in gens:
                yield from gg

        if tb == 0:
            load_Wg(0)
        def par(*gens):
            gens = list(gens)
            while gens:
                for gg in list(gens):
                    try:
                        next(gg)
                        yield
                    except StopIteration:
                        gens.remove(gg)

        P.phase = "G.AB0"
        run_all(stageA(0))
        run_all(stageB(0))
        for g in range(8):
            P.phase = "G.C"
            if g + 1 < 8:
                run_all(chainH(g), seq(stageB2(g), par(preR(g), stageA(g + 1))), strides=[CSTRIDE, 1])
            else:
                run_all(chainH(g), seq(stageB2(g), preR(g)), strides=[CSTRIDE, 1])
            P.phase = "G.D"
            if g + 1 < 8:
                run_all(chainR(g), stageB(g + 1))
            else:
                run_all(chainR(g))
            P.phase = "G.N"
            run_all(normR(g))

        P.phase = "M"
        OAK = ["A%d" % g for g in range(8)]
        OBK = ["A%d" % (8 + g) for g in range(8)]
        def load_Wm(m):
            W, WK = Wm[m % 2]
            ms = slice(m * 128, (m + 1) * 128)
            wload("Wm%d_0" % m, W[:, 0, :, :], wba_d[:, ms].rearrange("(c p) n -> p c n", p=128), [WK[0]], tb)
            wload("Wm%d_1" % m, W[:, 1, :, :], wbb_d[:, ms].rearrange("(c p) n -> p c n", p=128), [WK[1]], tb)
            wload("Wm%d_2" % m, W[:, 2, :, :],
                  win_d[:, 7424 + m * 128:7424 + (m + 1) * 128].rearrange("(c p) n -> p c n", p=128), [WK[2]], tb)
            wload("Wm%d_3" % m, W[:, 3, :, :],
                  win_d[:, 8448 + m * 128:8448 + (m + 1) * 128].rearrange("(c p) n -> p c n", p=128), [WK[3]], tb)

        load_Wm(0)
        dma("sp", gp[:], gpa_d.partition_broadcast(128), [], ["gp"])
        wload("WO", WO[:], wout_d[:, :].rearrange("(c p) n -> p c n", p=128), WOK, tb)
        if tb + 1 < NB:
            wload("Wl", Wl[:], win_d[:, 7168:7424].rearrange("(c p) n -> p c n", p=128), ["Wl"], tb + 1)
        for m in range(8):
            W, WK = Wm[m % 2]
            if m + 1 < 8:
                load_Wm(m + 1)

            def F():
                i = nxt("ft", NF)
                return ftmp[i][:], "ft%d" % i
            pa, pk = proj_fm(lambda c: W[:, 2, c, :], [WK[2]], xnT, XNT_KEYS, banks=B7)
            sga, sgak = F()
            act(sga, pa, AF.Sigmoid, [pk], [sgak])
            pa, pk = proj_fm(lambda c: W[:, 3, c, :], [WK[3]], xnT, XNT_KEYS, banks=B7)
            sgb, sgbk = F()
            act(sgb, pa, AF.Sigmoid, [pk], [sgbk])
            pa, pk = proj_fm(lambda c: W[:, 0, c, :], [WK[0]], OAT, OAK, banks=B7)
            tt("dve", sga, sga, pa, ALU.mult, [sgak, pk], [sgak])
            pa, pk = proj_fm(lambda c: W[:, 1, c, :], [WK[1]], OBT, OBK, banks=B7)
            tt("dve", sgb, sgb, pa, ALU.mult, [sgbk, pk], [sgbk])
            tt("dve", MT[:, m, :], sga, sgb, ALU.add, [sgak, sgbk], ["A%d" % (16 + m)])
        MTK = ["A%d" % (16 + m) for m in range(8)]
        P.phase = "W"

        def load_WU(j):
            W, WK = WU[j % 5]
            wload("WU%d_0" % j, W[:, 0, :, :], wup_d[:, j * 128:(j + 1) * 128].rearrange("(c p) n -> p c n", p=128),
                  [WK[0]], tb)
            wload("WU%d_1" % j, W[:, 1, :, :],
                  wup_d[:, DFF + j * 128:DFF + (j + 1) * 128].rearrange("(c p) n -> p c n", p=128), [WK[1]], tb)

        def load_WD(j):
            Wd, WdK = WDR[j % 4]
            wload("WD%d" % j, Wd[:, 0, :], wdn_d[j * 128:(j + 1) * 128, :], WdK, tb)

        load_WU(0)
        load_WU(1)
        load_WU(2)
        load_WU(3)

        def post_norm_residual(i, res_ap, res_key, gp, gpk, dst, dkey, bA=None, bB=None):
            if bA is None:
                bA, bB = (pj[0][:], ["pj0"]), (pj[1][:], ["pj1"])
            s = nxt("st1", 4)
            st = st1[s]
            stk = "st1_%d" % s
            act(junk[:, 0:512], bA[0], AF.Square, bA[1], ["junk", stk], accum=st[:, 0:1])
            act(junk[:, 512:1024], bB[0], AF.Square, bB[1], ["junk", stk], accum=st[:, 1:2])
            tt("dve", st[:, 2:3], st[:, 0:1], st[:, 1:2], ALU.add, [stk], [stk])
            ts("dve", st[:, 2:3], st[:, 2:3], 1.0 / D, ALU.mult, [stk], [stk], s2=EPS, op1=ALU.add)
            act(st[:, 2:3], st[:, 2:3], AF.Sqrt, [stk], [stk])
            recip(st[:, 3:4], st[:, 2:3], [stk], [stk])
            for hh, bk in enumerate((bA, bB)):
                sl = slice(hh * 512, (hh + 1) * 512)
                stt(dst[:, sl], bk[0], st[:, 3:4], gp[:, sl], ALU.mult, ALU.mult,
                    bk[1] + [stk, gpk], [dkey])
            tt("pool", dst, dst, res_ap, ALU.add, [dkey, res_key], [dkey])

        for i in range(NCC):
            isl = slice(i * 128, (i + 1) * 128)
            for hh in range(2):
                mmg(pj[hh][:], [(MT[:, c, isl], WO[:, c, hh * 512:(hh + 1) * 512]) for c in range(8)],
                    MTK + WOK, ["pj%d" % hh])
            b = nxt("xt", 2)
            dma("sp", xt[b][:], x_d[t0 + i * 128:t0 + (i + 1) * 128, :], [], ["xt%d" % b])
            post_norm_residual(i, xt[b][:], "xt%d" % b, gp, "gp", hblk[:, i, :], "hblk%d" % i)

        P.phase = "FN"
        norm_T(lambda i: (hblk[:, i, :], "hblk%d" % i), xn2T, "g3", "xnT")
        X2K = ["xnT_%d" % i for i in range(NCC)]
        P.phase = "FU"
        dma("sp", gp[:], gpf_d.partition_broadcast(128), [], ["gp"])
        for j in range(NJ):
            W, WK = WU[j % 5]
            if j + 4 < NJ:
                load_WU(j + 4)
            if j == NJ - 2:
                load_WD(0)
                load_WD(1)
            res = []
            for q in range(2):
                ci = q * NJ + j
                pa, pk = proj_fm(lambda c, q=q: W[:, q, c, :], [WK[q]], xn2T, X2K, banks=B7)
                hk = "halo%d" % ci
                f = nxt("ft", NF)
                A = ftmp[f][:]
                fk = "ft%d" % f
                cwo = CP["cw"]
                w0c = cols[:, cwo + ci:cwo + ci + 1]
                w1c = cols[:, cwo + 44 + ci:cwo + 44 + ci + 1]
                w2c = cols[:, cwo + 2 * 44 + ci:cwo + 2 * 44 + ci + 1]
                bc = cols[:, CP["cb"] + ci:CP["cb"] + ci + 1]
                act(A, pa, AF.Identity, [pk, "cols"], [fk], bias=bc, scale=w2c)
                stt(A[:, 1:TBS], pa[:, 0:TBS - 1], w1c, A[:, 1:TBS], ALU.mult, ALU.add, [pk, "cols", fk], [fk])
                stt(A[:, 2:TBS], pa[:, 0:TBS - 2], w0c, A[:, 2:TBS], ALU.mult, ALU.add, [pk, "cols", fk], [fk])
                stt(A[:, 0:1], halo[:, ci, 1:2], w1c, A[:, 0:1], ALU.mult, ALU.add, [hk, "cols", fk], [fk])
                stt(A[:, 0:1], halo[:, ci, 0:1], w0c, A[:, 0:1], ALU.mult, ALU.add, [hk, "cols", fk], [fk])
                stt(A[:, 1:2], halo[:, ci, 1:2], w0c, A[:, 1:2], ALU.mult, ALU.add, [hk, "cols", fk], [fk])
                cp("act", halo[:, ci, :], pa[:, TBS - 2:TBS], [pk], [hk])
                res.append((A, fk))
            (ga_, gak), (va_, vak) = res
            act(ga_, ga_, AF.Silu, [gak], [gak])
            tt("dve", ACTT[:, j, :], ga_, va_, ALU.mult, [gak, vak], ["A%d" % j])
        AK = ["A%d" % j for j in range(NJ)]
        P.phase = "FD"
        banks = [(pj[0][:], ["pj0"]), (pj[1][:], ["pj1"]), (psc[0][:], ["psc0"]),
                 (psc[1][:], ["psc1"]), (pmi[:], ["pmi"]),
                 (pst[:].rearrange("p a b -> p (a b)"), ["pst"]),
                 (phg[:].rearrange("p a b -> p (a b)"), ["phg"]),
                 (ptr[:].rearrange("p a b -> p (a b)").bitcast(F32), ["ptr"])]
        load_WD(2)
        for j in range(NJ):
            Wd, WdK = WDR[j % 4]
            if j + 3 < NJ:
                load_WD(j + 3)
            for i in range(NCC):
                isl = slice(i * 128, (i + 1) * 128)
                for hh in range(2):
                    bk = banks[2 * i + hh]
                    mm(bk[0], ACTT[:, j, isl], Wd[:, 0, hh * 512:(hh + 1) * 512], ["A%d" % j] + WdK, bk[1],
                       start=(j == 0), stop=(j == NJ - 1))
        if tb + 1 < NB:
            tb_next_g0 = True
            W_, WK_ = Wg[0]
            for j_, so_ in enumerate([0, 1024, 2048, 3072, 4096, 5120, 6144]):
                wload("Wg0_%d" % j_, W_[:, j_, :, :], win_d[:, so_: so_ + 128].rearrange("(c p) n -> p c n", p=128),
                      [WK_[j_]], tb + 1)
        for i in range(NCC):
            b = nxt("xt", 2)
            post_norm_residual(i, hblk[:, i, :], "hblk%d" % i, gp, "gp", xt[b][:], "xt%d" % b,
                               bA=banks[2 * i], bB=banks[2 * i + 1])
            dma("sp", out_d[t0 + i * 128:t0 + (i + 1) * 128, :], xt[b][:], ["xt%d" % b], ["out%d_%d" % (tb, i)])

    sems = {}
    for e in ("pe", "act", "dve", "pool", "sp"):
        sems[e] = es.enter_context(nc.semaphore("s_" + e))
    dsems = {}
    for e in ("sp", "pool"):
        for i in range(Prog.NDS):
            dsems[(e, i)] = es.enter_context(nc.semaphore("d_%s%d" % (e, i)))
    allsems = [h.num for h in list(sems.values()) + list(dsems.values())]
    srange = range(min(allsems), max(allsems) + 1)
    nc.gpsimd.sem_clear(srange)
    nc.all_engine_barrier()
    with nc.Block() as block:
        P.emit(nc, block, sems, dsems)
    nc.all_engine_barrier()
    nc.gpsimd.sem_clear(srange)
    nc.all_engine_barrier()
    global _SBUF_USED
    _SBUF_USED = (nc.sbuf_base, nc.sbuf_top)
    es.close()
    return nc


def _colpack(v):
    v = np.asarray(v, dtype=np.float32).reshape(-1, 128)
    return np.ascontiguousarray(v.T)


_NC = None


def kernel(x, attn_pre_norm, w_in, hgrn_lb, hgrn_gnorm, w_branch_a, rwkv_mu, rwkv_w0, rwkv_w2, rwkv_a0,
           rwkv_a2, rwkv_g2, rwkv_k_k, rwkv_k_a, rwkv_r_k, rwkv_ln_w, rwkv_ln_b, w_branch_b, w_out,
           attn_post_norm, ffn_pre_norm, w_up, conv_w, conv_b, w_down, ffn_post_norm):
    global _NC
    f = lambda a: np.ascontiguousarray(np.asarray(a, dtype=np.float32))
    cw = np.asarray(conv_w, dtype=np.float32)[0]
    cols = np.concatenate([
        _colpack(attn_pre_norm[0]), _colpack(hgrn_lb[0]), _colpack(hgrn_lb[1]), _colpack(hgrn_gnorm[0]),
        _colpack(rwkv_mu[0]), _colpack(rwkv_w0[0]), _colpack(rwkv_a0[0]), _colpack(rwkv_k_k[0]),
        _colpack(rwkv_k_a[0]), _colpack(np.asarray(rwkv_r_k[0]).reshape(-1)), _colpack(rwkv_ln_w[0]),
        _colpack(rwkv_ln_b[0]), _colpack(ffn_pre_norm[0]),
        _colpack(cw[0]), _colpack(cw[1]), _colpack(cw[2]), _colpack(conv_b[0]),
    ], axis=1)
    assert cols.shape == (128, NCOL), cols.shape
    shared = {
        "cols": f(cols),
        "w_in": f(w_in[0]), "w_branch_a": f(w_branch_a[0]), "w_branch_b": f(w_branch_b[0]),
        "w_out": f(w_out[0]),
        "w2a2": f(np.concatenate([np.asarray(rwkv_w2[0]), np.asarray(rwkv_a2[0])], axis=0)),
        "g2": f(rwkv_g2[0]),
        "gpost_a": f(np.asarray(attn_post_norm[0]).reshape(1, D)),
        "gpost_f": f(np.asarray(ffn_post_norm[0]).reshape(1, D)),
        "w_up": f(w_up[0]), "w_down": f(w_down[0]),
    }
    if _NC is None:
        _NC = build()
    xs = np.asarray(x, dtype=np.float32)
    in_maps = [dict(shared, x=f(xs[b])) for b in range(8)]
    res = run_bass_kernel_spmd(_NC, in_maps, core_ids=list(range(8)))
    out = np.stack([np.asarray(r["out"]) for r in res.results], axis=0)
    kernel.last_results = res.results
    return out.astype(np.float32)
```
